# Optimizing a Trainium2 kernel written in Bass

```python
import jax, jax.numpy as jnp
from jax import lax
import numpy as np

D_MODEL = 1024
BATCH = 16
SEQ = 2048
DEPTH = 1
DEC_BATCH = 32
DEC_SEQ = 1
PAST_LEN = 16384
PAGE_SIZE = 128

FOX_HEADS = 8
FOX_HD = 64
FOX_W = FOX_HEADS * FOX_HD
GLA_HEADS = 4
GLA_DK = 64
GLA_DV = 128
GLA_K_W = GLA_HEADS * GLA_DK
GLA_V_W = GLA_HEADS * GLA_DV
GLA_RANK = 16
GLA_GATE_NORM = 16.0
GLA_CHUNK = 64
Q_BLOCK = 128
D_FF = 4 * D_MODEL
EPS = 1e-6
D_IN = 3 * FOX_W + FOX_HEADS + 2 * GLA_K_W + 2 * GLA_V_W + GLA_RANK + 2 * D_MODEL

kernel_name = 'fox_gla_gated_hybrid_step'

F32 = jnp.float32


def _rmsnorm(x, g):
    xf = x.astype(F32)
    y = xf * lax.rsqrt(jnp.mean(xf * xf, axis=-1, keepdims=True) + EPS)
    return (y * g.astype(F32)).astype(x.dtype)


def _mixer_inputs(h, w_in, b_f, w_alpha_up, b_alpha):
    B, L, _ = h.shape
    sizes = (FOX_W, FOX_W, FOX_W, FOX_HEADS, GLA_K_W, GLA_K_W, GLA_V_W, GLA_V_W, GLA_RANK, D_MODEL, D_MODEL)
    idx = [int(i) for i in np.cumsum(sizes)[:-1]]
    qf, kf, vf, fl, qg, kg, vg, og, lr, gfox, ggla = jnp.split(h @ w_in, idx, axis=-1)
    q_fox = qf.reshape(B, L, FOX_HEADS, FOX_HD)
    k_fox = kf.reshape(B, L, FOX_HEADS, FOX_HD)
    v_fox = vf.reshape(B, L, FOX_HEADS, FOX_HD)
    logf = jax.nn.log_sigmoid(fl.astype(F32) + b_f.astype(F32))
    q_gla = qg.reshape(B, L, GLA_HEADS, GLA_DK) * (GLA_DK ** -0.5)
    k_gla = kg.reshape(B, L, GLA_HEADS, GLA_DK)
    v_gla = vg.reshape(B, L, GLA_HEADS, GLA_DV)
    log_alpha = (jax.nn.log_sigmoid((lr @ w_alpha_up + b_alpha).astype(F32)) / GLA_GATE_NORM
                 ).reshape(B, L, GLA_HEADS, GLA_DK)
    return q_fox, k_fox, v_fox, logf, q_gla, k_gla, v_gla, log_alpha, og, gfox, ggla


def _fox_prompt(q, k, v, logf):
    B, S, H, Dh = q.shape
    nb = S // Q_BLOCK
    scale = Dh ** -0.5
    F = jnp.cumsum(logf, axis=1)
    Fk = jnp.swapaxes(F, 1, 2)
    qb = jnp.moveaxis(q.reshape(B, nb, Q_BLOCK, H, Dh), 1, 0)
    Fq = jnp.moveaxis(jnp.swapaxes(F, 1, 2).reshape(B, H, nb, Q_BLOCK), 2, 0)
    starts = jnp.arange(nb) * Q_BLOCK
    key_pos = jnp.arange(S)

    def block(args):
        qi, Fi, s0 = args
        logits = jnp.einsum('bqhd,bkhd->bhqk', qi, k).astype(F32) * scale
        logits = logits + Fi[..., None] - Fk[:, :, None, :]
        qpos = s0 + jnp.arange(Q_BLOCK)
        logits = jnp.where(key_pos[None, :] <= qpos[:, None], logits, -jnp.inf)
        p = jax.nn.softmax(logits, axis=-1)
        return jnp.einsum('bhqk,bkhd->bqhd', p, v.astype(F32))

    o = lax.map(block, (qb, Fq, starts))
    return jnp.moveaxis(o, 0, 1).reshape(B, S, H * Dh)


def _fox_sample(q, k_new, v_new, logf_new, k_past, v_past, logf_past):
    B, L, H, Dh = q.shape
    scale = Dh ** -0.5
    lfp = logf_past.astype(F32)
    suffix = lax.cumsum(lfp, axis=1, reverse=True) - lfp
    cn = jnp.swapaxes(jnp.cumsum(logf_new, axis=1), 1, 2)
    lp = (jnp.einsum('bqhd,bkhd->bhqk', q, k_past).astype(F32) * scale
          + jnp.swapaxes(suffix, 1, 2)[:, :, None, :] + cn[..., None])
    ln = (jnp.einsum('bqhd,bkhd->bhqk', q, k_new).astype(F32) * scale
          + cn[..., :, None] - cn[..., None, :])
    causal = jnp.tril(jnp.ones((L, L), dtype=bool))
    ln = jnp.where(causal, ln, -jnp.inf)
    m = jnp.maximum(lp.max(-1, keepdims=True), ln.max(-1, keepdims=True))
    pp = jnp.exp(lp - m)
    pn = jnp.exp(ln - m)
    den = pp.sum(-1) + pn.sum(-1)
    num = (jnp.einsum('bhqk,bkhd->bqhd', pp, v_past.astype(F32))
           + jnp.einsum('bhqk,bkhd->bqhd', pn, v_new.astype(F32)))
    o = num / jnp.swapaxes(den, 1, 2)[..., None]
    return o.reshape(B, L, H * Dh)


def _gla_chunk(S0, q, k, v, la):
    S0 = S0.astype(F32)
    qf, kf, vf = q.astype(F32), k.astype(F32), v.astype(F32)
    L = q.shape[1]
    b = jnp.cumsum(la, axis=1)
    inter = jnp.einsum('blhd,bhde->blhe', qf * jnp.exp(b), S0)
    causal = jnp.tril(jnp.ones((L, L), dtype=bool))[None, :, :, None, None]
    diff = b[:, :, None] - b[:, None, :]
    decay = jnp.where(causal, jnp.exp(jnp.where(causal, diff, 0.0)), 0.0)
    A = jnp.einsum('bthd,bshd,btshd->bhts', qf, kf, decay)
    intra = jnp.einsum('bhts,bshe->bthe', A, vf)
    b_last = b[:, -1]
    S_new = (jnp.exp(b_last)[..., None] * S0
             + jnp.einsum('bshd,bshe->bhde', kf * jnp.exp(b_last[:, None] - b), vf))
    return S_new, inter + intra


def _gla_prompt(q, k, v, la):
    B, S, H, DK = q.shape
    nc = S // GLA_CHUNK

    def to_chunks(a):
        return jnp.moveaxis(a.reshape(B, nc, GLA_CHUNK, *a.shape[2:]), 1, 0)

    def step(state, inp):
        qc, kc, vc, lc = inp
        return _gla_chunk(state, qc, kc, vc, lc)

    S0 = jnp.zeros((B, H, DK, GLA_DV), F32)
    S_fin, o = lax.scan(step, S0, (to_chunks(q), to_chunks(k), to_chunks(v), to_chunks(la)))
    return S_fin, jnp.moveaxis(o, 0, 1).reshape(B, S, H, GLA_DV)


def _gla_out(o, og, gain):
    B, L = o.shape[:2]
    y = _rmsnorm(o, gain) * jax.nn.silu(og.astype(F32).reshape(B, L, GLA_HEADS, GLA_DV))
    return y.reshape(B, L, GLA_V_W)


def _finish(x, o_fox, o_gla, gfox, ggla, w_fox_out, w_gla_out, w_o, g_post_mix,
            g_pre_mlp, w_up, w_down, g_post_mlp):
    dt = x.dtype
    u = (jax.nn.sigmoid(gfox) * (o_fox.astype(dt) @ w_fox_out)
         + jax.nn.sigmoid(ggla) * (o_gla.astype(dt) @ w_gla_out))
    x = x + _rmsnorm(u @ w_o, g_post_mix)
    h = _rmsnorm(x, g_pre_mlp)
    m = jnp.square(jax.nn.relu(h @ w_up)) @ w_down
    return x + _rmsnorm(m, g_post_mlp)


def setup_inputs(seed: int = 0) -> dict:
    key = jax.random.key(seed)
    ks = jax.random.split(key, 24)
    n_pages = PAST_LEN // PAGE_SIZE
    n_used = DEC_BATCH * n_pages
    n_pool = n_used + max(1, n_used // 4)

    def nrm(k, shape, s):
        return jax.random.normal(k, shape, F32) * s

    def gain(k, n):
        return 1.0 + nrm(k, (DEPTH, n), 0.05)

    page_table = jax.random.permutation(ks[0], n_pool)[:n_used].reshape(DEC_BATCH, n_pages).astype(jnp.int32)
    return {
        'x_prompt': nrm(ks[1], (BATCH, SEQ, D_MODEL), 1.0),
        'x_sample': nrm(ks[2], (DEC_BATCH, DEC_SEQ, D_MODEL), 1.0),
        'cache_k': nrm(ks[3], (DEPTH, n_pool, PAGE_SIZE, FOX_HEADS, FOX_HD), 1.0),
        'cache_v': nrm(ks[4], (DEPTH, n_pool, PAGE_SIZE, FOX_HEADS, FOX_HD), 1.0),
        'cache_logf': jax.nn.log_sigmoid(nrm(ks[5], (DEPTH, n_pool, PAGE_SIZE, FOX_HEADS), 1.0) + 4.0),
        'state_gla': nrm(ks[6], (DEPTH, DEC_BATCH, GLA_HEADS, GLA_DK, GLA_DV), 0.3),
        'page_table': page_table,
        'g_pre_mix': gain(ks[7], D_MODEL),
        'w_in': nrm(ks[8], (DEPTH, D_MODEL, D_IN), D_MODEL ** -0.5),
        'b_f': jax.random.uniform(ks[9], (DEPTH, FOX_HEADS), F32, 2.0, 6.0),
        'w_alpha_up': nrm(ks[10], (DEPTH, GLA_RANK, GLA_K_W), GLA_RANK ** -0.5),
        'b_alpha': nrm(ks[11], (DEPTH, GLA_K_W), 0.1),
        'g_gla_norm': gain(ks[12], GLA_DV),
        'w_fox_out': nrm(ks[13], (DEPTH, FOX_W, D_MODEL), FOX_W ** -0.5),
        'w_gla_out': nrm(ks[14], (DEPTH, GLA_V_W, D_MODEL), GLA_V_W ** -0.5),
        'w_o': nrm(ks[15], (DEPTH, D_MODEL, D_MODEL), D_MODEL ** -0.5),
        'g_post_mix': gain(ks[16], D_MODEL),
        'g_pre_mlp': gain(ks[17], D_MODEL),
        'w_up': nrm(ks[18], (DEPTH, D_MODEL, D_FF), D_MODEL ** -0.5),
        'w_down': nrm(ks[19], (DEPTH, D_FF, D_MODEL), D_FF ** -0.5),
        'g_post_mlp': gain(ks[20], D_MODEL),
    }


def reference(x_prompt, x_sample, cache_k, cache_v, cache_logf, state_gla, page_table,
              g_pre_mix, w_in, b_f, w_alpha_up, b_alpha, g_gla_norm, w_fox_out, w_gla_out,
              w_o, g_post_mix, g_pre_mlp, w_up, w_down, g_post_mlp):
    xp, xs = x_prompt, x_sample
    DB, NP = page_table.shape
    kp_l, vp_l, fp_l, sp_l, ks_l, vs_l, fs_l, ss_l = [], [], [], [], [], [], [], []
    for l in range(DEPTH):
        hp = _rmsnorm(xp, g_pre_mix[l])
        qf, kf, vf, lf, qg, kg, vg, la, og, gfx, ggl = _mixer_inputs(hp, w_in[l], b_f[l], w_alpha_up[l], b_alpha[l])
        o_fox = _fox_prompt(qf, kf, vf, lf)
        S_p, o_g = _gla_prompt(qg, kg, vg, la)
        o_gla = _gla_out(o_g, og, g_gla_norm[l])
        xp = _finish(xp, o_fox, o_gla, gfx, ggl, w_fox_out[l], w_gla_out[l], w_o[l], g_post_mix[l],
                     g_pre_mlp[l], w_up[l], w_down[l], g_post_mlp[l])
        kp_l.append(kf); vp_l.append(vf); fp_l.append(lf); sp_l.append(S_p)
        hs = _rmsnorm(xs, g_pre_mix[l])
        qf, kf, vf, lf, qg, kg, vg, la, og, gfx, ggl = _mixer_inputs(hs, w_in[l], b_f[l], w_alpha_up[l], b_alpha[l])
        k_past = cache_k[l][page_table].reshape(DB, NP * PAGE_SIZE, FOX_HEADS, FOX_HD)
        v_past = cache_v[l][page_table].reshape(DB, NP * PAGE_SIZE, FOX_HEADS, FOX_HD)
        f_past = cache_logf[l][page_table].reshape(DB, NP * PAGE_SIZE, FOX_HEADS)
        o_fox = _fox_sample(qf, kf, vf, lf, k_past, v_past, f_past)
        S_s, o_g = _gla_chunk(state_gla[l], qg, kg, vg, la)
        o_gla = _gla_out(o_g, og, g_gla_norm[l])
        xs = _finish(xs, o_fox, o_gla, gfx, ggl, w_fox_out[l], w_gla_out[l], w_o[l], g_post_mix[l],
                     g_pre_mlp[l], w_up[l], w_down[l], g_post_mlp[l])
        ks_l.append(kf); vs_l.append(vf); fs_l.append(lf); ss_l.append(S_s)
    k_prompt, v_prompt = jnp.stack(kp_l), jnp.stack(vp_l)
    logf_prompt, gla_state_prompt = jnp.stack(fp_l), jnp.stack(sp_l)
    k_sample, v_sample = jnp.stack(ks_l), jnp.stack(vs_l)
    logf_sample, gla_state_sample = jnp.stack(fs_l), jnp.stack(ss_l)
    return (xp, xs, k_prompt, v_prompt, logf_prompt, gla_state_prompt,
            k_sample, v_sample, logf_sample, gla_state_sample)
```

```python
import contextlib
import numpy as np
import concourse.bass as bass
import concourse.mybir as mybir
from concourse.bass_utils import run_bass_kernel_spmd

F32 = mybir.dt.float32
BF16 = mybir.dt.bfloat16
I32 = mybir.dt.int32
AF = mybir.ActivationFunctionType
ALU = mybir.AluOpType
AX = mybir.AxisListType
EPS = 1e-6
N_CORES = 8


class _Stop(Exception):
    pass


class Buf:
    def __init__(self, name, t):
        self.name = name
        self.t = t
        self.w = None
        self.r = []
        self.psum = False

    def __getitem__(self, idx):
        return self.t[idx]


class Prog:
    NDMA = 40

    def __init__(self, nc, es):
        self.nc = nc
        self.eng = {'pe': nc.tensor, 'act': nc.scalar, 'dve': nc.vector, 'pool': nc.gpsimd, 'sp': nc.sync}
        self.sems = {}
        self.cnt = {}
        for e in ['pe', 'act', 'dve', 'pool']:
            self.sems[e] = es.enter_context(nc.semaphore('s_' + e))
            self.cnt[e] = 0
        self.dsems = [es.enter_context(nc.semaphore('s_dma%d' % i)) for i in range(self.NDMA)]
        self.dcnt = [0] * self.NDMA
        self.dnext = 0
        self.waited = {}
        self.ninst = 0

    def sb(self, es, name, shape, dt):
        return Buf(name, es.enter_context(self.nc.sbuf_tensor(name, list(shape), dt)))

    def ps(self, es, name, shape, dt=F32):
        b = Buf(name, es.enter_context(self.nc.psum_tensor(name, list(shape), dt)))
        b.psum = True
        return b

    def _sem(self, key):
        return self.sems[key] if isinstance(key, str) else self.dsems[key]

    def _wait(self, e, ev):
        key, val, _ = ev
        k = (e, key)
        if self.waited.get(k, 0) >= val:
            return
        self.waited[k] = val
        self.eng[e].wait_ge(self._sem(key), val)
        self.ninst += 1

    def _deps(self, e, reads, writes):
        best = {}

        def add(ev):
            if ev[0] not in best or best[ev[0]][1] < ev[1]:
                best[ev[0]] = ev
        for b in reads:
            if b.w is not None:
                add(b.w)
            if b.psum:
                for r in b.r:
                    if r[2] != e:
                        add(r)
        for b in writes:
            if b.w is not None and b.w[2] != e:
                add(b.w)
            for r in b.r:
                if r[2] != e:
                    add(r)
        for ev in best.values():
            self._wait(e, ev)

    def _record(self, ev, reads, writes):
        for b in reads:
            b.r = [r for r in b.r if r[0] != ev[0]] + [ev]
        for b in writes:
            b.w = ev
            b.r = []

    def op(self, e, fn, reads=(), writes=()):
        self._deps(e, reads, writes)
        inst = fn(self.eng[e])
        self.cnt[e] += 1
        inst.then_inc(self.sems[e], 1)
        self.ninst += 1
        self._record((e, self.cnt[e], e), reads, writes)

    def mm(self, groups, reads, writes):
        e = 'pe'
        self._deps(e, reads, writes)
        inst = None
        for out_ap, pairs in groups:
            n = len(pairs)
            for i, (l, r) in enumerate(pairs):
                inst = self.nc.tensor.matmul(out_ap, l, r, start=(i == 0), stop=(i == n - 1))
                self.ninst += 1
        self.cnt[e] += 1
        inst.then_inc(self.sems[e], 1)
        self._record((e, self.cnt[e], e), reads, writes)

    def mm_raw(self, out_ap, l, r, start, stop, reads, writes, skip=False):
        e = 'pe'
        self._deps(e, reads, writes)
        inst = self.nc.tensor.matmul(out_ap, l, r, start=start, stop=stop, skip_group_check=skip)
        self.cnt[e] += 1
        inst.then_inc(self.sems[e], 1)
        self.ninst += 1
        self._record((e, self.cnt[e], e), reads, writes)

    def tr(self, items, reads, writes):
        e = 'pe'
        self._deps(e, reads, writes)
        inst = None
        for out_ap, in_ap, ident_ap in items:
            inst = self.nc.tensor.transpose(out_ap, in_ap, ident_ap)
            self.ninst += 1
        self.cnt[e] += 1
        inst.then_inc(self.sems[e], 1)
        self._record((e, self.cnt[e], e), reads, writes)

    def dma(self, q, fn, reads=(), writes=()):
        k = self.dnext
        self.dnext = (self.dnext + 1) % self.NDMA
        if self.dcnt[k] > 0:
            self._wait(q, (k, self.dcnt[k], 'dma'))
        self._deps(q, reads, writes)
        inst = fn(self.eng[q])
        self.dcnt[k] += 16
        inst.then_inc(self.dsems[k], 16)
        self.ninst += 1
        self._record((k, self.dcnt[k], 'dma'), reads, writes)

    def barrier(self, engines=('pe', 'act', 'dve', 'pool', 'sp')):
        for e in engines:
            for o in ['pe', 'act', 'dve', 'pool']:
                if o != e and self.cnt[o] > 0:
                    self._wait(e, (o, self.cnt[o], o))
            for k in range(self.NDMA):
                if self.dcnt[k] > 0:
                    self._wait(e, (k, self.dcnt[k], 'dma'))


def make_consts():
    c = np.zeros((128, 1280), np.float32)
    i = np.arange(128)
    c[:, 0:128] = np.eye(128)
    tri_u = (i[:, None] <= i[None, :]).astype(np.float32)
    tri_su = (i[:, None] > i[None, :]).astype(np.float32)
    c[:, 128:256] = tri_u
    c[:, 256:384] = 1.0
    c[:, 384:512] = tri_su
    c[:, 512:640] = tri_u * (-1.0 / 16.0)
    c[:, 640:768] = tri_su * (-1.0 / 16.0)
    for h in range(8):
        c[h, 768 + h * 64:768 + (h + 1) * 64] = 1.0
    return c


def build(cfg):
    S, NSEQ, NS, NPG, NPOOL = cfg['S'], cfg['NSEQ'], cfg['NS'], cfg['NPG'], cfg['NPOOL']
    T = 512
    NT = S // T
    NKB = S // 128
    NTILE = NSEQ * NT
    nc = bass.Bass("TRN2", target_bir_lowering=False)

    def din(name, shape, dt=F32):
        return nc.dram_tensor(name, list(shape), dt, kind="ExternalInput").ap()

    def dout(name, shape, dt=F32):
        return nc.dram_tensor(name, list(shape), dt, kind="ExternalOutput").ap()

    def dscr(name, shape, dt=F32):
        return nc.dram_tensor(name, list(shape), dt, kind="Internal").ap()

    xp = din("xp", [NSEQ, S, 1024]); xsm = din("xsm", [NS, 1024])
    ck = din("ck", [NPOOL * 128, 512]); cv = din("cv", [NPOOL * 128, 512]); cl = din("cl", [NPOOL * 128, 8])
    sg = din("sg", [NS, 4, 64, 128]); pt = din("pt", [1, NS * NPG], I32)
    w_in = din("w_in", [1024, 5144]); gpre = din("gpre", [128, 8]); bfv = din("bfv", [1, 8])
    wau = din("wau", [16, 256]); bal = din("bal", [1, 256]); ggn = din("ggn", [1, 128])
    wfo = din("wfo", [512, 1024]); wgo = din("wgo", [512, 1024]); wo = din("wo", [1024, 1024])
    gpm = din("gpm", [1, 1024]); gpl = din("gpl", [128, 8])
    wup = din("wup", [1024, 4096]); wdn = din("wdn", [4096, 1024]); gpo = din("gpo", [1, 1024])
    cst = din("cst", [128, 1280])
    yp = dout("yp", [NSEQ, S, 1024]); ysm = dout("ysm", [NS, 1024])
    kp = dout("kp", [NSEQ, S, 512]); vp = dout("vp", [NSEQ, S, 512]); lfp = dout("lfp", [NSEQ, S, 8])
    gsp = dout("gsp", [NSEQ, 4, 64, 128])
    ksm = dout("ksm", [NS, 512]); vsm = dout("vsm", [NS, 512]); lfs = dout("lfs", [NS, 8])
    gss = dout("gss", [NS, 4, 64, 128])
    hT_s = dscr("hT_s", [NTILE, 128, 8, 512], BF16)
    oT_s = dscr("oT_s", [NTILE, 128, 8, 512], BF16)
    x1_s = dscr("x1_s", [NTILE, 4, 128, 1024])
    x1s_s = dscr("x1s_s", [NS, 1024])

    w_in_v = w_in.rearrange("(kc p) n -> p kc n", p=128)

    try:
        with contextlib.ExitStack() as es0:
            P = Prog(nc, es0)
            es0.enter_context(nc.Block())
            op, mm, dma = P.op, P.mm, P.dma
            hTs = P.sb(es0, "hTs", [128, 8, NS], BF16)
            oTs = P.sb(es0, "oTs", [128, 8, NS], BF16)
            hT_sb = [Buf("hT_s%d" % i, hT_s[i]) for i in range(NTILE)]
            oT_sb = [Buf("oT_s%d" % i, oT_s[i]) for i in range(NTILE)]
            x1_sb = [Buf("x1_s%d" % i, x1_s[i]) for i in range(NTILE)]
            x1s_b = Buf("x1s_s", x1s_s)

            def load_w(es, name, src_v, kc, ncols, c0=0, chunk=1024):
                wb = P.sb(es, name, [128, kc, ncols], BF16)
                for k0 in range(0, kc, 8):
                    for a in range(0, ncols, chunk):
                        b = min(ncols, a + chunk)
                        dma('pool', lambda e: e.dma_start(out=wb[:, k0:min(kc, k0 + 8), a:b], in_=src_v[:, k0:min(kc, k0 + 8), c0 + a:c0 + b]), [], [wb])
                return wb

            def norm_T(es_unused, xt, npart, g_sb, identb, xn, stat, trb, hT, col0):
                ssq, std, rstd = stat
                op('dve', lambda e: e.memset(ssq[0:npart, 0:1], 0.0), [], [ssq])
                op('act', lambda e: e.activation(out=xn[0:npart, :], in_=xt[0:npart, :], func=AF.Square,
                                                 accum_out=ssq[0:npart, 0:1]), [xt, ssq], [xn, ssq])
                op('act', lambda e: e.activation(out=std[0:npart, 0:1], in_=ssq[0:npart, 0:1], func=AF.Sqrt,
                                                 bias=EPS, scale=1.0 / 1024.0), [ssq], [std])
                op('dve', lambda e: e.reciprocal(out=rstd[0:npart, 0:1], in_=std[0:npart, 0:1]), [std], [rstd])
                op('dve', lambda e: e.tensor_scalar(out=xn[0:npart, :], in0=xt[0:npart, :], scalar1=rstd[0:npart, 0:1],
                                                    scalar2=None, op0=ALU.mult), [xt, rstd], [xn])
                P.tr([(trb[:, kc * 128:kc * 128 + npart], xn[0:npart, kc * 128:(kc + 1) * 128], identb[0:npart, 0:npart])
                      for kc in range(8)], [xn, identb], [trb])
                op('dve', lambda e: e.tensor_tensor(
                    out=hT[:, :, col0:col0 + npart],
                    in0=trb[:, :].rearrange("p (k t) -> p k t", k=8)[:, :, 0:npart],
                    in1=g_sb[:, 0:8].unsqueeze(2).to_broadcast([128, 8, npart]), op=ALU.mult), [trb, g_sb], [hT])

            def chk(stage):
                if cfg.get('stop') == stage:
                    raise _Stop()

            def phase_a1():
              with contextlib.ExitStack() as es:
                cf = P.sb(es, "cf", [128, 1280], F32)
                cb = P.sb(es, "cb", [128, 256], BF16)
                dma('sp', lambda e: e.dma_start(out=cf[:], in_=cst[:, :]), [], [cf])
                dma('pool', lambda e: e.dma_start(out=cb[:], in_=cst[:, 0:256]), [], [cb])
                ID = lambda n: cb[0:n, 0:n]
                MASKB = cb[:, 128:256]
                TRIU, ONES, TRISU, TRIU16, TRISU16 = cf[:, 128:256], cf[:, 256:384], cf[:, 384:512], cf[:, 512:640], cf[:, 640:768]
                BLKM = cf[0:8, 768:1280]
                gpre_sb = P.sb(es, "gpre_sb", [128, 8], F32)
                bf_b = P.sb(es, "bf_b", [128, 8], F32)
                wau_aug = P.sb(es, "wau_aug", [17, 256], F32)
                ggn_b = P.sb(es, "ggn_b", [128, 128], F32)
                dma('sp', lambda e: e.dma_start(out=gpre_sb[:], in_=gpre[:, :]), [], [gpre_sb])
                dma('sp', lambda e: e.dma_start(out=bf_b[:], in_=bfv[0:1, :].partition_broadcast(128)), [], [bf_b])
                dma('sp', lambda e: e.dma_start(out=wau_aug[0:16, :], in_=wau[:, :]), [], [wau_aug])
                dma('sp', lambda e: e.dma_start(out=wau_aug[16:17, :], in_=bal[0:1, :]), [], [wau_aug])
                dma('sp', lambda e: e.dma_start(out=ggn_b[:], in_=ggn[0:1, :].partition_broadcast(128)), [], [ggn_b])
                W1 = load_w(es, "W1", w_in_v, 8, 3096, 0, 516)
                CQ, CK, CV, CF, CQG, CKG, CVG, COG, CLR = 0, 512, 1024, 1536, 1544, 1800, 2056, 2568, 3080

                G = [P.ps(es, "G%d" % i, [128, 512], F32) for i in range(3)]
                OACC = [P.ps(es, "OACC%d" % i, [128, 4, 128], F32) for i in range(2)]
                SACC = P.ps(es, "SACC", [128, 512], F32)
                TRB = [P.ps(es, "TRB%d" % i, [128, 1024], BF16) for i in range(2)]
                gi = [0]; ti = [0]

                def gp():
                    gi[0] = (gi[0] + 1) % len(G)
                    return G[gi[0]]

                def tp():
                    ti[0] = (ti[0] + 1) % len(TRB)
                    return TRB[ti[0]]

                class Ring:
                    def __init__(self, name, n, shape, dt):
                        self.b = [P.sb(es, "%s%d" % (name, i), shape, dt) for i in range(n)]
                        self.i = 0

                    def next(self):
                        self.i = (self.i + 1) % len(self.b)
                        return self.b[self.i]

                xt_r = Ring("xt", 2, [128, 1024], F32)
                junk = P.sb(es, "junk", [128, 128], BF16)
                xn_r = Ring("xn", 2, [128, 1024], BF16)
                ssq = P.sb(es, "ssq", [128, 8], F32); std = P.sb(es, "std", [128, 8], F32); rstd = P.sb(es, "rstd", [128, 8], F32)
                stat = (ssq, std, rstd)
                hT = P.sb(es, "hT", [128, 8, T], BF16)
                QT = P.sb(es, "QT", [128, 4, T], BF16)
                KT = P.sb(es, "KT", [128, 4, S], BF16)
                Vaug = P.sb(es, "Vaug", [128, NKB, 8, 65], BF16)
                Cb = P.sb(es, "Cb", [128, NKB + 1, 8], F32)
                Ffull = P.sb(es, "Ffull", [128, NKB, 8], F32)
                biasT = P.sb(es, "biasT", [128, 4, NKB, 8], F32)
                kv_r = Ring("kvo", 2, [128, 512], F32)
                kb16_r = Ring("kb16", 2, [128, 512], BF16)
                lf_e = P.sb(es, "lf_e", [128, 8], F32)
                lf_r = Ring("lf", 2, [128, 8], F32)
                PT_r = Ring("PT", 6, [128, 128], BF16)
                rden_r = Ring("rden", 2, [128, 4], F32)
                ofx = P.sb(es, "ofx", [128, 4, 512], BF16)
                oT = P.sb(es, "oT", [128, 8, T], BF16)
                qgT = P.sb(es, "qgT", [128, 2, T], F32); kgT = P.sb(es, "kgT", [128, 2, T], F32)
                lrT = P.sb(es, "lrT", [17, T], F32)
                la_r = Ring("la", 2, [128, 256], F32)
                le_t = P.sb(es, "le_t", [128, 256], F32)
                kg_r = Ring("kg", 2, [128, 256], F32)
                vg_r = Ring("vg", 2, [128, 512], BF16)
                sog_r = Ring("sog", 2, [128, 512], BF16)
                EbT_r = Ring("EbT", 2, [128, 2, 128], F32); EnbT_r = Ring("EnbT", 2, [128, 2, 128], F32)
                qtl_r = Ring("qtl", 2, [128, 2, 128], F32); ktl_r = Ring("ktl", 2, [128, 2, 128], F32)
                ebrev = P.sb(es, "ebrev", [128, 256], F32)
                khat_r = Ring("khat", 2, [128, 256], BF16)
                ATs_r = Ring("ATs", 2, [128, 4, 128], BF16)
                Sst = P.sb(es, "Sst", [128, 2, 128], F32); Sb = P.sb(es, "Sb", [128, 2, 128], BF16)
                t1 = P.sb(es, "t1", [128, 4, 128], F32)
                ogl_r = Ring("ogl", 2, [128, 512], BF16)

                op('dve', lambda e: e.memset(Vaug[:, :, :, 64:65], 1.0), [], [Vaug])
                op('dve', lambda e: e.memset(lrT[:, :], 1.0), [], [lrT])

                chk('a1_setup')
                idxA = P.sb(es, "idxA", [128, NS * NPG], I32)
                idxB = P.sb(es, "idxB", [128, NS * NPG], F32)
                rid = P.sb(es, "rid", [128, 1], I32); ridf = P.sb(es, "ridf", [128, 1], F32)
                dma('sp', lambda e: e.dma_start(out=idxA[:], in_=pt[0:1, :].partition_broadcast(128)), [], [idxA])
                op('pool', lambda e: e.iota(rid[:], pattern=[[0, 1]], base=0, channel_multiplier=1), [], [rid])
                op('dve', lambda e: e.tensor_copy(out=ridf[:], in_=rid[:]), [rid], [ridf])
                op('dve', lambda e: e.tensor_copy(out=idxB[:], in_=idxA[:]), [idxA], [idxB])
                op('dve', lambda e: e.tensor_scalar(out=idxB[:], in0=idxB[:], scalar1=128.0, scalar2=ridf[:, 0:1],
                                                    op0=ALU.mult, op1=ALU.add), [idxB, ridf], [idxB])
                op('dve', lambda e: e.tensor_copy(out=idxA[:], in_=idxB[:]), [idxB], [idxA])

                row_r = Ring("row", 5, [1, 512], F32)
                qrow1 = P.sb(es, "qrow1", [1, 512], F32)
                krow1 = P.sb(es, "krow1", [1, 512], F32)
                vrowb1 = P.sb(es, "vrowb1", [1, 512], BF16)
                lfrow = [P.sb(es, "lfrow%d" % b, [1, 8], F32) for b in range(NS)]
                lrTs = P.sb(es, "lrTs", [17, NS], F32)
                qgTs = P.sb(es, "qgTs", [128, 2, NS], F32)
                laTs = P.sb(es, "laTs", [128, 2], F32); elaTs = P.sb(es, "elaTs", [128, 2], F32)
                Ssm = P.sb(es, "Ssm", [128, 2, 128], F32)
                one11 = cf[0:1, 256:257]

                chk('a1_idx')
                xs_t = xt_r.next()
                dma('sp', lambda e: e.dma_start(out=xs_t[0:NS, :], in_=xsm[:, :]), [], [xs_t])
                norm_T(es, xs_t, NS, gpre_sb, cb, xn_r.next(), stat, tp(), hTs, 0)
                chk('s1')
                op('dve', lambda e: e.memset(lrTs[:, :], 1.0), [], [lrTs])
                g = gp()
                mm([(g[0:16, 0:NS], [(W1[:, kc, CLR:CLR + 16], hTs[:, kc, 0:NS]) for kc in range(8)])], [W1, hTs], [g])
                op('dve', lambda e: e.tensor_copy(out=lrTs[0:16, :], in_=g[0:16, 0:NS]), [g], [lrTs])
                g = gp()
                mm([(g[:, hp * NS:(hp + 1) * NS], [(W1[:, kc, CQG + hp * 128:CQG + (hp + 1) * 128], hTs[:, kc, 0:NS]) for kc in range(8)])
                    for hp in range(2)], [W1, hTs], [g])
                op('act', lambda e: e.activation(out=qgTs[:, :, :], in_=g[:, 0:2 * NS].rearrange("p (h b) -> p h b", h=2),
                                                 func=AF.Copy, scale=0.125), [g], [qgTs])

                chk('s2')
                def rowproj(b, c0, n):
                    g = gp()
                    mm([(g[0:1, 0:n], [(hTs[:, kc, b:b + 1], W1[:, kc, c0:c0 + n]) for kc in range(8)])], [W1, hTs], [g])
                    return g

                def softplus_neg(dst, src_ap, npart, n, tmp, bias_ap=None, src_bufs=()):
                    if bias_ap is not None:
                        op('dve', lambda e: e.tensor_tensor(out=tmp[0:npart, 0:n], in0=src_ap, in1=bias_ap, op=ALU.add),
                           list(src_bufs), [tmp])
                        op('act', lambda e: e.activation(out=tmp[0:npart, 0:n], in_=tmp[0:npart, 0:n], func=AF.Exp, scale=-1.0),
                           [tmp], [tmp])
                    else:
                        op('act', lambda e: e.activation(out=tmp[0:npart, 0:n], in_=src_ap, func=AF.Exp, scale=-1.0),
                           list(src_bufs), [tmp])
                    op('act', lambda e: e.activation(out=dst[0:npart, 0:n], in_=tmp[0:npart, 0:n], func=AF.Ln, bias=1.0, scale=1.0),
                       [tmp], [dst])

                for b in range(NS):
                    g = rowproj(b, CK, 512)
                    kr = row_r.next()
                    op('act', lambda e: e.copy(out=kr[:, :], in_=g[0:1, :]), [g], [kr])
                    dma('sp', lambda e: e.dma_start(out=ksm[b:b + 1, :], in_=kr[:, :]), [kr], [])
                    g = rowproj(b, CV, 512)
                    vr = row_r.next()
                    op('act', lambda e: e.copy(out=vr[:, :], in_=g[0:1, :]), [g], [vr])
                    dma('sp', lambda e: e.dma_start(out=vsm[b:b + 1, :], in_=vr[:, :]), [vr], [])
                    chk('s3')
                    g = rowproj(b, CF, 8)
                    tmpr = row_r.next()
                    softplus_neg(lfrow[b], g[0:1, 0:8], 1, 8, tmpr, bias_ap=bf_b[0:1, :], src_bufs=[g, bf_b])
                    op('dve', lambda e: e.tensor_scalar(out=lfrow[b][:, :], in0=lfrow[b][:, :], scalar1=-1.0, scalar2=None,
                                                        op0=ALU.mult), [lfrow[b]], [lfrow[b]])
                    dma('sp', lambda e: e.dma_start(out=lfs[b:b + 1, :], in_=lfrow[b][:, :]), [lfrow[b]], [])
                    chk('s4')
                    g = rowproj(b, CKG, 256)
                    kgr = row_r.next()
                    op('act', lambda e: e.copy(out=kgr[:, 0:256], in_=g[0:1, 0:256]), [g], [kgr])
                    g = rowproj(b, CVG, 512)
                    vgr = row_r.next()
                    op('act', lambda e: e.copy(out=vgr[:, :], in_=g[0:1, :]), [g], [vgr])
                    g = rowproj(b, COG, 512)
                    sogr = row_r.next()
                    op('act', lambda e: e.activation(out=sogr[:, :], in_=g[0:1, :], func=AF.Silu), [g], [sogr])
                    chk('s5')
                    g = gp()
                    mm([(g[:, hp:hp + 1], [(wau_aug[0:17, hp * 128:(hp + 1) * 128], lrTs[0:17, b:b + 1])]) for hp in range(2)],
                       [wau_aug, lrTs], [g])
                    softplus_neg(laTs, g[:, 0:2], 128, 2, le_t, src_bufs=[g])
                    op('act', lambda e: e.activation(out=elaTs[:, :], in_=laTs[:, :], func=AF.Exp, scale=-1.0 / 16.0), [laTs], [elaTs])
                    chk('s6')
                    dma('sp', lambda e: e.dma_start(out=Ssm[:, :, :], in_=sg[b].rearrange("(hp dh) d e -> (dh d) hp e", dh=2)), [], [Ssm])
                    g = gp()
                    mm([(g[:, hp * 256:(hp + 1) * 256], [(kgr[0:1, hp * 128:(hp + 1) * 128], vgr[0:1, hp * 256:(hp + 1) * 256])])
                        for hp in range(2)], [kgr, vgr], [g])
                    for hp in range(2):
                        for dh in range(2):
                            pd = slice(dh * 64, dh * 64 + 64)
                            op('dve', lambda e: e.scalar_tensor_tensor(
                                out=Ssm[pd, hp, :], in0=Ssm[pd, hp, :], scalar=elaTs[pd, hp:hp + 1],
                                in1=g[pd, hp * 256 + dh * 128:hp * 256 + dh * 128 + 128], op0=ALU.mult, op1=ALU.add),
                               [Ssm, elaTs, g], [Ssm])
                    chk('s7')
                    dma('sp', lambda e: e.dma_start(out=gss[b].rearrange("(hp dh) d e -> (dh d) hp e", dh=2), in_=Ssm[:, :, :]), [Ssm], [])
                    chk('s7b')
                    gab = [gp(), gp()]
                    mm([(gab[hd % 2][0:1, (hd // 2) * 128:(hd // 2 + 1) * 128],
                         [(qgTs[(hd % 2) * 64:(hd % 2) * 64 + 64, hd // 2, b:b + 1], Ssm[(hd % 2) * 64:(hd % 2) * 64 + 64, hd // 2, :])])
                        for hd in (0, 2, 1, 3)], [qgTs, Ssm], gab)
                    chk('s8')
                    orow = row_r.next()
                    op('dve', lambda e: e.memset(ssq[0:1, 0:4], 0.0), [], [ssq])
                    for hd in range(4):
                        gsrc = gab[hd % 2]
                        op('act', lambda e: e.activation(out=junk[0:1, 0:128], in_=gsrc[0:1, (hd // 2) * 128:(hd // 2 + 1) * 128], func=AF.Square,
                                                         accum_out=ssq[0:1, hd:hd + 1]), [gsrc, ssq], [junk, ssq])
                    op('act', lambda e: e.activation(out=std[0:1, 0:4], in_=ssq[0:1, 0:4], func=AF.Sqrt, bias=EPS, scale=1.0 / 128.0), [ssq], [std])
                    op('dve', lambda e: e.reciprocal(out=rstd[0:1, 0:4], in_=std[0:1, 0:4]), [std], [rstd])
                    for hd in range(4):
                        gsrc = gab[hd % 2]
                        op('dve', lambda e: e.tensor_scalar(out=orow[0:1, hd * 128:(hd + 1) * 128], in0=gsrc[0:1, (hd // 2) * 128:(hd // 2 + 1) * 128],
                                                            scalar1=rstd[0:1, hd:hd + 1], scalar2=None, op0=ALU.mult), [gsrc, rstd], [orow])
                    op('dve', lambda e: e.tensor_tensor(out=orow[:, :].rearrange("p (h e) -> p h e", h=4),
                                                        in0=orow[:, :].rearrange("p (h e) -> p h e", h=4),
                                                        in1=ggn_b[0:1, :].unsqueeze(1).to_broadcast([1, 4, 128]), op=ALU.mult), [orow, ggn_b], [orow])
                    op('dve', lambda e: e.tensor_tensor(out=orow[:, :], in0=orow[:, :], in1=sogr[:, :], op=ALU.mult), [orow, sogr], [orow])
                    chk('s9')
                    g = gp()
                    mm([(g[:, kc:kc + 1], [(orow[0:1, kc * 128:(kc + 1) * 128], one11)]) for kc in range(4)], [orow, cf], [g])
                    op('dve', lambda e: e.tensor_copy(out=oTs[:, 4:8, b], in_=g[:, 0:4]), [g], [oTs])

                chk('a1_sproj')
                Kpg_r = Ring("Kpg", 3, [128, 512], BF16); Vpg_r = Ring("Vpg", 3, [128, 512], BF16)
                lpg_r = Ring("lpg", 3, [128, 8], F32)
                prod_r = Ring("prod", 2, [128, 512], F32)
                qbc = P.sb(es, "qbc", [128, 512], BF16)
                sc_r = Ring("sc", 2, [128, 8], F32); lg_r = Ring("lg", 2, [128, 8], F32)
                pb_r = Ring("pb", 3, [128, 8], BF16)
                carry = P.sb(es, "carry", [128, 8], F32); pacc = P.sb(es, "pacc", [128, 8], F32)
                den8 = P.sb(es, "den8", [8, 1], F32)

                def sample_decode():
                    for b in range(NS):
                        g = rowproj(b, CQ, 512)
                        op('act', lambda e: e.activation(out=qrow1[:, :], in_=g[0:1, :], func=AF.Copy, scale=0.125), [g], [qrow1])
                        g = rowproj(b, CK, 512)
                        op('act', lambda e: e.copy(out=krow1[:, :], in_=g[0:1, :]), [g], [krow1])
                        g = rowproj(b, CV, 512)
                        op('dve', lambda e: e.tensor_copy(out=vrowb1[:, :], in_=g[0:1, :]), [g], [vrowb1])
                        g = gp()
                        mm([(g[:, :], [(cf[0:1, 256:384], qrow1[0:1, :])])], [cf, qrow1], [g])
                        op('dve', lambda e: e.tensor_copy(out=qbc[:, :], in_=g[:, :]), [g], [qbc])
                        g = gp()
                        mm([(g[:, 0:8], [(cf[0:1, 256:384], lfrow[b][0:1, :])])], [cf, lfrow[b]], [g])
                        op('dve', lambda e: e.tensor_copy(out=carry[:, :], in_=g[:, 0:8]), [g], [carry])
                        op('dve', lambda e: e.memset(pacc[:, :], 0.0), [], [pacc])
                        pr = prod_r.next()
                        op('dve', lambda e: e.tensor_tensor(out=pr[0:1, :], in0=krow1[0:1, :], in1=qrow1[0:1, :], op=ALU.mult),
                           [krow1, qrow1], [pr])
                        sc = sc_r.next()
                        op('dve', lambda e: e.tensor_reduce(out=sc[0:1, :], in_=pr[0:1, :].rearrange("p (h d) -> p h d", h=8),
                                                            axis=AX.X, op=ALU.add), [pr], [sc])
                        pb = pb_r.next()
                        op('act', lambda e: e.activation(out=pb[0:1, :], in_=sc[0:1, :], func=AF.Exp), [sc], [pb])
                        op('dve', lambda e: e.tensor_tensor(out=pacc[0:1, :], in0=pacc[0:1, :], in1=pb[0:1, :], op=ALU.add), [pacc, pb], [pacc])
                        P.mm_raw(SACC[0:8, :], pb[0:1, :], vrowb1[0:1, :], True, False, [pb, vrowb1], [SACC])
                        yield
                        for j in range(NPG - 1, -1, -1):
                            col = b * NPG + j
                            Kp, Vp, lp = Kpg_r.next(), Vpg_r.next(), lpg_r.next()
                            off = bass.IndirectOffsetOnAxis(ap=idxA[:, col:col + 1], axis=0)
                            dma('pool', lambda e: e.indirect_dma_start(out=Kp[:, :], out_offset=None, in_=ck[:, :], in_offset=off), [idxA], [Kp])
                            dma('pool', lambda e: e.indirect_dma_start(out=Vp[:, :], out_offset=None, in_=cv[:, :], in_offset=off), [idxA], [Vp])
                            dma('pool', lambda e: e.indirect_dma_start(out=lp[:, :], out_offset=None, in_=cl[:, :], in_offset=off), [idxA], [lp])
                            pr = prod_r.next()
                            op('dve', lambda e: e.tensor_tensor(out=pr[:, :], in0=Kp[:, :], in1=qbc[:, :], op=ALU.mult), [Kp, qbc], [pr])
                            sc = sc_r.next()
                            op('dve', lambda e: e.tensor_reduce(out=sc[:, :], in_=pr[:, :].rearrange("p (h d) -> p h d", h=8),
                                                                axis=AX.X, op=ALU.add), [pr], [sc])
                            g = gp()
                            mm([(g[:, 0:8], [(TRISU, lp[:, :])]), (g[:, 8:16], [(ONES, lp[:, :])])], [cf, lp], [g])
                            lg = lg_r.next()
                            op('dve', lambda e: e.tensor_tensor(out=lg[:, :], in0=g[:, 0:8], in1=carry[:, :], op=ALU.add), [g, carry], [lg])
                            op('dve', lambda e: e.tensor_tensor(out=lg[:, :], in0=lg[:, :], in1=sc[:, :], op=ALU.add), [lg, sc], [lg])
                            op('dve', lambda e: e.tensor_tensor(out=carry[:, :], in0=g[:, 8:16], in1=carry[:, :], op=ALU.add), [g, carry], [carry])
                            pb = pb_r.next()
                            op('act', lambda e: e.activation(out=pb[:, :], in_=lg[:, :], func=AF.Exp), [lg], [pb])
                            op('dve', lambda e: e.tensor_tensor(out=pacc[:, :], in0=pacc[:, :], in1=pb[:, :], op=ALU.add), [pacc, pb], [pacc])
                            P.mm_raw(SACC[0:8, :], pb[:, :], Vp[:, :], False, (j == 0), [pb, Vp], [SACC])
                            yield
                        g = gp()
                        mm([(g[0:8, 0:1], [(pacc[:, :], cf[:, 256:257])])], [pacc, cf], [g])
                        op('dve', lambda e: e.reciprocal(out=den8[:, :], in_=g[0:8, 0:1]), [g], [den8])
                        on8 = prod_r.next()
                        op('dve', lambda e: e.scalar_tensor_tensor(out=on8[0:8, :], in0=SACC[0:8, :], scalar=den8[:, 0:1], in1=BLKM,
                                                                   op0=ALU.mult, op1=ALU.mult), [SACC, den8, cf], [on8])
                        g = gp()
                        mm([(g[:, kc:kc + 1], [(on8[0:8, kc * 128:(kc + 1) * 128], cf[0:8, 256:257])]) for kc in range(4)], [on8, cf], [g])
                        op('dve', lambda e: e.tensor_copy(out=oTs[:, 0:4, b], in_=g[:, 0:4]), [g], [oTs])
                        yield

                sgen = sample_decode()
                sdone = [False]

                def pump(n=1):
                    for _ in range(n):
                        if sdone[0]:
                            return
                        try:
                            next(sgen)
                        except StopIteration:
                            sdone[0] = True

                total_pumps = NS * (NPG + 2)
                n_slots = NSEQ * sum(8 * (4 * it + 4) for it in range(NT))
                ppump = max(1, -(-total_pumps // max(1, n_slots)))

                if cfg.get('stop') == 'a1_sdec':
                    while not sdone[0]:
                        pump(1)
                    raise _Stop()
                for q in range(NSEQ):
                    op('dve', lambda e: e.memset(Cb[:, 0, :], 0.0), [], [Cb])
                    op('dve', lambda e: e.memset(Sst[:, :, :], 0.0), [], [Sst])
                    for it in range(NT):
                        tile_id = q * NT + it
                        t0 = it * T
                        for sub in range(4):
                            r0 = t0 + sub * 128
                            kb = it * 4 + sub
                            xt = xt_r.next()
                            dma('sp', lambda e: e.dma_start(out=xt[:, :], in_=xp[q, r0:r0 + 128, :]), [], [xt])
                            norm_T(es, xt, 128, gpre_sb, cb, xn_r.next(), stat, tp(), hT, sub * 128)
                            chk('p1')
                            tok = slice(sub * 128, sub * 128 + 128)

                            def tproj(c0, n):
                                g = gp()
                                mm([(g[:, 0:n], [(hT[:, kc, tok], W1[:, kc, c0:c0 + n]) for kc in range(8)])], [hT, W1], [g])
                                return g
                            g = tproj(CK, 512)
                            chk('p1a')
                            ko = kv_r.next()
                            op('act', lambda e: e.copy(out=ko[:, :], in_=g[:, :]), [g], [ko])
                            chk('p1b')
                            k16 = kb16_r.next()
                            op('dve', lambda e: e.tensor_copy(out=k16[:, :], in_=ko[:, :]), [ko], [k16])
                            chk('p1c')
                            dma('sp', lambda e: e.dma_start(out=kp[q, r0:r0 + 128, :], in_=ko[:, :]), [ko], [])
                            chk('p2')
                            tb = tp()
                            P.tr([(tb[:, hp * 128:(hp + 1) * 128], k16[:, hp * 128:(hp + 1) * 128], ID(128)) for hp in range(4)], [k16, cb], [tb])
                            op('act', lambda e: e.copy(out=KT[:, :, kb * 128:(kb + 1) * 128],
                                                       in_=tb[:, 0:512].rearrange("p (h t) -> p h t", h=4)), [tb], [KT])
                            chk('p3')
                            g = tproj(CV, 512)
                            vo = kv_r.next()
                            op('act', lambda e: e.copy(out=vo[:, :], in_=g[:, :]), [g], [vo])
                            op('dve', lambda e: e.tensor_copy(out=Vaug[:, kb, :, 0:64], in_=vo[:, :].rearrange("p (h d) -> p h d", h=8)), [vo], [Vaug])
                            dma('sp', lambda e: e.dma_start(out=vp[q, r0:r0 + 128, :], in_=vo[:, :]), [vo], [])
                            chk('p4')
                            g = tproj(CF, 8)
                            lf = lf_r.next()
                            softplus_neg(lf, g[:, 0:8], 128, 8, lf_e, bias_ap=bf_b[:, :], src_bufs=[g, bf_b])
                            op('dve', lambda e: e.tensor_scalar(out=lf[:, :], in0=lf[:, :], scalar1=-1.0, scalar2=None, op0=ALU.mult), [lf], [lf])
                            dma('sp', lambda e: e.dma_start(out=lfp[q, r0:r0 + 128, :], in_=lf[:, :]), [lf], [])
                            chk('p5')
                            g = gp()
                            mm([(g[:, 0:8], [(TRIU, lf[:, :])]), (g[:, 8:16], [(ONES, lf[:, :])])], [cf, lf], [g])
                            op('dve', lambda e: e.tensor_tensor(out=Ffull[:, kb, :], in0=g[:, 0:8], in1=Cb[:, kb, :], op=ALU.add), [g, Cb], [Ffull])
                            op('dve', lambda e: e.tensor_tensor(out=Cb[:, kb + 1, :], in0=g[:, 8:16], in1=Cb[:, kb, :], op=ALU.add), [g, Cb], [Cb])
                        chk('a1_t1')
                        for hp in range(4):
                            g = gp()
                            mm([(g[:, :], [(W1[:, kc, CQ + hp * 128:CQ + (hp + 1) * 128], hT[:, kc, :]) for kc in range(8)])], [W1, hT], [g])
                            op('act', lambda e: e.activation(out=QT[:, hp, :], in_=g[:, :], func=AF.Copy, scale=0.125), [g], [QT])
                        for jl in range(4):
                            jg = it * 4 + jl
                            op('dve', lambda e: e.tensor_tensor(out=biasT[:, jl, 0:jg + 1, :],
                                                                in0=Cb[:, jg:jg + 1, :].to_broadcast([128, jg + 1, 8]),
                                                                in1=Ffull[:, 0:jg + 1, :], op=ALU.subtract), [Cb, Ffull], [biasT])
                        for h in range(8):
                            hp, dh = h // 2, h % 2
                            pd = slice(dh * 64, dh * 64 + 64)
                            oacc = OACC[h % 2]
                            nkb = it * 4 + 4
                            for kb in range(nkb):
                                jl0 = max(0, kb - it * 4)
                                c0 = jl0 * 128
                                g = gp()
                                mm([(g[:, c0:512], [(KT[pd, hp, kb * 128:(kb + 1) * 128], QT[pd, hp, c0:512])])], [KT, QT], [g])
                                for jl in range(jl0, 4):
                                    jg = it * 4 + jl
                                    pt_ = PT_r.next()
                                    op('act', lambda e: e.activation(out=pt_[:, :], in_=g[:, jl * 128:(jl + 1) * 128], func=AF.Exp,
                                                                     bias=biasT[:, jl, kb, h:h + 1], scale=1.0), [g, biasT], [pt_])
                                    if kb == jg:
                                        op('pool', lambda e: e.tensor_tensor(out=pt_[:, :], in0=pt_[:, :], in1=MASKB, op=ALU.mult), [pt_, cb], [pt_])
                                    P.mm_raw(oacc[:, jl, 0:65], pt_[:, :], Vaug[:, kb, h, :], (kb == 0 and jl == 0), (kb == jg), [pt_, Vaug], [oacc], skip=True)
                                pump(ppump)
                            rd = rden_r.next()
                            op('dve', lambda e: e.reciprocal(out=rd[:, :], in_=oacc[:, :, 64]), [oacc], [rd])
                            op('dve', lambda e: e.tensor_tensor(out=ofx[:, :, h * 64:(h + 1) * 64], in0=oacc[:, :, 0:64],
                                                                in1=rd[:, 0:4].unsqueeze(2).to_broadcast([128, 4, 64]), op=ALU.mult), [oacc, rd], [ofx])
                        chk('a1_t4')
                        for half in range(2):
                            tb = tp()
                            P.tr([(tb[:, (i * 4 + jl) * 128:(i * 4 + jl + 1) * 128], ofx[:, jl, (half * 2 + i) * 128:(half * 2 + i + 1) * 128], ID(128))
                                  for i in range(2) for jl in range(4)], [ofx, cb], [tb])
                            op('act', lambda e: e.copy(out=oT[:, half * 2:half * 2 + 2, :],
                                                       in_=tb[:, :].rearrange("p (c t) -> p c t", c=2)), [tb], [oT])
                        chk('a1_t4b')
                        for hp in range(2):
                            g = gp()
                            mm([(g[:, :], [(W1[:, kc, CQG + hp * 128:CQG + (hp + 1) * 128], hT[:, kc, :]) for kc in range(8)])], [W1, hT], [g])
                            op('act', lambda e: e.activation(out=qgT[:, hp, :], in_=g[:, :], func=AF.Copy, scale=0.125), [g], [qgT])
                            g = gp()
                            mm([(g[:, :], [(W1[:, kc, CKG + hp * 128:CKG + (hp + 1) * 128], hT[:, kc, :]) for kc in range(8)])], [W1, hT], [g])
                            op('act', lambda e: e.copy(out=kgT[:, hp, :], in_=g[:, :]), [g], [kgT])
                        g = gp()
                        mm([(g[0:16, :], [(W1[:, kc, CLR:CLR + 16], hT[:, kc, :]) for kc in range(8)])], [W1, hT], [g])
                        op('dve', lambda e: e.tensor_copy(out=lrT[0:16, :], in_=g[0:16, :]), [g], [lrT])
                        for sub in range(4):
                            tok = slice(sub * 128, sub * 128 + 128)
                            g = gp()
                            mm([(g[:, 0:256], [(hT[:, kc, tok], W1[:, kc, CKG:CKG + 256]) for kc in range(8)])], [hT, W1], [g])
                            kg = kg_r.next()
                            op('act', lambda e: e.copy(out=kg[:, :], in_=g[:, 0:256]), [g], [kg])
                            g = gp()
                            mm([(g[:, :], [(hT[:, kc, tok], W1[:, kc, CVG:CVG + 512]) for kc in range(8)])], [hT, W1], [g])
                            vg = vg_r.next()
                            op('dve', lambda e: e.tensor_copy(out=vg[:, :], in_=g[:, :]), [g], [vg])
                            g = gp()
                            mm([(g[:, :], [(hT[:, kc, tok], W1[:, kc, COG:COG + 512]) for kc in range(8)])], [hT, W1], [g])
                            sog = sog_r.next()
                            op('act', lambda e: e.activation(out=sog[:, :], in_=g[:, :], func=AF.Silu), [g], [sog])
                            g = gp()
                            mm([(g[:, 0:256], [(lrT[0:17, tok], wau_aug[0:17, :])])], [lrT, wau_aug], [g])
                            la = la_r.next()
                            softplus_neg(la, g[:, 0:256], 128, 256, le_t, src_bufs=[g])
                            g = gp()
                            mm([(g[:, hp * 128:(hp + 1) * 128], [(la[:, hp * 128:(hp + 1) * 128], TRIU16)]) for hp in range(2)]
                               + [(g[:, 256:512], [(TRISU16, la[:, :])])], [la, cf], [g])
                            EbT, EnbT = EbT_r.next(), EnbT_r.next()
                            gv = g[:, 0:256].rearrange("p (h t) -> p h t", h=2)
                            op('act', lambda e: e.activation(out=EbT[:, :, :], in_=gv, func=AF.Exp), [g], [EbT])
                            op('act', lambda e: e.activation(out=EnbT[:, :, :], in_=gv, func=AF.Exp, scale=-1.0), [g], [EnbT])
                            op('act', lambda e: e.activation(out=ebrev[:, :], in_=g[:, 256:512], func=AF.Exp), [g], [ebrev])
                            qtl, ktl, khat = qtl_r.next(), ktl_r.next(), khat_r.next()
                            op('dve', lambda e: e.tensor_tensor(out=qtl[:, :, :], in0=qgT[:, :, tok], in1=EbT[:, :, :], op=ALU.mult), [qgT, EbT], [qtl])
                            op('pool', lambda e: e.tensor_tensor(out=ktl[:, :, :], in0=kgT[:, :, tok], in1=EnbT[:, :, :], op=ALU.mult), [kgT, EnbT], [ktl])
                            op('pool', lambda e: e.tensor_tensor(out=khat[:, :], in0=kg[:, :], in1=ebrev[:, :], op=ALU.mult), [kg, ebrev], [khat])
                            gat = [gp(), gp()]
                            mm([(gat[hd % 2][:, (hd // 2) * 128:(hd // 2 + 1) * 128],
                                 [(ktl[(hd % 2) * 64:(hd % 2) * 64 + 64, hd // 2, :], qtl[(hd % 2) * 64:(hd % 2) * 64 + 64, hd // 2, :])])
                                for hd in (0, 2, 1, 3)], [ktl, qtl], gat)
                            ATs = ATs_r.next()
                            for dh in range(2):
                                op('dve', lambda e: e.tensor_tensor(out=ATs[:, dh * 2:dh * 2 + 2, :],
                                                                    in0=gat[dh][:, 0:256].rearrange("p (h t) -> p h t", h=2),
                                                                    in1=cf[:, 128:256].unsqueeze(1).to_broadcast([128, 2, 128]), op=ALU.mult),
                                   [gat[dh], cf], [ATs])
                            gov = [gp(), gp()]
                            mm([(gov[hd % 2][:, (hd // 2) * 128:(hd // 2 + 1) * 128],
                                 [(qtl[(hd % 2) * 64:(hd % 2) * 64 + 64, hd // 2, :], Sst[(hd % 2) * 64:(hd % 2) * 64 + 64, hd // 2, :]),
                                  (ATs[:, (hd % 2) * 2 + hd // 2, :], vg[:, hd * 128:(hd + 1) * 128])]) for hd in (0, 2, 1, 3)],
                               [qtl, Sst, ATs, vg], gov)
                            gs = gp()
                            mm([(gs[:, hp * 256:(hp + 1) * 256], [(khat[:, hp * 128:(hp + 1) * 128], vg[:, hp * 256:(hp + 1) * 256])])
                                for hp in range(2)], [khat, vg], [gs])
                            for hp in range(2):
                                for dh in range(2):
                                    pd = slice(dh * 64, dh * 64 + 64)
                                    op('dve', lambda e: e.scalar_tensor_tensor(
                                        out=Sst[pd, hp, :], in0=Sst[pd, hp, :], scalar=EbT[pd, hp, 127:128],
                                        in1=gs[pd, hp * 256 + dh * 128:hp * 256 + dh * 128 + 128], op0=ALU.mult, op1=ALU.add),
                                       [Sst, EbT, gs], [Sst])
                            op('dve', lambda e: e.memset(ssq[:, 0:4], 0.0), [], [ssq])
                            for hd in range(4):
                                gsrc = gov[hd % 2]
                                op('act', lambda e: e.activation(out=junk[:, 0:128], in_=gsrc[:, (hd // 2) * 128:(hd // 2 + 1) * 128], func=AF.Square,
                                                                 accum_out=ssq[:, hd:hd + 1]), [gsrc, ssq], [junk, ssq])
                            op('act', lambda e: e.activation(out=std[:, 0:4], in_=ssq[:, 0:4], func=AF.Sqrt, bias=EPS, scale=1.0 / 128.0), [ssq], [std])
                            op('dve', lambda e: e.reciprocal(out=rstd[:, 0:4], in_=std[:, 0:4]), [std], [rstd])
                            for dh in range(2):
                                op('dve', lambda e: e.tensor_tensor(
                                    out=t1[:, :, :].rearrange("p (hp dh) e -> p hp dh e", dh=2)[:, :, dh, :],
                                    in0=gov[dh][:, 0:256].rearrange("p (h e) -> p h e", h=2),
                                    in1=rstd[:, 0:4].rearrange("p (hp dh) -> p hp dh", dh=2)[:, :, dh].unsqueeze(2).to_broadcast([128, 2, 128]),
                                    op=ALU.mult), [gov[dh], rstd], [t1])
                            op('pool', lambda e: e.tensor_tensor(out=t1[:, :, :], in0=t1[:, :, :],
                                                                 in1=ggn_b[:, :].unsqueeze(1).to_broadcast([128, 4, 128]), op=ALU.mult), [t1, ggn_b], [t1])
                            ogl = ogl_r.next()
                            op('pool', lambda e: e.tensor_tensor(out=ogl[:, :], in0=t1[:, :, :].rearrange("p h e -> p (h e)"), in1=sog[:, :], op=ALU.mult),
                               [t1, sog], [ogl])
                            tb = tp()
                            P.tr([(tb[:, kc * 128:(kc + 1) * 128], ogl[:, kc * 128:(kc + 1) * 128], ID(128)) for kc in range(4)], [ogl, cb], [tb])
                            op('act', lambda e: e.copy(out=oT[:, 4:8, tok], in_=tb[:, 0:512].rearrange("p (c t) -> p c t", c=4)), [tb], [oT])
                        chk('a1_t5')
                        dma('sp', lambda e: e.dma_start(out=hT_s[tile_id], in_=hT[:, :, :]), [hT], [hT_sb[tile_id]])
                        dma('sp', lambda e: e.dma_start(out=oT_s[tile_id], in_=oT[:, :, :]), [oT], [oT_sb[tile_id]])
                    dma('sp', lambda e: e.dma_start(out=gsp[q].rearrange("(hp dh) d e -> (dh d) hp e", dh=2), in_=Sst[:, :, :]), [Sst], [])
                while not sdone[0]:
                    pump(1)
                P.barrier()

            def phase_a2():
              with contextlib.ExitStack() as es:
                Wg = load_w(es, "Wg", w_in_v, 8, 2048, 3096, 512)
                Wfo = load_w(es, "Wfo", wfo.rearrange("(kc p) n -> p kc n", p=128), 4, 1024, 0, 1024)
                Wgo = load_w(es, "Wgo", wgo.rearrange("(kc p) n -> p kc n", p=128), 4, 1024, 0, 1024)
                Wo = load_w(es, "Wo", wo.rearrange("(kc p) n -> p kc n", p=128), 8, 1024, 0, 512)
                gpm_b = P.sb(es, "gpm_b", [128, 1024], F32)
                dma('sp', lambda e: e.dma_start(out=gpm_b[:], in_=gpm[0:1, :].partition_broadcast(128)), [], [gpm_b])
                G = [P.ps(es, "H%d" % i, [128, 512], F32) for i in range(8)]
                gi = [0]

                def gp2():
                    gi[0] = (gi[0] + 1) % len(G)
                    return G[gi[0]]
                hT2 = [P.sb(es, "hT2_%d" % i, [128, 8, T], BF16) for i in range(2)]
                oT2 = [P.sb(es, "oT2_%d" % i, [128, 8, T], BF16) for i in range(2)]
                s1 = [P.sb(es, "s1_%d" % i, [128, T], F32) for i in range(2)]
                s2 = [P.sb(es, "s2_%d" % i, [128, T], F32) for i in range(2)]
                uT = P.sb(es, "uT", [128, 8, T], BF16)
                xr = [P.sb(es, "xr%d" % i, [128, 1024], F32) for i in range(2)]
                zn = [P.sb(es, "zn%d" % i, [128, 1024], F32) for i in range(2)]
                junk2 = P.sb(es, "junk2", [128, 512], BF16)
                ssq = P.sb(es, "ssq2", [128, 2], F32); std = P.sb(es, "std2", [128, 2], F32); rstd = P.sb(es, "rstd2", [128, 2], F32)
                cnt = [0]
                tiles = [(i, T, 128) for i in range(NTILE)] + [(-1, NS, NS)]
                for (tile_id, Tn, npart) in tiles:
                    nsub = max(1, Tn // 128)
                    if tile_id >= 0:
                        hb, ob = hT2[tile_id % 2], oT2[tile_id % 2]
                        dma('sp', lambda e: e.dma_start(out=hb[:, :, :], in_=hT_s[tile_id]), [hT_sb[tile_id]], [hb])
                        dma('sp', lambda e: e.dma_start(out=ob[:, :, :], in_=oT_s[tile_id]), [oT_sb[tile_id]], [ob])
                    else:
                        hb, ob = hTs, oTs
                    for c in range(8):
                        cs = slice(c * 128, (c + 1) * 128)
                        ga, gb, g1, g2 = gp2(), gp2(), gp2(), gp2()
                        mm([(g1[:, 0:Tn], [(Wg[:, kc, c * 128:(c + 1) * 128], hb[:, kc, 0:Tn]) for kc in range(8)])], [Wg, hb], [g1])
                        mm([(g2[:, 0:Tn], [(Wg[:, kc, 1024 + c * 128:1024 + (c + 1) * 128], hb[:, kc, 0:Tn]) for kc in range(8)])], [Wg, hb], [g2])
                        mm([(ga[:, 0:Tn], [(Wfo[:, kc, cs], ob[:, kc, 0:Tn]) for kc in range(4)])], [Wfo, ob], [ga])
                        mm([(gb[:, 0:Tn], [(Wgo[:, kc, cs], ob[:, 4 + kc, 0:Tn]) for kc in range(4)])], [Wgo, ob], [gb])
                        a1, a2 = s1[c % 2], s2[c % 2]
                        op('act', lambda e: e.activation(out=a1[:, 0:Tn], in_=g1[:, 0:Tn], func=AF.Sigmoid), [g1], [a1])
                        op('act', lambda e: e.activation(out=a2[:, 0:Tn], in_=g2[:, 0:Tn], func=AF.Sigmoid), [g2], [a2])
                        op('dve', lambda e: e.tensor_tensor(out=a1[:, 0:Tn], in0=a1[:, 0:Tn], in1=ga[:, 0:Tn], op=ALU.mult), [a1, ga], [a1])
                        op('dve', lambda e: e.tensor_tensor(out=a2[:, 0:Tn], in0=a2[:, 0:Tn], in1=gb[:, 0:Tn], op=ALU.mult), [a2, gb], [a2])
                        op('pool', lambda e: e.tensor_tensor(out=uT[:, c, 0:Tn], in0=a1[:, 0:Tn], in1=a2[:, 0:Tn], op=ALU.add), [a1, a2], [uT])
                    for sub in range(nsub):
                        tok = slice(sub * npart, (sub + 1) * npart)
                        xb, zb = xr[cnt[0] % 2], zn[cnt[0] % 2]
                        cnt[0] += 1
                        if tile_id >= 0:
                            q, it = tile_id // NT, tile_id % NT
                            r0 = it * T + sub * 128
                            dma('sp', lambda e: e.dma_start(out=xb[:, :], in_=xp[q, r0:r0 + 128, :]), [], [xb])
                        else:
                            dma('sp', lambda e: e.dma_start(out=xb[0:NS, :], in_=xsm[:, :]), [], [xb])
                        gz = [gp2(), gp2()]
                        for half in range(2):
                            mm([(gz[half][0:npart, :], [(uT[:, kc, tok], Wo[:, kc, half * 512:(half + 1) * 512]) for kc in range(8)])], [uT, Wo], [gz[half]])
                        op('dve', lambda e: e.memset(ssq[0:npart, 0:2], 0.0), [], [ssq])
                        for half in range(2):
                            op('act', lambda e: e.activation(out=junk2[0:npart, :], in_=gz[half][0:npart, :], func=AF.Square,
                                                             accum_out=ssq[0:npart, half:half + 1]), [gz[half], ssq], [junk2, ssq])
                        op('dve', lambda e: e.tensor_tensor(out=ssq[0:npart, 0:1], in0=ssq[0:npart, 0:1], in1=ssq[0:npart, 1:2], op=ALU.add), [ssq], [ssq])
                        op('act', lambda e: e.activation(out=std[0:npart, 0:1], in_=ssq[0:npart, 0:1], func=AF.Sqrt, bias=EPS, scale=1.0 / 1024.0), [ssq], [std])
                        op('dve', lambda e: e.reciprocal(out=rstd[0:npart, 0:1], in_=std[0:npart, 0:1]), [std], [rstd])
                        for half in range(2):
                            hs = slice(half * 512, (half + 1) * 512)
                            op('dve', lambda e: e.scalar_tensor_tensor(out=zb[0:npart, hs], in0=gz[half][0:npart, :], scalar=rstd[0:npart, 0:1],
                                                                       in1=gpm_b[0:npart, hs], op0=ALU.mult, op1=ALU.mult), [gz[half], rstd, gpm_b], [zb])
                        op('pool', lambda e: e.tensor_tensor(out=zb[0:npart, :], in0=zb[0:npart, :], in1=xb[0:npart, :], op=ALU.add), [zb, xb], [zb])
                        if tile_id >= 0:
                            dma('sp', lambda e: e.dma_start(out=x1_s[tile_id, sub], in_=zb[:, :]), [zb], [x1_sb[tile_id]])
                        else:
                            dma('sp', lambda e: e.dma_start(out=x1s_s[:, :], in_=zb[0:NS, :]), [zb], [x1s_b])
                P.barrier()

            def phase_b():
              with contextlib.ExitStack() as es:
                Wup = load_w(es, "Wup", wup.rearrange("(kc p) n -> p kc n", p=128), 8, 4096, 0, 512)
                Wdn = load_w(es, "Wdn", wdn.rearrange("(kc p) n -> p kc n", p=128), 32, 1024, 0, 512)
                identb = P.sb(es, "identb", [128, 128], BF16)
                dma('pool', lambda e: e.dma_start(out=identb[:], in_=cst[:, 0:128]), [], [identb])
                gpl_sb = P.sb(es, "gpl_sb", [128, 8], F32)
                dma('sp', lambda e: e.dma_start(out=gpl_sb[:], in_=gpl[:, :]), [], [gpl_sb])
                gpo_b = P.sb(es, "gpo_b", [128, 1024], F32)
                dma('sp', lambda e: e.dma_start(out=gpo_b[:], in_=gpo[0:1, :].partition_broadcast(128)), [], [gpo_b])
                G = [P.ps(es, "M%d" % i, [128, 512], F32) for i in range(6)]
                TRB = [P.ps(es, "TRC%d" % i, [128, 1024], BF16) for i in range(2)]
                gi = [0]; ti = [0]

                def gp3():
                    gi[0] = (gi[0] + 1) % len(G)
                    return G[gi[0]]

                def tp3():
                    ti[0] = (ti[0] + 1) % len(TRB)
                    return TRB[ti[0]]
                x1a = [P.sb(es, "x1a%d" % i, [128, 1024], F32) for i in range(2)]
                x1b = [P.sb(es, "x1b%d" % i, [128, 1024], F32) for i in range(2)]
                xn2 = [P.sb(es, "xn2_%d" % i, [128, 1024], BF16) for i in range(2)]
                junk3 = P.sb(es, "junk3", [128, 512], BF16)
                ssq = P.sb(es, "ssq3", [128, 2], F32); std = P.sb(es, "std3", [128, 2], F32); rstd = P.sb(es, "rstd3", [128, 2], F32)
                stat = (ssq, std, rstd)
                h2T = P.sb(es, "h2T", [128, 8, T], BF16)
                aT = P.sb(es, "aT", [128, 32, T], BF16)
                rt = [P.sb(es, "rt%d" % i, [128, T], F32) for i in range(2)]
                zn = [P.sb(es, "zn3_%d" % i, [128, 1024], F32) for i in range(2)]
                cnt = [0]
                tiles = [(i, T, 128) for i in range(NTILE)] + [(-1, NS, NS)]
                for (tile_id, Tn, npart) in tiles:
                    nsub = max(1, Tn // 128)
                    for sub in range(nsub):
                        xa = x1a[cnt[0] % 2]
                        if tile_id >= 0:
                            dma('sp', lambda e: e.dma_start(out=xa[:, :], in_=x1_s[tile_id, sub]), [x1_sb[tile_id]], [xa])
                        else:
                            dma('sp', lambda e: e.dma_start(out=xa[0:NS, :], in_=x1s_s[:, :]), [x1s_b], [xa])
                        norm_T(es, xa, npart, gpl_sb, identb, xn2[cnt[0] % 2], stat, tp3(), h2T, sub * npart)
                        cnt[0] += 1
                    for c in range(32):
                        g = gp3()
                        mm([(g[:, 0:Tn], [(Wup[:, kc, c * 128:(c + 1) * 128], h2T[:, kc, 0:Tn]) for kc in range(8)])], [Wup, h2T], [g])
                        r = rt[c % 2]
                        op('act', lambda e: e.activation(out=r[:, 0:Tn], in_=g[:, 0:Tn], func=AF.Relu), [g], [r])
                        op('dve' if c % 2 == 0 else 'pool', lambda e: e.tensor_tensor(out=aT[:, c, 0:Tn], in0=r[:, 0:Tn], in1=r[:, 0:Tn], op=ALU.mult), [r], [aT])
                    for sub in range(nsub):
                        tok = slice(sub * npart, (sub + 1) * npart)
                        xb, zb = x1b[cnt[0] % 2], zn[cnt[0] % 2]
                        cnt[0] += 1
                        if tile_id >= 0:
                            dma('sp', lambda e: e.dma_start(out=xb[:, :], in_=x1_s[tile_id, sub]), [x1_sb[tile_id]], [xb])
                        else:
                            dma('sp', lambda e: e.dma_start(out=xb[0:NS, :], in_=x1s_s[:, :]), [x1s_b], [xb])
                        gz = [gp3(), gp3()]
                        for half in range(2):
                            mm([(gz[half][0:npart, :], [(aT[:, c, tok], Wdn[:, c, half * 512:(half + 1) * 512]) for c in range(32)])], [aT, Wdn], [gz[half]])
                        op('dve', lambda e: e.memset(ssq[0:npart, 0:2], 0.0), [], [ssq])
                        for half in range(2):
                            op('act', lambda e: e.activation(out=junk3[0:npart, 0:512], in_=gz[half][0:npart, :], func=AF.Square,
                                                             accum_out=ssq[0:npart, half:half + 1]), [gz[half], ssq], [junk3, ssq])
                        op('dve', lambda e: e.tensor_tensor(out=ssq[0:npart, 0:1], in0=ssq[0:npart, 0:1], in1=ssq[0:npart, 1:2], op=ALU.add), [ssq], [ssq])
                        op('act', lambda e: e.activation(out=std[0:npart, 0:1], in_=ssq[0:npart, 0:1], func=AF.Sqrt, bias=EPS, scale=1.0 / 1024.0), [ssq], [std])
                        op('dve', lambda e: e.reciprocal(out=rstd[0:npart, 0:1], in_=std[0:npart, 0:1]), [std], [rstd])
                        for half in range(2):
                            hs = slice(half * 512, (half + 1) * 512)
                            op('dve', lambda e: e.scalar_tensor_tensor(out=zb[0:npart, hs], in0=gz[half][0:npart, :], scalar=rstd[0:npart, 0:1],
                                                                       in1=gpo_b[0:npart, hs], op0=ALU.mult, op1=ALU.mult), [gz[half], rstd, gpo_b], [zb])
                        op('pool', lambda e: e.tensor_tensor(out=zb[0:npart, :], in0=zb[0:npart, :], in1=xb[0:npart, :], op=ALU.add), [zb, xb], [zb])
                        if tile_id >= 0:
                            q, it = tile_id // NT, tile_id % NT
                            r0 = it * T + sub * 128
                            dma('sp', lambda e: e.dma_start(out=yp[q, r0:r0 + 128, :], in_=zb[:, :]), [zb], [])
                        else:
                            dma('sp', lambda e: e.dma_start(out=ysm[:, :], in_=zb[0:NS, :]), [zb], [])
            try:
                phase_a1()
                chk('A1')
                phase_a2()
                chk('A2')
                phase_b()
            except _Stop:
                pass
            P.barrier(engines=('sp',))
    except AssertionError:
        if not cfg.get('stop'):
            raise
    return nc, P


FULL_CFG = dict(S=2048, NSEQ=2, NS=4, NPG=128, NPOOL=5120)
_CACHE = {}


def make_in_maps(inp, cfg, n_cores):
    NSEQ, NS = cfg['NSEQ'], cfg['NS']
    f = lambda a: np.ascontiguousarray(a, dtype=np.float32)
    ck = f(inp['cache_k'][0]).reshape(-1, 512)
    cv = f(inp['cache_v'][0]).reshape(-1, 512)
    cl = f(inp['cache_logf'][0]).reshape(-1, 8)
    shared = dict(
        ck=ck, cv=cv, cl=cl,
        w_in=f(inp['w_in'][0]), gpre=f(inp['g_pre_mix'][0].reshape(8, 128).T), bfv=f(inp['b_f'][0].reshape(1, 8)),
        wau=f(inp['w_alpha_up'][0]), bal=f(inp['b_alpha'][0].reshape(1, 256)), ggn=f(inp['g_gla_norm'][0].reshape(1, 128)),
        wfo=f(inp['w_fox_out'][0]), wgo=f(inp['w_gla_out'][0]), wo=f(inp['w_o'][0]),
        gpm=f(inp['g_post_mix'][0].reshape(1, 1024)), gpl=f(inp['g_pre_mlp'][0].reshape(8, 128).T),
        wup=f(inp['w_up'][0]), wdn=f(inp['w_down'][0]), gpo=f(inp['g_post_mlp'][0].reshape(1, 1024)),
        cst=make_consts(),
    )
    maps = []
    for c in range(n_cores):
        m = dict(shared)
        m['xp'] = f(inp['x_prompt'][c * NSEQ:(c + 1) * NSEQ])
        m['xsm'] = f(inp['x_sample'][c * NS:(c + 1) * NS, 0, :])
        m['sg'] = f(inp['state_gla'][0, c * NS:(c + 1) * NS])
        m['pt'] = np.ascontiguousarray(inp['page_table'][c * NS:(c + 1) * NS].reshape(1, -1), dtype=np.int32)
        maps.append(m)
    return maps


def assemble(results, cfg, n_cores):
    S, NSEQ, NS = cfg['S'], cfg['NSEQ'], cfg['NS']
    cat = lambda k: np.concatenate([np.asarray(r[k]) for r in results], axis=0)
    B = NSEQ * n_cores
    DB = NS * n_cores
    return (
        cat('yp').reshape(B, S, 1024).astype(np.float32),
        cat('ysm').reshape(DB, 1, 1024).astype(np.float32),
        cat('kp').reshape(1, B, S, 8, 64).astype(np.float32),
        cat('vp').reshape(1, B, S, 8, 64).astype(np.float32),
        cat('lfp').reshape(1, B, S, 8).astype(np.float32),
        cat('gsp').reshape(1, B, 4, 64, 128).astype(np.float32),
        cat('ksm').reshape(1, DB, 1, 8, 64).astype(np.float32),
        cat('vsm').reshape(1, DB, 1, 8, 64).astype(np.float32),
        cat('lfs').reshape(1, DB, 1, 8).astype(np.float32),
        cat('gss').reshape(1, DB, 4, 64, 128).astype(np.float32),
    )


def run(inp, cfg, n_cores):
    key = tuple(sorted(cfg.items()))
    if key not in _CACHE:
        _CACHE[key] = build(cfg)[0]
    nc = _CACHE[key]
    maps = make_in_maps(inp, cfg, n_cores)
    res = run_bass_kernel_spmd(nc, maps, core_ids=list(range(n_cores)))
    return assemble(res.results, cfg, n_cores)


def kernel(**inputs):
    inp = {k: np.asarray(v) for k, v in inputs.items()}
    return run(inp, FULL_CFG, N_CORES)
```

```python
import contextlib
import numpy as np
import concourse.bass as bass
import concourse.mybir as mybir
from concourse.bass_utils import run_bass_kernel_spmd

F32 = mybir.dt.float32
BF16 = mybir.dt.bfloat16
I32 = mybir.dt.int32
AF = mybir.ActivationFunctionType
ALU = mybir.AluOpType
AX = mybir.AxisListType
EPS = 1e-6
N_CORES = 8


class _Stop(Exception):
    pass


class Buf:
    def __init__(self, name, t):
        self.name = name
        self.t = t
        self.w = None
        self.r = []
        self.psum = False

    def __getitem__(self, idx):
        return self.t[idx]


class Prog:
    NDMA = 40

    def __init__(self, nc, es):
        self.nc = nc
        self.eng = {'pe': nc.tensor, 'act': nc.scalar, 'dve': nc.vector, 'pool': nc.gpsimd, 'sp': nc.sync}
        self.sems = {}
        self.cnt = {}
        for e in ['pe', 'act', 'dve', 'pool']:
            self.sems[e] = es.enter_context(nc.semaphore('s_' + e))
            self.cnt[e] = 0
        self.dsems = [es.enter_context(nc.semaphore('s_dma%d' % i)) for i in range(self.NDMA)]
        self.dcnt = [0] * self.NDMA
        self.dnext = 0
        self.waited = {}
        self.ninst = 0

    def sb(self, es, name, shape, dt):
        return Buf(name, es.enter_context(self.nc.sbuf_tensor(name, list(shape), dt)))

    def ps(self, es, name, shape, dt=F32):
        b = Buf(name, es.enter_context(self.nc.psum_tensor(name, list(shape), dt)))
        b.psum = True
        return b

    def _sem(self, key):
        return self.sems[key] if isinstance(key, str) else self.dsems[key]

    def _wait(self, e, ev):
        key, val, _ = ev
        k = (e, key)
        if self.waited.get(k, 0) >= val:
            return
        self.waited[k] = val
        self.eng[e].wait_ge(self._sem(key), val)
        self.ninst += 1

    def _deps(self, e, reads, writes):
        best = {}

        def add(ev):
            if ev[0] not in best or best[ev[0]][1] < ev[1]:
                best[ev[0]] = ev
        for b in reads:
            if b.w is not None:
                add(b.w)
            if b.psum:
                for r in b.r:
                    if r[2] != e:
                        add(r)
        for b in writes:
            if b.w is not None and b.w[2] != e:
                add(b.w)
            for r in b.r:
                if r[2] != e:
                    add(r)
        for ev in best.values():
            self._wait(e, ev)

    def _record(self, ev, reads, writes):
        for b in reads:
            b.r = [r for r in b.r if r[0] != ev[0]] + [ev]
        for b in writes:
            b.w = ev
            b.r = []

    def op(self, e, fn, reads=(), writes=()):
        self._deps(e, reads, writes)
        inst = fn(self.eng[e])
        self.cnt[e] += 1
        inst.then_inc(self.sems[e], 1)
        self.ninst += 1
        self._record((e, self.cnt[e], e), reads, writes)

    def mm(self, groups, reads, writes):
        e = 'pe'
        self._deps(e, reads, writes)
        inst = None
        for out_ap, pairs in groups:
            n = len(pairs)
            for i, (l, r) in enumerate(pairs):
                inst = self.nc.tensor.matmul(out_ap, l, r, start=(i == 0), stop=(i == n - 1))
                self.ninst += 1
        self.cnt[e] += 1
        inst.then_inc(self.sems[e], 1)
        self._record((e, self.cnt[e], e), reads, writes)

    def mm_raw(self, out_ap, l, r, start, stop, reads, writes, skip=False):
        e = 'pe'
        self._deps(e, reads, writes)
        inst = self.nc.tensor.matmul(out_ap, l, r, start=start, stop=stop, skip_group_check=skip)
        self.cnt[e] += 1
        inst.then_inc(self.sems[e], 1)
        self.ninst += 1
        self._record((e, self.cnt[e], e), reads, writes)

    def tr(self, items, reads, writes):
        e = 'pe'
        self._deps(e, reads, writes)
        inst = None
        for out_ap, in_ap, ident_ap in items:
            inst = self.nc.tensor.transpose(out_ap, in_ap, ident_ap)
            self.ninst += 1
        self.cnt[e] += 1
        inst.then_inc(self.sems[e], 1)
        self._record((e, self.cnt[e], e), reads, writes)

    def dma(self, q, fn, reads=(), writes=()):
        k = self.dnext
        self.dnext = (self.dnext + 1) % self.NDMA
        if self.dcnt[k] > 0:
            self._wait(q, (k, self.dcnt[k], 'dma'))
        self._deps(q, reads, writes)
        inst = fn(self.eng[q])
        self.dcnt[k] += 16
        inst.then_inc(self.dsems[k], 16)
        self.ninst += 1
        self._record((k, self.dcnt[k], 'dma'), reads, writes)

    def barrier(self, engines=('pe', 'act', 'dve', 'pool', 'sp')):
        for e in engines:
            for o in ['pe', 'act', 'dve', 'pool']:
                if o != e and self.cnt[o] > 0:
                    self._wait(e, (o, self.cnt[o], o))
            for k in range(self.NDMA):
                if self.dcnt[k] > 0:
                    self._wait(e, (k, self.dcnt[k], 'dma'))


def make_consts():
    c = np.zeros((128, 1280), np.float32)
    i = np.arange(128)
    c[:, 0:128] = np.eye(128)
    tri_u = (i[:, None] <= i[None, :]).astype(np.float32)
    tri_su = (i[:, None] > i[None, :]).astype(np.float32)
    c[:, 128:256] = tri_u
    c[:, 256:384] = 1.0
    c[:, 384:512] = tri_su
    c[:, 512:640] = tri_u * (-1.0 / 16.0)
    c[:, 640:768] = tri_su * (-1.0 / 16.0)
    for h in range(8):
        c[h, 768 + h * 64:768 + (h + 1) * 64] = 1.0
    return c


def build(cfg):
    S, NSEQ, NS, NPG, NPOOL = cfg['S'], cfg['NSEQ'], cfg['NS'], cfg['NPG'], cfg['NPOOL']
    T = 512
    NT = S // T
    NKB = S // 128
    NTILE = NSEQ * NT
    nc = bass.Bass("TRN2", target_bir_lowering=False)

    def din(name, shape, dt=F32):
        return nc.dram_tensor(name, list(shape), dt, kind="ExternalInput").ap()

    def dout(name, shape, dt=F32):
        return nc.dram_tensor(name, list(shape), dt, kind="ExternalOutput").ap()

    def dscr(name, shape, dt=F32):
        return nc.dram_tensor(name, list(shape), dt, kind="Internal").ap()

    xp = din("xp", [NSEQ, S, 1024]); xsm = din("xsm", [NS, 1024])
    ck = din("ck", [NPOOL * 128, 512]); cv = din("cv", [NPOOL * 128, 512]); cl = din("cl", [NPOOL * 128, 8])
    sg = din("sg", [NS, 4, 64, 128]); pt = din("pt", [1, NS * NPG], I32)
    w_in = din("w_in", [1024, 5144]); gpre = din("gpre", [128, 8]); bfv = din("bfv", [1, 8])
    wau = din("wau", [16, 256]); bal = din("bal", [1, 256]); ggn = din("ggn", [1, 128])
    wfo = din("wfo", [512, 1024]); wgo = din("wgo", [512, 1024]); wo = din("wo", [1024, 1024])
    gpm = din("gpm", [1, 1024]); gpl = din("gpl", [128, 8])
    wup = din("wup", [1024, 4096]); wdn = din("wdn", [4096, 1024]); gpo = din("gpo", [1, 1024])
    cst = din("cst", [128, 1280])
    yp = dout("yp", [NSEQ, S, 1024]); ysm = dout("ysm", [NS, 1024])
    kp = dout("kp", [NSEQ, S, 512]); vp = dout("vp", [NSEQ, S, 512]); lfp = dout("lfp", [NSEQ, S, 8])
    gsp = dout("gsp", [NSEQ, 4, 64, 128])
    ksm = dout("ksm", [NS, 512]); vsm = dout("vsm", [NS, 512]); lfs = dout("lfs", [NS, 8])
    gss = dout("gss", [NS, 4, 64, 128])
    hT_s = dscr("hT_s", [NTILE, 128, 8, 512], BF16)
    oT_s = dscr("oT_s", [NTILE, 128, 8, 512], BF16)
    x1_s = dscr("x1_s", [NTILE, 4, 128, 1024])
    x1s_s = dscr("x1s_s", [NS, 1024])

    w_in_v = w_in.rearrange("(kc p) n -> p kc n", p=128)

    try:
        with contextlib.ExitStack() as es0:
            P = Prog(nc, es0)
            es0.enter_context(nc.Block())
            op, mm, dma = P.op, P.mm, P.dma
            hTs = P.sb(es0, "hTs", [128, 8, NS], BF16)
            oTs = P.sb(es0, "oTs", [128, 8, NS], BF16)
            hT_sb = [Buf("hT_s%d" % i, hT_s[i]) for i in range(NTILE)]
            oT_sb = [Buf("oT_s%d" % i, oT_s[i]) for i in range(NTILE)]
            x1_sb = [Buf("x1_s%d" % i, x1_s[i]) for i in range(NTILE)]
            x1s_b = Buf("x1s_s", x1s_s)

            def load_w(es, name, src_v, kc, ncols, c0=0, chunk=1024):
                wb = P.sb(es, name, [128, kc, ncols], BF16)
                for k0 in range(0, kc, 8):
                    for a in range(0, ncols, chunk):
                        b = min(ncols, a + chunk)
                        dma('pool', lambda e: e.dma_start(out=wb[:, k0:min(kc, k0 + 8), a:b], in_=src_v[:, k0:min(kc, k0 + 8), c0 + a:c0 + b]), [], [wb])
                return wb

            def norm_T(es_unused, xt, npart, g_sb, identb, xn, stat, trb, hT, col0):
                ssq, std, rstd = stat
                op('dve', lambda e: e.memset(ssq[0:npart, 0:1], 0.0), [], [ssq])
                op('act', lambda e: e.activation(out=xn[0:npart, :], in_=xt[0:npart, :], func=AF.Square,
                                                 accum_out=ssq[0:npart, 0:1]), [xt, ssq], [xn, ssq])
                op('act', lambda e: e.activation(out=std[0:npart, 0:1], in_=ssq[0:npart, 0:1], func=AF.Sqrt,
                                                 bias=EPS, scale=1.0 / 1024.0), [ssq], [std])
                op('dve', lambda e: e.reciprocal(out=rstd[0:npart, 0:1], in_=std[0:npart, 0:1]), [std], [rstd])
                op('dve', lambda e: e.tensor_scalar(out=xn[0:npart, :], in0=xt[0:npart, :], scalar1=rstd[0:npart, 0:1],
                                                    scalar2=None, op0=ALU.mult), [xt, rstd], [xn])
                P.tr([(trb[:, kc * 128:kc * 128 + npart], xn[0:npart, kc * 128:(kc + 1) * 128], identb[0:npart, 0:npart])
                      for kc in range(8)], [xn, identb], [trb])
                op('dve', lambda e: e.tensor_tensor(
                    out=hT[:, :, col0:col0 + npart],
                    in0=trb[:, :].rearrange("p (k t) -> p k t", k=8)[:, :, 0:npart],
                    in1=g_sb[:, 0:8].unsqueeze(2).to_broadcast([128, 8, npart]), op=ALU.mult), [trb, g_sb], [hT])

            def chk(stage):
                if cfg.get('stop') == stage:
                    raise _Stop()

            def phase_a1():
              with contextlib.ExitStack() as es:
                cf = P.sb(es, "cf", [128, 1280], F32)
                cb = P.sb(es, "cb", [128, 256], BF16)
                dma('sp', lambda e: e.dma_start(out=cf[:], in_=cst[:, :]), [], [cf])
                dma('pool', lambda e: e.dma_start(out=cb[:], in_=cst[:, 0:256]), [], [cb])
                ID = lambda n: cb[0:n, 0:n]
                MASKB = cb[:, 128:256]
                TRIU, ONES, TRISU, TRIU16, TRISU16 = cf[:, 128:256], cf[:, 256:384], cf[:, 384:512], cf[:, 512:640], cf[:, 640:768]
                BLKM = cf[0:8, 768:1280]
                gpre_sb = P.sb(es, "gpre_sb", [128, 8], F32)
                bf_b = P.sb(es, "bf_b", [128, 8], F32)
                wau_aug = P.sb(es, "wau_aug", [17, 256], F32)
                ggn_b = P.sb(es, "ggn_b", [128, 128], F32)
                dma('sp', lambda e: e.dma_start(out=gpre_sb[:], in_=gpre[:, :]), [], [gpre_sb])
                dma('sp', lambda e: e.dma_start(out=bf_b[:], in_=bfv[0:1, :].partition_broadcast(128)), [], [bf_b])
                dma('sp', lambda e: e.dma_start(out=wau_aug[0:16, :], in_=wau[:, :]), [], [wau_aug])
                dma('sp', lambda e: e.dma_start(out=wau_aug[16:17, :], in_=bal[0:1, :]), [], [wau_aug])
                dma('sp', lambda e: e.dma_start(out=ggn_b[:], in_=ggn[0:1, :].partition_broadcast(128)), [], [ggn_b])
                W1 = load_w(es, "W1", w_in_v, 8, 3096, 0, 516)
                CQ, CK, CV, CF, CQG, CKG, CVG, COG, CLR = 0, 512, 1024, 1536, 1544, 1800, 2056, 2568, 3080

                G = [P.ps(es, "G%d" % i, [128, 512], F32) for i in range(2)]
                STB = [P.ps(es, "STB%d" % i, [128, 512], F32) for i in range(2)]
                OACC = [P.ps(es, "OACC%d" % i, [128, 4, 128], F32) for i in range(2)]
                SACC = P.ps(es, "SACC", [128, 512], F32)
                TRB = [P.ps(es, "TRB%d" % i, [128, 1024], BF16) for i in range(1)]
                gi = [0]; ti = [0]; sti = [0]

                fox_mode = [False]

                def gp():
                    ring = G if fox_mode[0] else G + STB
                    gi[0] = (gi[0] + 1) % len(ring)
                    return ring[gi[0]]

                def tp():
                    ti[0] = (ti[0] + 1) % len(TRB)
                    return TRB[ti[0]]

                class Ring:
                    def __init__(self, name, n, shape, dt):
                        self.b = [P.sb(es, "%s%d" % (name, i), shape, dt) for i in range(n)]
                        self.i = 0

                    def next(self):
                        self.i = (self.i + 1) % len(self.b)
                        return self.b[self.i]

                xt_r = Ring("xt", 2, [128, 1024], F32)
                junk = P.sb(es, "junk", [128, 128], BF16)
                xn_r = Ring("xn", 2, [128, 1024], BF16)
                ssq = P.sb(es, "ssq", [128, 8], F32); std = P.sb(es, "std", [128, 8], F32); rstd = P.sb(es, "rstd", [128, 8], F32)
                stat = (ssq, std, rstd)
                hT = P.sb(es, "hT", [128, 8, T], BF16)
                QT = P.sb(es, "QT", [128, 8, T], BF16)
                KT = P.sb(es, "KT", [128, 4, S], BF16)
                Vaug = P.sb(es, "Vaug", [128, NKB, 8, 65], BF16)
                Cb = P.sb(es, "Cb", [128, NKB + 1, 8], F32)
                Ffull = P.sb(es, "Ffull", [128, NKB, 8], F32)
                biasT = P.sb(es, "biasT", [128, 4, NKB, 8], F32)
                kv_r = Ring("kvo", 2, [128, 512], F32)
                kb16_r = Ring("kb16", 2, [128, 512], BF16)
                lf_e = P.sb(es, "lf_e", [128, 8], F32)
                lf_r = Ring("lf", 2, [128, 8], F32)
                PT_r = Ring("PT", 8, [128, 128], BF16)
                rden_r = Ring("rden", 2, [128, 4], F32)
                ofx = P.sb(es, "ofx", [128, 4, 512], BF16)
                oT = P.sb(es, "oT", [128, 8, T], BF16)
                qgT = P.sb(es, "qgT", [128, 2, T], F32); kgT = P.sb(es, "kgT", [128, 2, T], F32)
                lrT = P.sb(es, "lrT", [17, T], F32)
                la_r = Ring("la", 2, [128, 256], F32)
                le_t = P.sb(es, "le_t", [128, 256], F32)
                kg_r = Ring("kg", 2, [128, 256], F32)
                vg_r = Ring("vg", 2, [128, 512], BF16)
                sog_r = Ring("sog", 2, [128, 512], BF16)
                EbT_r = Ring("EbT", 2, [128, 2, 128], F32); EnbT_r = Ring("EnbT", 2, [128, 2, 128], F32)
                qtl_r = Ring("qtl", 2, [128, 2, 128], F32); ktl_r = Ring("ktl", 2, [128, 2, 128], F32)
                ebrev = P.sb(es, "ebrev", [128, 256], F32)
                khat_r = Ring("khat", 2, [128, 256], BF16)
                ATs_r = Ring("ATs", 2, [128, 4, 128], BF16)
                Sst = P.sb(es, "Sst", [128, 2, 128], F32)
                t1 = P.sb(es, "t1", [128, 4, 128], F32)
                ogl_r = Ring("ogl", 2, [128, 512], BF16)

                op('dve', lambda e: e.memset(Vaug[:, :, :, 64:65], 1.0), [], [Vaug])
                op('dve', lambda e: e.memset(QT[:, :, :], 0.0), [], [QT])
                op('dve', lambda e: e.memset(lrT[:, :], 1.0), [], [lrT])

                chk('a1_setup')
                idxA = P.sb(es, "idxA", [128, NS * NPG], I32)
                idxB = P.sb(es, "idxB", [128, NS * NPG], F32)
                rid = P.sb(es, "rid", [128, 1], I32); ridf = P.sb(es, "ridf", [128, 1], F32)
                dma('sp', lambda e: e.dma_start(out=idxA[:], in_=pt[0:1, :].partition_broadcast(128)), [], [idxA])
                op('pool', lambda e: e.iota(rid[:], pattern=[[0, 1]], base=0, channel_multiplier=1), [], [rid])
                op('dve', lambda e: e.tensor_copy(out=ridf[:], in_=rid[:]), [rid], [ridf])
                op('dve', lambda e: e.tensor_copy(out=idxB[:], in_=idxA[:]), [idxA], [idxB])
                op('dve', lambda e: e.tensor_scalar(out=idxB[:], in0=idxB[:], scalar1=128.0, scalar2=ridf[:, 0:1],
                                                    op0=ALU.mult, op1=ALU.add), [idxB, ridf], [idxB])
                op('dve', lambda e: e.tensor_copy(out=idxA[:], in_=idxB[:]), [idxB], [idxA])

                row_r = Ring("row", 4, [1, 512], F32)
                qrow1 = P.sb(es, "qrow1", [1, 512], F32)
                krow1 = P.sb(es, "krow1", [1, 512], F32)
                vrowb1 = P.sb(es, "vrowb1", [1, 512], BF16)
                lfrow = [P.sb(es, "lfrow%d" % b, [1, 8], F32) for b in range(NS)]
                lrTs = P.sb(es, "lrTs", [17, NS], F32)
                qgTs = P.sb(es, "qgTs", [128, 2, NS], F32)
                laTs = P.sb(es, "laTs", [128, 2], F32); elaTs = P.sb(es, "elaTs", [128, 2], F32)
                Ssm = P.sb(es, "Ssm", [128, 2, 128], F32)
                one11 = cf[0:1, 256:257]

                chk('a1_idx')
                xs_t = xt_r.next()
                dma('sp', lambda e: e.dma_start(out=xs_t[0:NS, :], in_=xsm[:, :]), [], [xs_t])
                norm_T(es, xs_t, NS, gpre_sb, cb, xn_r.next(), stat, tp(), hTs, 0)
                chk('s1')
                op('dve', lambda e: e.memset(lrTs[:, :], 1.0), [], [lrTs])
                g = gp()
                mm([(g[0:16, 0:NS], [(W1[:, kc, CLR:CLR + 16], hTs[:, kc, 0:NS]) for kc in range(8)])], [W1, hTs], [g])
                op('dve', lambda e: e.tensor_copy(out=lrTs[0:16, :], in_=g[0:16, 0:NS]), [g], [lrTs])
                g = gp()
                mm([(g[:, hp * NS:(hp + 1) * NS], [(W1[:, kc, CQG + hp * 128:CQG + (hp + 1) * 128], hTs[:, kc, 0:NS]) for kc in range(8)])
                    for hp in range(2)], [W1, hTs], [g])
                op('act', lambda e: e.activation(out=qgTs[:, :, :], in_=g[:, 0:2 * NS].rearrange("p (h b) -> p h b", h=2),
                                                 func=AF.Copy, scale=0.125), [g], [qgTs])

                chk('s2')
                def rowproj(b, c0, n):
                    g = gp()
                    mm([(g[0:1, 0:n], [(hTs[:, kc, b:b + 1], W1[:, kc, c0:c0 + n]) for kc in range(8)])], [W1, hTs], [g])
                    return g

                def softplus_neg(dst, src_ap, npart, n, tmp, bias_ap=None, src_bufs=()):
                    if bias_ap is not None:
                        op('dve', lambda e: e.tensor_tensor(out=tmp[0:npart, 0:n], in0=src_ap, in1=bias_ap, op=ALU.add),
                           list(src_bufs), [tmp])
                        op('act', lambda e: e.activation(out=tmp[0:npart, 0:n], in_=tmp[0:npart, 0:n], func=AF.Exp, scale=-1.0),
                           [tmp], [tmp])
                    else:
                        op('act', lambda e: e.activation(out=tmp[0:npart, 0:n], in_=src_ap, func=AF.Exp, scale=-1.0),
                           list(src_bufs), [tmp])
                    op('act', lambda e: e.activation(out=dst[0:npart, 0:n], in_=tmp[0:npart, 0:n], func=AF.Ln, bias=1.0, scale=1.0),
                       [tmp], [dst])

                for b in range(NS):
                    g = rowproj(b, CK, 512)
                    kr = row_r.next()
                    op('act', lambda e: e.copy(out=kr[:, :], in_=g[0:1, :]), [g], [kr])
                    dma('sp', lambda e: e.dma_start(out=ksm[b:b + 1, :], in_=kr[:, :]), [kr], [])
                    g = rowproj(b, CV, 512)
                    vr = row_r.next()
                    op('act', lambda e: e.copy(out=vr[:, :], in_=g[0:1, :]), [g], [vr])
                    dma('sp', lambda e: e.dma_start(out=vsm[b:b + 1, :], in_=vr[:, :]), [vr], [])
                    chk('s3')
                    g = rowproj(b, CF, 8)
                    tmpr = row_r.next()
                    softplus_neg(lfrow[b], g[0:1, 0:8], 1, 8, tmpr, bias_ap=bf_b[0:1, :], src_bufs=[g, bf_b])
                    op('dve', lambda e: e.tensor_scalar(out=lfrow[b][:, :], in0=lfrow[b][:, :], scalar1=-1.0, scalar2=None,
                                                        op0=ALU.mult), [lfrow[b]], [lfrow[b]])
                    dma('sp', lambda e: e.dma_start(out=lfs[b:b + 1, :], in_=lfrow[b][:, :]), [lfrow[b]], [])
                    chk('s4')
                    g = rowproj(b, CKG, 256)
                    kgr = row_r.next()
                    op('act', lambda e: e.copy(out=kgr[:, 0:256], in_=g[0:1, 0:256]), [g], [kgr])
                    g = rowproj(b, CVG, 512)
                    vgr = row_r.next()
                    op('act', lambda e: e.copy(out=vgr[:, :], in_=g[0:1, :]), [g], [vgr])
                    g = rowproj(b, COG, 512)
                    sogr = row_r.next()
                    op('act', lambda e: e.activation(out=sogr[:, :], in_=g[0:1, :], func=AF.Silu), [g], [sogr])
                    chk('s5')
                    g = gp()
                    mm([(g[:, hp:hp + 1], [(wau_aug[0:17, hp * 128:(hp + 1) * 128], lrTs[0:17, b:b + 1])]) for hp in range(2)],
                       [wau_aug, lrTs], [g])
                    softplus_neg(laTs, g[:, 0:2], 128, 2, le_t, src_bufs=[g])
                    op('act', lambda e: e.activation(out=elaTs[:, :], in_=laTs[:, :], func=AF.Exp, scale=-1.0 / 16.0), [laTs], [elaTs])
                    chk('s6')
                    dma('sp', lambda e: e.dma_start(out=Ssm[:, :, :], in_=sg[b].rearrange("(hp dh) d e -> (dh d) hp e", dh=2)), [], [Ssm])
                    g = gp()
                    mm([(g[:, hp * 256:(hp + 1) * 256], [(kgr[0:1, hp * 128:(hp + 1) * 128], vgr[0:1, hp * 256:(hp + 1) * 256])])
                        for hp in range(2)], [kgr, vgr], [g])
                    for hp in range(2):
                        for dh in range(2):
                            pd = slice(dh * 64, dh * 64 + 64)
                            op('dve', lambda e: e.scalar_tensor_tensor(
                                out=Ssm[pd, hp, :], in0=Ssm[pd, hp, :], scalar=elaTs[pd, hp:hp + 1],
                                in1=g[pd, hp * 256 + dh * 128:hp * 256 + dh * 128 + 128], op0=ALU.mult, op1=ALU.add),
                               [Ssm, elaTs, g], [Ssm])
                    chk('s7')
                    dma('sp', lambda e: e.dma_start(out=gss[b].rearrange("(hp dh) d e -> (dh d) hp e", dh=2), in_=Ssm[:, :, :]), [Ssm], [])
                    chk('s7b')
                    gab = [gp(), gp()]
                    mm([(gab[hd % 2][0:1, (hd // 2) * 128:(hd // 2 + 1) * 128],
                         [(qgTs[(hd % 2) * 64:(hd % 2) * 64 + 64, hd // 2, b:b + 1], Ssm[(hd % 2) * 64:(hd % 2) * 64 + 64, hd // 2, :])])
                        for hd in (0, 2, 1, 3)], [qgTs, Ssm], gab)
                    chk('s8')
                    orow = row_r.next()
                    op('dve', lambda e: e.memset(ssq[0:1, 0:4], 0.0), [], [ssq])
                    for hd in range(4):
                        gsrc = gab[hd % 2]
                        op('act', lambda e: e.activation(out=junk[0:1, 0:128], in_=gsrc[0:1, (hd // 2) * 128:(hd // 2 + 1) * 128], func=AF.Square,
                                                         accum_out=ssq[0:1, hd:hd + 1]), [gsrc, ssq], [junk, ssq])
                    op('act', lambda e: e.activation(out=std[0:1, 0:4], in_=ssq[0:1, 0:4], func=AF.Sqrt, bias=EPS, scale=1.0 / 128.0), [ssq], [std])
                    op('dve', lambda e: e.reciprocal(out=rstd[0:1, 0:4], in_=std[0:1, 0:4]), [std], [rstd])
                    for hd in range(4):
                        gsrc = gab[hd % 2]
                        op('dve', lambda e: e.tensor_scalar(out=orow[0:1, hd * 128:(hd + 1) * 128], in0=gsrc[0:1, (hd // 2) * 128:(hd // 2 + 1) * 128],
                                                            scalar1=rstd[0:1, hd:hd + 1], scalar2=None, op0=ALU.mult), [gsrc, rstd], [orow])
                    op('dve', lambda e: e.tensor_tensor(out=orow[:, :].rearrange("p (h e) -> p h e", h=4),
                                                        in0=orow[:, :].rearrange("p (h e) -> p h e", h=4),
                                                        in1=ggn_b[0:1, :].unsqueeze(1).to_broadcast([1, 4, 128]), op=ALU.mult), [orow, ggn_b], [orow])
                    op('dve', lambda e: e.tensor_tensor(out=orow[:, :], in0=orow[:, :], in1=sogr[:, :], op=ALU.mult), [orow, sogr], [orow])
                    chk('s9')
                    g = gp()
                    mm([(g[:, kc:kc + 1], [(orow[0:1, kc * 128:(kc + 1) * 128], one11)]) for kc in range(4)], [orow, cf], [g])
                    op('dve', lambda e: e.tensor_copy(out=oTs[:, 4:8, b], in_=g[:, 0:4]), [g], [oTs])

                chk('a1_sproj')
                Kpg_r = Ring("Kpg", 3, [128, 512], BF16); Vpg_r = Ring("Vpg", 4, [128, 512], BF16)
                lpg_r = Ring("lpg", 4, [128, 8], F32)
                prod_r = Ring("prod", 2, [128, 512], F32)
                qbc = P.sb(es, "qbc", [128, 512], BF16)
                sc_r = Ring("sc", 2, [128, 8], F32); lg_r = Ring("lg", 2, [128, 8], F32)
                pb_r = Ring("pb", 3, [128, 8], BF16)
                carry = P.sb(es, "carry", [128, 8], F32); pacc = P.sb(es, "pacc", [128, 8], F32)
                den8 = P.sb(es, "den8", [8, 1], F32)

                def sample_decode():
                    for b in range(NS):
                        g = rowproj(b, CQ, 512)
                        op('act', lambda e: e.activation(out=qrow1[:, :], in_=g[0:1, :], func=AF.Copy, scale=0.125), [g], [qrow1])
                        g = rowproj(b, CK, 512)
                        op('act', lambda e: e.copy(out=krow1[:, :], in_=g[0:1, :]), [g], [krow1])
                        g = rowproj(b, CV, 512)
                        op('dve', lambda e: e.tensor_copy(out=vrowb1[:, :], in_=g[0:1, :]), [g], [vrowb1])
                        g = gp()
                        mm([(g[:, :], [(cf[0:1, 256:384], qrow1[0:1, :])])], [cf, qrow1], [g])
                        op('dve', lambda e: e.tensor_copy(out=qbc[:, :], in_=g[:, :]), [g], [qbc])
                        g = gp()
                        mm([(g[:, 0:8], [(cf[0:1, 256:384], lfrow[b][0:1, :])])], [cf, lfrow[b]], [g])
                        op('dve', lambda e: e.tensor_copy(out=carry[:, :], in_=g[:, 0:8]), [g], [carry])
                        op('dve', lambda e: e.memset(pacc[:, :], 0.0), [], [pacc])
                        pr = prod_r.next()
                        op('dve', lambda e: e.tensor_tensor(out=pr[0:1, :], in0=krow1[0:1, :], in1=qrow1[0:1, :], op=ALU.mult),
                           [krow1, qrow1], [pr])
                        sc = sc_r.next()
                        op('dve', lambda e: e.tensor_reduce(out=sc[0:1, :], in_=pr[0:1, :].rearrange("p (h d) -> p h d", h=8),
                                                            axis=AX.X, op=ALU.add), [pr], [sc])
                        pb = pb_r.next()
                        op('act', lambda e: e.activation(out=pb[0:1, :], in_=sc[0:1, :], func=AF.Exp), [sc], [pb])
                        op('dve', lambda e: e.tensor_tensor(out=pacc[0:1, :], in0=pacc[0:1, :], in1=pb[0:1, :], op=ALU.add), [pacc, pb], [pacc])
                        P.mm_raw(SACC[0:8, :], pb[0:1, :], vrowb1[0:1, :], True, False, [pb, vrowb1], [SACC])
                        yield
                        order = list(range(NPG - 1, -1, -1))
                        stp = {}

                        def stA(j):
                            col = b * NPG + j
                            Kp, Vp, lp = Kpg_r.next(), Vpg_r.next(), lpg_r.next()
                            off = bass.IndirectOffsetOnAxis(ap=idxA[:, col:col + 1], axis=0)
                            dma('pool', lambda e: e.indirect_dma_start(out=Kp[:, :], out_offset=None, in_=ck[:, :], in_offset=off), [idxA], [Kp])
                            dma('pool', lambda e: e.indirect_dma_start(out=Vp[:, :], out_offset=None, in_=cv[:, :], in_offset=off), [idxA], [Vp])
                            dma('pool', lambda e: e.indirect_dma_start(out=lp[:, :], out_offset=None, in_=cl[:, :], in_offset=off), [idxA], [lp])
                            stp[j] = [Kp, Vp, lp, None]

                        def stB(j):
                            Kp, Vp, lp, _ = stp[j]
                            g = gp()
                            mm([(g[:, 0:8], [(TRISU, lp[:, :])]), (g[:, 8:16], [(ONES, lp[:, :])])], [cf, lp], [g])
                            pr = prod_r.next()
                            op('dve', lambda e: e.tensor_tensor(out=pr[:, :], in0=Kp[:, :], in1=qbc[:, :], op=ALU.mult), [Kp, qbc], [pr])
                            sc = sc_r.next()
                            op('dve', lambda e: e.tensor_reduce(out=sc[:, :], in_=pr[:, :].rearrange("p (h d) -> p h d", h=8),
                                                                axis=AX.X, op=ALU.add), [pr], [sc])
                            lg = lg_r.next()
                            op('dve', lambda e: e.tensor_tensor(out=lg[:, :], in0=g[:, 0:8], in1=carry[:, :], op=ALU.add), [g, carry], [lg])
                            op('dve', lambda e: e.tensor_tensor(out=lg[:, :], in0=lg[:, :], in1=sc[:, :], op=ALU.add), [lg, sc], [lg])
                            op('dve', lambda e: e.tensor_tensor(out=carry[:, :], in0=g[:, 8:16], in1=carry[:, :], op=ALU.add), [g, carry], [carry])
                            pb = pb_r.next()
                            op('act', lambda e: e.activation(out=pb[:, :], in_=lg[:, :], func=AF.Exp), [lg], [pb])
                            op('dve', lambda e: e.tensor_tensor(out=pacc[:, :], in0=pacc[:, :], in1=pb[:, :], op=ALU.add), [pacc, pb], [pacc])
                            stp[j][3] = pb

                        def stC(j, last):
                            Kp, Vp, lp, pb = stp.pop(j)
                            P.mm_raw(SACC[0:8, :], pb[:, :], Vp[:, :], False, last, [pb, Vp], [SACC])

                        stA(order[0])
                        if NPG > 1:
                            stA(order[1])
                        for i, j in enumerate(order):
                            if i + 2 < NPG:
                                stA(order[i + 2])
                            stB(j)
                            if i >= 1:
                                stC(order[i - 1], False)
                            yield
                        stC(order[-1], True)
                        g = gp()
                        mm([(g[0:8, 0:1], [(pacc[:, :], cf[:, 256:257])])], [pacc, cf], [g])
                        op('dve', lambda e: e.reciprocal(out=den8[:, :], in_=g[0:8, 0:1]), [g], [den8])
                        on8 = prod_r.next()
                        op('dve', lambda e: e.scalar_tensor_tensor(out=on8[0:8, :], in0=SACC[0:8, :], scalar=den8[:, 0:1], in1=BLKM,
                                                                   op0=ALU.mult, op1=ALU.mult), [SACC, den8, cf], [on8])
                        g = gp()
                        mm([(g[:, kc:kc + 1], [(on8[0:8, kc * 128:(kc + 1) * 128], cf[0:8, 256:257])]) for kc in range(4)], [on8, cf], [g])
                        op('dve', lambda e: e.tensor_copy(out=oTs[:, 0:4, b], in_=g[:, 0:4]), [g], [oTs])
                        yield

                sgen = sample_decode()
                sdone = [False]

                def pump(n=1):
                    for _ in range(n):
                        if sdone[0]:
                            return
                        try:
                            next(sgen)
                        except StopIteration:
                            sdone[0] = True

                total_pumps = NS * (NPG + 2)
                n_slots = NSEQ * sum(8 * (4 * it + 4) for it in range(NT))
                ppump = max(1, -(-total_pumps // max(1, n_slots)))

                if cfg.get('stop') == 'a1_sdec':
                    while not sdone[0]:
                        pump(1)
                    raise _Stop()
                for q in range(NSEQ):
                    op('dve', lambda e: e.memset(Cb[:, 0, :], 0.0), [], [Cb])
                    op('dve', lambda e: e.memset(Sst[:, :, :], 0.0), [], [Sst])
                    for it in range(NT):
                        tile_id = q * NT + it
                        t0 = it * T
                        for sub in range(4):
                            r0 = t0 + sub * 128
                            kb = it * 4 + sub
                            xt = xt_r.next()
                            dma('sp', lambda e: e.dma_start(out=xt[:, :], in_=xp[q, r0:r0 + 128, :]), [], [xt])
                            norm_T(es, xt, 128, gpre_sb, cb, xn_r.next(), stat, tp(), hT, sub * 128)
                            chk('p1')
                            tok = slice(sub * 128, sub * 128 + 128)

                            def tproj(c0, n):
                                g = gp()
                                mm([(g[:, 0:n], [(hT[:, kc, tok], W1[:, kc, c0:c0 + n]) for kc in range(8)])], [hT, W1], [g])
                                return g
                            g = tproj(CK, 512)
                            chk('p1a')
                            ko = kv_r.next()
                            op('act', lambda e: e.copy(out=ko[:, :], in_=g[:, :]), [g], [ko])
                            chk('p1b')
                            k16 = kb16_r.next()
                            op('dve', lambda e: e.tensor_copy(out=k16[:, :], in_=ko[:, :]), [ko], [k16])
                            chk('p1c')
                            dma('sp', lambda e: e.dma_start(out=kp[q, r0:r0 + 128, :], in_=ko[:, :]), [ko], [])
                            chk('p2')
                            tb = tp()
                            P.tr([(tb[:, hp * 128:(hp + 1) * 128], k16[:, hp * 128:(hp + 1) * 128], ID(128)) for hp in range(4)], [k16, cb], [tb])
                            op('act', lambda e: e.copy(out=KT[:, :, kb * 128:(kb + 1) * 128],
                                                       in_=tb[:, 0:512].rearrange("p (h t) -> p h t", h=4)), [tb], [KT])
                            chk('p3')
                            g = tproj(CV, 512)
                            vo = kv_r.next()
                            op('act', lambda e: e.copy(out=vo[:, :], in_=g[:, :]), [g], [vo])
                            op('dve', lambda e: e.tensor_copy(out=Vaug[:, kb, :, 0:64], in_=vo[:, :].rearrange("p (h d) -> p h d", h=8)), [vo], [Vaug])
                            dma('sp', lambda e: e.dma_start(out=vp[q, r0:r0 + 128, :], in_=vo[:, :]), [vo], [])
                            chk('p4')
                            g = tproj(CF, 8)
                            lf = lf_r.next()
                            softplus_neg(lf, g[:, 0:8], 128, 8, lf_e, bias_ap=bf_b[:, :], src_bufs=[g, bf_b])
                            op('dve', lambda e: e.tensor_scalar(out=lf[:, :], in0=lf[:, :], scalar1=-1.0, scalar2=None, op0=ALU.mult), [lf], [lf])
                            dma('sp', lambda e: e.dma_start(out=lfp[q, r0:r0 + 128, :], in_=lf[:, :]), [lf], [])
                            chk('p5')
                            g = gp()
                            mm([(g[:, 0:8], [(TRIU, lf[:, :])]), (g[:, 8:16], [(ONES, lf[:, :])])], [cf, lf], [g])
                            op('dve', lambda e: e.tensor_tensor(out=Ffull[:, kb, :], in0=g[:, 0:8], in1=Cb[:, kb, :], op=ALU.add), [g, Cb], [Ffull])
                            op('dve', lambda e: e.tensor_tensor(out=Cb[:, kb + 1, :], in0=g[:, 8:16], in1=Cb[:, kb, :], op=ALU.add), [g, Cb], [Cb])
                        chk('a1_t1')
                        for hp in range(4):
                            g = gp()
                            mm([(g[:, :], [(W1[:, kc, CQ + hp * 128:CQ + (hp + 1) * 128], hT[:, kc, :]) for kc in range(8)])], [W1, hT], [g])
                            op('act', lambda e: e.activation(out=QT[0:64, 2 * hp, :], in_=g[0:64, :], func=AF.Copy, scale=0.125), [g], [QT])
                            op('act', lambda e: e.activation(out=QT[64:128, 2 * hp + 1, :], in_=g[64:128, :], func=AF.Copy, scale=0.125), [g], [QT])
                        for jl in range(4):
                            jg = it * 4 + jl
                            op('dve', lambda e: e.tensor_tensor(out=biasT[:, jl, 0:jg + 1, :],
                                                                in0=Cb[:, jg:jg + 1, :].to_broadcast([128, jg + 1, 8]),
                                                                in1=Ffull[:, 0:jg + 1, :], op=ALU.subtract), [Cb, Ffull], [biasT])
                        nkb = it * 4 + 4
                        slots = [(h, kb) for h in range(8) for kb in range(nkb)]

                        def issue_st(h, kb):
                            hp, dh = h // 2, h % 2
                            pd = slice(dh * 64, dh * 64 + 64)
                            c0 = max(0, kb - it * 4) * 128
                            sti[0] += 1
                            g = STB[sti[0] % 2]
                            mm([(g[:, c0:512], [(KT[:, hp, kb * 128:(kb + 1) * 128], QT[:, h, c0:512])])], [KT, QT], [g])
                            return g
                        ahead = cfg.get('st_ahead', True)
                        fox_mode[0] = True
                        g_next = issue_st(*slots[0]) if ahead else None
                        for si, (h, kb) in enumerate(slots):
                            if ahead:
                                g = g_next
                                g_next = issue_st(*slots[si + 1]) if si + 1 < len(slots) else None
                            else:
                                g = issue_st(h, kb)
                            oacc = OACC[h % 2]
                            jl0 = max(0, kb - it * 4)
                            if kb == 0 and cfg.get('zero_acc'):
                                op('dve', lambda e: e.memset(oacc[:, :, :], 0.0), [], [oacc])
                            for jl in range(jl0, 4):
                                jg = it * 4 + jl
                                pt_ = PT_r.next()
                                op('act', lambda e: e.activation(out=pt_[:, :], in_=g[:, jl * 128:(jl + 1) * 128], func=AF.Exp,
                                                                 bias=biasT[:, jl, kb, h:h + 1], scale=1.0), [g, biasT], [pt_])
                                if kb == jg:
                                    op(cfg.get('mask_eng', 'dve'), lambda e: e.tensor_tensor(out=pt_[:, :], in0=pt_[:, :], in1=MASKB, op=ALU.mult), [pt_, cb], [pt_])
                                P.mm_raw(oacc[:, jl, 0:65], pt_[:, :], Vaug[:, kb, h, :], (kb == 0 and jl == 0) and not cfg.get('zero_acc'), (kb == jg), [pt_, Vaug], [oacc], skip=True)
                            pump(ppump)
                            if kb == nkb - 1:
                                rd = rden_r.next()
                                op('dve', lambda e: e.reciprocal(out=rd[:, :], in_=oacc[:, :, 64]), [oacc], [rd])
                                op('dve', lambda e: e.tensor_tensor(out=ofx[:, :, h * 64:(h + 1) * 64], in0=oacc[:, :, 0:64],
                                                                    in1=rd[:, 0:4].unsqueeze(2).to_broadcast([128, 4, 64]), op=ALU.mult), [oacc, rd], [ofx])
                        fox_mode[0] = False
                        chk('a1_t4')
                        for half in range(2):
                            tb = tp()
                            P.tr([(tb[:, (i * 4 + jl) * 128:(i * 4 + jl + 1) * 128], ofx[:, jl, (half * 2 + i) * 128:(half * 2 + i + 1) * 128], ID(128))
                                  for i in range(2) for jl in range(4)], [ofx, cb], [tb])
                            op('act', lambda e: e.copy(out=oT[:, half * 2:half * 2 + 2, :],
                                                       in_=tb[:, :].rearrange("p (c t) -> p c t", c=2)), [tb], [oT])
                        chk('a1_t4b')
                        for hp in range(2):
                            g = gp()
                            mm([(g[:, :], [(W1[:, kc, CQG + hp * 128:CQG + (hp + 1) * 128], hT[:, kc, :]) for kc in range(8)])], [W1, hT], [g])
                            op('act', lambda e: e.activation(out=qgT[:, hp, :], in_=g[:, :], func=AF.Copy, scale=0.125), [g], [qgT])
                            g = gp()
                            mm([(g[:, :], [(W1[:, kc, CKG + hp * 128:CKG + (hp + 1) * 128], hT[:, kc, :]) for kc in range(8)])], [W1, hT], [g])
                            op('act', lambda e: e.copy(out=kgT[:, hp, :], in_=g[:, :]), [g], [kgT])
                        g = gp()
                        mm([(g[0:16, :], [(W1[:, kc, CLR:CLR + 16], hT[:, kc, :]) for kc in range(8)])], [W1, hT], [g])
                        op('dve', lambda e: e.tensor_copy(out=lrT[0:16, :], in_=g[0:16, :]), [g], [lrT])
                        for sub in range(4):
                            tok = slice(sub * 128, sub * 128 + 128)
                            g = gp()
                            mm([(g[:, 0:256], [(hT[:, kc, tok], W1[:, kc, CKG:CKG + 256]) for kc in range(8)])], [hT, W1], [g])
                            kg = kg_r.next()
                            op('act', lambda e: e.copy(out=kg[:, :], in_=g[:, 0:256]), [g], [kg])
                            g = gp()
                            mm([(g[:, :], [(hT[:, kc, tok], W1[:, kc, CVG:CVG + 512]) for kc in range(8)])], [hT, W1], [g])
                            vg = vg_r.next()
                            op('dve', lambda e: e.tensor_copy(out=vg[:, :], in_=g[:, :]), [g], [vg])
                            g = gp()
                            mm([(g[:, :], [(hT[:, kc, tok], W1[:, kc, COG:COG + 512]) for kc in range(8)])], [hT, W1], [g])
                            sog = sog_r.next()
                            op('act', lambda e: e.activation(out=sog[:, :], in_=g[:, :], func=AF.Silu), [g], [sog])
                            g = gp()
                            mm([(g[:, 0:256], [(lrT[0:17, tok], wau_aug[0:17, :])])], [lrT, wau_aug], [g])
                            la = la_r.next()
                            softplus_neg(la, g[:, 0:256], 128, 256, le_t, src_bufs=[g])
                            g = gp()
                            mm([(g[:, hp * 128:(hp + 1) * 128], [(la[:, hp * 128:(hp + 1) * 128], TRIU16)]) for hp in range(2)]
                               + [(g[:, 256:512], [(TRISU16, la[:, :])])], [la, cf], [g])
                            EbT, EnbT = EbT_r.next(), EnbT_r.next()
                            gv = g[:, 0:256].rearrange("p (h t) -> p h t", h=2)
                            op('act', lambda e: e.activation(out=EbT[:, :, :], in_=gv, func=AF.Exp), [g], [EbT])
                            op('act', lambda e: e.activation(out=EnbT[:, :, :], in_=gv, func=AF.Exp, scale=-1.0), [g], [EnbT])
                            op('act', lambda e: e.activation(out=ebrev[:, :], in_=g[:, 256:512], func=AF.Exp), [g], [ebrev])
                            qtl, ktl, khat = qtl_r.next(), ktl_r.next(), khat_r.next()
                            op('dve', lambda e: e.tensor_tensor(out=qtl[:, :, :], in0=qgT[:, :, tok], in1=EbT[:, :, :], op=ALU.mult), [qgT, EbT], [qtl])
                            op('pool', lambda e: e.tensor_tensor(out=ktl[:, :, :], in0=kgT[:, :, tok], in1=EnbT[:, :, :], op=ALU.mult), [kgT, EnbT], [ktl])
                            op('pool', lambda e: e.tensor_tensor(out=khat[:, :], in0=kg[:, :], in1=ebrev[:, :], op=ALU.mult), [kg, ebrev], [khat])
                            gat = [gp(), gp()]
                            mm([(gat[hd % 2][:, (hd // 2) * 128:(hd // 2 + 1) * 128],
                                 [(ktl[(hd % 2) * 64:(hd % 2) * 64 + 64, hd // 2, :], qtl[(hd % 2) * 64:(hd % 2) * 64 + 64, hd // 2, :])])
                                for hd in (0, 2, 1, 3)], [ktl, qtl], gat)
                            ATs = ATs_r.next()
                            for dh in range(2):
                                op('dve', lambda e: e.tensor_tensor(out=ATs[:, dh * 2:dh * 2 + 2, :],
                                                                    in0=gat[dh][:, 0:256].rearrange("p (h t) -> p h t", h=2),
                                                                    in1=cf[:, 128:256].unsqueeze(1).to_broadcast([128, 2, 128]), op=ALU.mult),
                                   [gat[dh], cf], [ATs])
                            gov = [gp(), gp()]
                            mm([(gov[hd % 2][:, (hd // 2) * 128:(hd // 2 + 1) * 128],
                                 [(qtl[(hd % 2) * 64:(hd % 2) * 64 + 64, hd // 2, :], Sst[(hd % 2) * 64:(hd % 2) * 64 + 64, hd // 2, :]),
                                  (ATs[:, (hd % 2) * 2 + hd // 2, :], vg[:, hd * 128:(hd + 1) * 128])]) for hd in (0, 2, 1, 3)],
                               [qtl, Sst, ATs, vg], gov)
                            gs = gp()
                            mm([(gs[:, hp * 256:(hp + 1) * 256], [(khat[:, hp * 128:(hp + 1) * 128], vg[:, hp * 256:(hp + 1) * 256])])
                                for hp in range(2)], [khat, vg], [gs])
                            for hp in range(2):
                                for dh in range(2):
                                    pd = slice(dh * 64, dh * 64 + 64)
                                    op('dve', lambda e: e.scalar_tensor_tensor(
                                        out=Sst[pd, hp, :], in0=Sst[pd, hp, :], scalar=EbT[pd, hp, 127:128],
                                        in1=gs[pd, hp * 256 + dh * 128:hp * 256 + dh * 128 + 128], op0=ALU.mult, op1=ALU.add),
                                       [Sst, EbT, gs], [Sst])
                            op('dve', lambda e: e.memset(ssq[:, 0:4], 0.0), [], [ssq])
                            for hd in range(4):
                                gsrc = gov[hd % 2]
                                op('act', lambda e: e.activation(out=junk[:, 0:128], in_=gsrc[:, (hd // 2) * 128:(hd // 2 + 1) * 128], func=AF.Square,
                                                                 accum_out=ssq[:, hd:hd + 1]), [gsrc, ssq], [junk, ssq])
                            op('act', lambda e: e.activation(out=std[:, 0:4], in_=ssq[:, 0:4], func=AF.Sqrt, bias=EPS, scale=1.0 / 128.0), [ssq], [std])
                            op('dve', lambda e: e.reciprocal(out=rstd[:, 0:4], in_=std[:, 0:4]), [std], [rstd])
                            for dh in range(2):
                                op('dve', lambda e: e.tensor_tensor(
                                    out=t1[:, :, :].rearrange("p (hp dh) e -> p hp dh e", dh=2)[:, :, dh, :],
                                    in0=gov[dh][:, 0:256].rearrange("p (h e) -> p h e", h=2),
                                    in1=rstd[:, 0:4].rearrange("p (hp dh) -> p hp dh", dh=2)[:, :, dh].unsqueeze(2).to_broadcast([128, 2, 128]),
                                    op=ALU.mult), [gov[dh], rstd], [t1])
                            op('pool', lambda e: e.tensor_tensor(out=t1[:, :, :], in0=t1[:, :, :],
                                                                 in1=ggn_b[:, :].unsqueeze(1).to_broadcast([128, 4, 128]), op=ALU.mult), [t1, ggn_b], [t1])
                            ogl = ogl_r.next()
                            op('pool', lambda e: e.tensor_tensor(out=ogl[:, :], in0=t1[:, :, :].rearrange("p h e -> p (h e)"), in1=sog[:, :], op=ALU.mult),
                               [t1, sog], [ogl])
                            tb = tp()
                            P.tr([(tb[:, kc * 128:(kc + 1) * 128], ogl[:, kc * 128:(kc + 1) * 128], ID(128)) for kc in range(4)], [ogl, cb], [tb])
                            op('act', lambda e: e.copy(out=oT[:, 4:8, tok], in_=tb[:, 0:512].rearrange("p (c t) -> p c t", c=4)), [tb], [oT])
                        chk('a1_t5')
                        dma('sp', lambda e: e.dma_start(out=hT_s[tile_id], in_=hT[:, :, :]), [hT], [hT_sb[tile_id]])
                        dma('sp', lambda e: e.dma_start(out=oT_s[tile_id], in_=oT[:, :, :]), [oT], [oT_sb[tile_id]])
                    dma('sp', lambda e: e.dma_start(out=gsp[q].rearrange("(hp dh) d e -> (dh d) hp e", dh=2), in_=Sst[:, :, :]), [Sst], [])
                while not sdone[0]:
                    pump(1)
                P.barrier()

            def phase_a2():
              with contextlib.ExitStack() as es:
                Wg = load_w(es, "Wg", w_in_v, 8, 2048, 3096, 512)
                Wfo = load_w(es, "Wfo", wfo.rearrange("(kc p) n -> p kc n", p=128), 4, 1024, 0, 1024)
                Wgo = load_w(es, "Wgo", wgo.rearrange("(kc p) n -> p kc n", p=128), 4, 1024, 0, 1024)
                Wo = load_w(es, "Wo", wo.rearrange("(kc p) n -> p kc n", p=128), 8, 1024, 0, 512)
                gpm_b = P.sb(es, "gpm_b", [128, 1024], F32)
                dma('sp', lambda e: e.dma_start(out=gpm_b[:], in_=gpm[0:1, :].partition_broadcast(128)), [], [gpm_b])
                G = [P.ps(es, "H%d" % i, [128, 512], F32) for i in range(8)]
                gi = [0]

                def gp2():
                    gi[0] = (gi[0] + 1) % len(G)
                    return G[gi[0]]
                hT2 = [P.sb(es, "hT2_%d" % i, [128, 8, T], BF16) for i in range(2)]
                oT2 = [P.sb(es, "oT2_%d" % i, [128, 8, T], BF16) for i in range(2)]
                s1 = [P.sb(es, "s1_%d" % i, [128, T], F32) for i in range(2)]
                s2 = [P.sb(es, "s2_%d" % i, [128, T], F32) for i in range(2)]
                uT = P.sb(es, "uT", [128, 8, T], BF16)
                xr = [P.sb(es, "xr%d" % i, [128, 1024], F32) for i in range(2)]
                zn = [P.sb(es, "zn%d" % i, [128, 1024], F32) for i in range(2)]
                junk2 = P.sb(es, "junk2", [128, 512], BF16)
                ssq = P.sb(es, "ssq2", [128, 2], F32); std = P.sb(es, "std2", [128, 2], F32); rstd = P.sb(es, "rstd2", [128, 2], F32)
                cnt = [0]
                tiles = [(i, T, 128) for i in range(NTILE)] + [(-1, NS, NS)]
                for (tile_id, Tn, npart) in tiles:
                    nsub = max(1, Tn // 128)
                    if tile_id >= 0:
                        hb, ob = hT2[tile_id % 2], oT2[tile_id % 2]
                        dma('sp', lambda e: e.dma_start(out=hb[:, :, :], in_=hT_s[tile_id]), [hT_sb[tile_id]], [hb])
                        dma('sp', lambda e: e.dma_start(out=ob[:, :, :], in_=oT_s[tile_id]), [oT_sb[tile_id]], [ob])
                    else:
                        hb, ob = hTs, oTs
                    for c in range(8):
                        cs = slice(c * 128, (c + 1) * 128)
                        ga, gb, g1, g2 = gp2(), gp2(), gp2(), gp2()
                        mm([(g1[:, 0:Tn], [(Wg[:, kc, c * 128:(c + 1) * 128], hb[:, kc, 0:Tn]) for kc in range(8)])], [Wg, hb], [g1])
                        mm([(g2[:, 0:Tn], [(Wg[:, kc, 1024 + c * 128:1024 + (c + 1) * 128], hb[:, kc, 0:Tn]) for kc in range(8)])], [Wg, hb], [g2])
                        mm([(ga[:, 0:Tn], [(Wfo[:, kc, cs], ob[:, kc, 0:Tn]) for kc in range(4)])], [Wfo, ob], [ga])
                        mm([(gb[:, 0:Tn], [(Wgo[:, kc, cs], ob[:, 4 + kc, 0:Tn]) for kc in range(4)])], [Wgo, ob], [gb])
                        a1, a2 = s1[c % 2], s2[c % 2]
                        op('act', lambda e: e.activation(out=a1[:, 0:Tn], in_=g1[:, 0:Tn], func=AF.Sigmoid), [g1], [a1])
                        op('act', lambda e: e.activation(out=a2[:, 0:Tn], in_=g2[:, 0:Tn], func=AF.Sigmoid), [g2], [a2])
                        op('dve', lambda e: e.tensor_tensor(out=a1[:, 0:Tn], in0=a1[:, 0:Tn], in1=ga[:, 0:Tn], op=ALU.mult), [a1, ga], [a1])
                        op('dve', lambda e: e.tensor_tensor(out=a2[:, 0:Tn], in0=a2[:, 0:Tn], in1=gb[:, 0:Tn], op=ALU.mult), [a2, gb], [a2])
                        op('pool', lambda e: e.tensor_tensor(out=uT[:, c, 0:Tn], in0=a1[:, 0:Tn], in1=a2[:, 0:Tn], op=ALU.add), [a1, a2], [uT])
                    for sub in range(nsub):
                        tok = slice(sub * npart, (sub + 1) * npart)
                        xb, zb = xr[cnt[0] % 2], zn[cnt[0] % 2]
                        cnt[0] += 1
                        if tile_id >= 0:
                            q, it = tile_id // NT, tile_id % NT
                            r0 = it * T + sub * 128
                            dma('sp', lambda e: e.dma_start(out=xb[:, :], in_=xp[q, r0:r0 + 128, :]), [], [xb])
                        else:
                            dma('sp', lambda e: e.dma_start(out=xb[0:NS, :], in_=xsm[:, :]), [], [xb])
                        gz = [gp2(), gp2()]
                        for half in range(2):
                            mm([(gz[half][0:npart, :], [(uT[:, kc, tok], Wo[:, kc, half * 512:(half + 1) * 512]) for kc in range(8)])], [uT, Wo], [gz[half]])
                        op('dve', lambda e: e.memset(ssq[0:npart, 0:2], 0.0), [], [ssq])
                        for half in range(2):
                            op('act', lambda e: e.activation(out=junk2[0:npart, :], in_=gz[half][0:npart, :], func=AF.Square,
                                                             accum_out=ssq[0:npart, half:half + 1]), [gz[half], ssq], [junk2, ssq])
                        op('dve', lambda e: e.tensor_tensor(out=ssq[0:npart, 0:1], in0=ssq[0:npart, 0:1], in1=ssq[0:npart, 1:2], op=ALU.add), [ssq], [ssq])
                        op('act', lambda e: e.activation(out=std[0:npart, 0:1], in_=ssq[0:npart, 0:1], func=AF.Sqrt, bias=EPS, scale=1.0 / 1024.0), [ssq], [std])
                        op('dve', lambda e: e.reciprocal(out=rstd[0:npart, 0:1], in_=std[0:npart, 0:1]), [std], [rstd])
                        for half in range(2):
                            hs = slice(half * 512, (half + 1) * 512)
                            op('dve', lambda e: e.scalar_tensor_tensor(out=zb[0:npart, hs], in0=gz[half][0:npart, :], scalar=rstd[0:npart, 0:1],
                                                                       in1=gpm_b[0:npart, hs], op0=ALU.mult, op1=ALU.mult), [gz[half], rstd, gpm_b], [zb])
                        op('pool', lambda e: e.tensor_tensor(out=zb[0:npart, :], in0=zb[0:npart, :], in1=xb[0:npart, :], op=ALU.add), [zb, xb], [zb])
                        if tile_id >= 0:
                            dma('sp', lambda e: e.dma_start(out=x1_s[tile_id, sub], in_=zb[:, :]), [zb], [x1_sb[tile_id]])
                        else:
                            dma('sp', lambda e: e.dma_start(out=x1s_s[:, :], in_=zb[0:NS, :]), [zb], [x1s_b])
                P.barrier()

            def phase_b():
              with contextlib.ExitStack() as es:
                Wup = load_w(es, "Wup", wup.rearrange("(kc p) n -> p kc n", p=128), 8, 4096, 0, 512)
                Wdn = load_w(es, "Wdn", wdn.rearrange("(kc p) n -> p kc n", p=128), 32, 1024, 0, 512)
                identb = P.sb(es, "identb", [128, 128], BF16)
                dma('pool', lambda e: e.dma_start(out=identb[:], in_=cst[:, 0:128]), [], [identb])
                gpl_sb = P.sb(es, "gpl_sb", [128, 8], F32)
                dma('sp', lambda e: e.dma_start(out=gpl_sb[:], in_=gpl[:, :]), [], [gpl_sb])
                gpo_b = P.sb(es, "gpo_b", [128, 1024], F32)
                dma('sp', lambda e: e.dma_start(out=gpo_b[:], in_=gpo[0:1, :].partition_broadcast(128)), [], [gpo_b])
                G = [P.ps(es, "M%d" % i, [128, 512], F32) for i in range(6)]
                TRB = [P.ps(es, "TRC%d" % i, [128, 1024], BF16) for i in range(2)]
                gi = [0]; ti = [0]

                def gp3():
                    gi[0] = (gi[0] + 1) % len(G)
                    return G[gi[0]]

                def tp3():
                    ti[0] = (ti[0] + 1) % len(TRB)
                    return TRB[ti[0]]
                x1a = [P.sb(es, "x1a%d" % i, [128, 1024], F32) for i in range(2)]
                x1b = [P.sb(es, "x1b%d" % i, [128, 1024], F32) for i in range(2)]
                xn2 = [P.sb(es, "xn2_%d" % i, [128, 1024], BF16) for i in range(2)]
                junk3 = P.sb(es, "junk3", [128, 512], BF16)
                ssq = P.sb(es, "ssq3", [128, 2], F32); std = P.sb(es, "std3", [128, 2], F32); rstd = P.sb(es, "rstd3", [128, 2], F32)
                stat = (ssq, std, rstd)
                h2T = P.sb(es, "h2T", [128, 8, T], BF16)
                aT = P.sb(es, "aT", [128, 32, T], BF16)
                rt = [P.sb(es, "rt%d" % i, [128, T], F32) for i in range(2)]
                zn = [P.sb(es, "zn3_%d" % i, [128, 1024], F32) for i in range(2)]
                cnt = [0]
                tiles = [(i, T, 128) for i in range(NTILE)] + [(-1, NS, NS)]
                for (tile_id, Tn, npart) in tiles:
                    nsub = max(1, Tn // 128)
                    for sub in range(nsub):
                        xa = x1a[cnt[0] % 2]
                        if tile_id >= 0:
                            dma('sp', lambda e: e.dma_start(out=xa[:, :], in_=x1_s[tile_id, sub]), [x1_sb[tile_id]], [xa])
                        else:
                            dma('sp', lambda e: e.dma_start(out=xa[0:NS, :], in_=x1s_s[:, :]), [x1s_b], [xa])
                        norm_T(es, xa, npart, gpl_sb, identb, xn2[cnt[0] % 2], stat, tp3(), h2T, sub * npart)
                        cnt[0] += 1
                    for c in range(32):
                        g = gp3()
                        mm([(g[:, 0:Tn], [(Wup[:, kc, c * 128:(c + 1) * 128], h2T[:, kc, 0:Tn]) for kc in range(8)])], [Wup, h2T], [g])
                        r = rt[c % 2]
                        op('act', lambda e: e.activation(out=r[:, 0:Tn], in_=g[:, 0:Tn], func=AF.Relu), [g], [r])
                        op('dve' if c % 2 == 0 else 'pool', lambda e: e.tensor_tensor(out=aT[:, c, 0:Tn], in0=r[:, 0:Tn], in1=r[:, 0:Tn], op=ALU.mult), [r], [aT])
                    for sub in range(nsub):
                        tok = slice(sub * npart, (sub + 1) * npart)
                        xb, zb = x1b[cnt[0] % 2], zn[cnt[0] % 2]
                        cnt[0] += 1
                        if tile_id >= 0:
                            dma('sp', lambda e: e.dma_start(out=xb[:, :], in_=x1_s[tile_id, sub]), [x1_sb[tile_id]], [xb])
                        else:
                            dma('sp', lambda e: e.dma_start(out=xb[0:NS, :], in_=x1s_s[:, :]), [x1s_b], [xb])
                        gz = [gp3(), gp3()]
                        for half in range(2):
                            mm([(gz[half][0:npart, :], [(aT[:, c, tok], Wdn[:, c, half * 512:(half + 1) * 512]) for c in range(32)])], [aT, Wdn], [gz[half]])
                        op('dve', lambda e: e.memset(ssq[0:npart, 0:2], 0.0), [], [ssq])
                        for half in range(2):
                            op('act', lambda e: e.activation(out=junk3[0:npart, 0:512], in_=gz[half][0:npart, :], func=AF.Square,
                                                             accum_out=ssq[0:npart, half:half + 1]), [gz[half], ssq], [junk3, ssq])
                        op('dve', lambda e: e.tensor_tensor(out=ssq[0:npart, 0:1], in0=ssq[0:npart, 0:1], in1=ssq[0:npart, 1:2], op=ALU.add), [ssq], [ssq])
                        op('act', lambda e: e.activation(out=std[0:npart, 0:1], in_=ssq[0:npart, 0:1], func=AF.Sqrt, bias=EPS, scale=1.0 / 1024.0), [ssq], [std])
                        op('dve', lambda e: e.reciprocal(out=rstd[0:npart, 0:1], in_=std[0:npart, 0:1]), [std], [rstd])
                        for half in range(2):
                            hs = slice(half * 512, (half + 1) * 512)
                            op('dve', lambda e: e.scalar_tensor_tensor(out=zb[0:npart, hs], in0=gz[half][0:npart, :], scalar=rstd[0:npart, 0:1],
                                                                       in1=gpo_b[0:npart, hs], op0=ALU.mult, op1=ALU.mult), [gz[half], rstd, gpo_b], [zb])
                        op('pool', lambda e: e.tensor_tensor(out=zb[0:npart, :], in0=zb[0:npart, :], in1=xb[0:npart, :], op=ALU.add), [zb, xb], [zb])
                        if tile_id >= 0:
                            q, it = tile_id // NT, tile_id % NT
                            r0 = it * T + sub * 128
                            dma('sp', lambda e: e.dma_start(out=yp[q, r0:r0 + 128, :], in_=zb[:, :]), [zb], [])
                        else:
                            dma('sp', lambda e: e.dma_start(out=ysm[:, :], in_=zb[0:NS, :]), [zb], [])
            try:
                phase_a1()
                chk('A1')
                phase_a2()
                chk('A2')
                phase_b()
            except _Stop:
                pass
            P.barrier(engines=('sp',))
    except AssertionError:
        if not cfg.get('stop'):
            raise
    return nc, P


FULL_CFG = dict(S=2048, NSEQ=2, NS=4, NPG=128, NPOOL=5120)
_CACHE = {}


def make_in_maps(inp, cfg, n_cores):
    NSEQ, NS = cfg['NSEQ'], cfg['NS']
    f = lambda a: np.ascontiguousarray(a, dtype=np.float32)
    ck = f(inp['cache_k'][0]).reshape(-1, 512)
    cv = f(inp['cache_v'][0]).reshape(-1, 512)
    cl = f(inp['cache_logf'][0]).reshape(-1, 8)
    shared = dict(
        ck=ck, cv=cv, cl=cl,
        w_in=f(inp['w_in'][0]), gpre=f(inp['g_pre_mix'][0].reshape(8, 128).T), bfv=f(inp['b_f'][0].reshape(1, 8)),
        wau=f(inp['w_alpha_up'][0]), bal=f(inp['b_alpha'][0].reshape(1, 256)), ggn=f(inp['g_gla_norm'][0].reshape(1, 128)),
        wfo=f(inp['w_fox_out'][0]), wgo=f(inp['w_gla_out'][0]), wo=f(inp['w_o'][0]),
        gpm=f(inp['g_post_mix'][0].reshape(1, 1024)), gpl=f(inp['g_pre_mlp'][0].reshape(8, 128).T),
        wup=f(inp['w_up'][0]), wdn=f(inp['w_down'][0]), gpo=f(inp['g_post_mlp'][0].reshape(1, 1024)),
        cst=make_consts(),
    )
    maps = []
    for c in range(n_cores):
        m = dict(shared)
        m['xp'] = f(inp['x_prompt'][c * NSEQ:(c + 1) * NSEQ])
        m['xsm'] = f(inp['x_sample'][c * NS:(c + 1) * NS, 0, :])
        m['sg'] = f(inp['state_gla'][0, c * NS:(c + 1) * NS])
        m['pt'] = np.ascontiguousarray(inp['page_table'][c * NS:(c + 1) * NS].reshape(1, -1), dtype=np.int32)
        maps.append(m)
    return maps


def assemble(results, cfg, n_cores):
    S, NSEQ, NS = cfg['S'], cfg['NSEQ'], cfg['NS']
    cat = lambda k: np.concatenate([np.asarray(r[k]) for r in results], axis=0)
    B = NSEQ * n_cores
    DB = NS * n_cores
    return (
        cat('yp').reshape(B, S, 1024).astype(np.float32),
        cat('ysm').reshape(DB, 1, 1024).astype(np.float32),
        cat('kp').reshape(1, B, S, 8, 64).astype(np.float32),
        cat('vp').reshape(1, B, S, 8, 64).astype(np.float32),
        cat('lfp').reshape(1, B, S, 8).astype(np.float32),
        cat('gsp').reshape(1, B, 4, 64, 128).astype(np.float32),
        cat('ksm').reshape(1, DB, 1, 8, 64).astype(np.float32),
        cat('vsm').reshape(1, DB, 1, 8, 64).astype(np.float32),
        cat('lfs').reshape(1, DB, 1, 8).astype(np.float32),
        cat('gss').reshape(1, DB, 4, 64, 128).astype(np.float32),
    )


def run(inp, cfg, n_cores):
    key = tuple(sorted(cfg.items()))
    if key not in _CACHE:
        _CACHE[key] = build(cfg)[0]
    nc = _CACHE[key]
    maps = make_in_maps(inp, cfg, n_cores)
    res = run_bass_kernel_spmd(nc, maps, core_ids=list(range(n_cores)))
    return assemble(res.results, cfg, n_cores)


def kernel(**inputs):
    inp = {k: np.asarray(v) for k, v in inputs.items()}
    return run(inp, FULL_CFG, N_CORES)
```

```python
import contextlib
import numpy as np
import concourse.bass as bass
import concourse.mybir as mybir
from concourse.bass_utils import run_bass_kernel_spmd

F32 = mybir.dt.float32
BF16 = mybir.dt.bfloat16
I32 = mybir.dt.int32
AF = mybir.ActivationFunctionType
ALU = mybir.AluOpType
AX = mybir.AxisListType
EPS = 1e-6
N_CORES = 8


class _Stop(Exception):
    pass


class Buf:
    def __init__(self, name, t):
        self.name = name
        self.t = t
        self.w = None
        self.r = []
        self.psum = False

    def __getitem__(self, idx):
        return self.t[idx]


class Prog:
    NDMA = 40

    def __init__(self, nc, es):
        self.nc = nc
        self.eng = {'pe': nc.tensor, 'act': nc.scalar, 'dve': nc.vector, 'pool': nc.gpsimd, 'sp': nc.sync}
        self.sems = {}
        self.cnt = {}
        for e in ['pe', 'act', 'dve', 'pool']:
            self.sems[e] = es.enter_context(nc.semaphore('s_' + e))
            self.cnt[e] = 0
        self.dsems = [es.enter_context(nc.semaphore('s_dma%d' % i)) for i in range(self.NDMA)]
        self.dcnt = [0] * self.NDMA
        self.dnext = 0
        self.waited = {}
        self.ninst = 0

    def sb(self, es, name, shape, dt):
        return Buf(name, es.enter_context(self.nc.sbuf_tensor(name, list(shape), dt)))

    def ps(self, es, name, shape, dt=F32):
        b = Buf(name, es.enter_context(self.nc.psum_tensor(name, list(shape), dt)))
        b.psum = True
        return b

    def _sem(self, key):
        return self.sems[key] if isinstance(key, str) else self.dsems[key]

    def _wait(self, e, ev):
        key, val, _ = ev
        k = (e, key)
        if self.waited.get(k, 0) >= val:
            return
        self.waited[k] = val
        self.eng[e].wait_ge(self._sem(key), val)
        self.ninst += 1

    def _deps(self, e, reads, writes):
        best = {}

        def add(ev):
            if ev[0] not in best or best[ev[0]][1] < ev[1]:
                best[ev[0]] = ev
        for b in reads:
            if b.w is not None:
                add(b.w)
            if b.psum:
                for r in b.r:
                    if r[2] != e:
                        add(r)
        for b in writes:
            if b.w is not None and b.w[2] != e:
                add(b.w)
            for r in b.r:
                if r[2] != e:
                    add(r)
        for ev in best.values():
            self._wait(e, ev)

    def _record(self, ev, reads, writes):
        for b in reads:
            b.r = [r for r in b.r if r[0] != ev[0]] + [ev]
        for b in writes:
            b.w = ev
            b.r = []

    def op(self, e, fn, reads=(), writes=()):
        self._deps(e, reads, writes)
        inst = fn(self.eng[e])
        self.cnt[e] += 1
        inst.then_inc(self.sems[e], 1)
        self.ninst += 1
        self._record((e, self.cnt[e], e), reads, writes)

    def mm(self, groups, reads, writes):
        e = 'pe'
        self._deps(e, reads, writes)
        inst = None
        for out_ap, pairs in groups:
            n = len(pairs)
            for i, (l, r) in enumerate(pairs):
                inst = self.nc.tensor.matmul(out_ap, l, r, start=(i == 0), stop=(i == n - 1))
                self.ninst += 1
        self.cnt[e] += 1
        inst.then_inc(self.sems[e], 1)
        self._record((e, self.cnt[e], e), reads, writes)

    def mm_raw(self, out_ap, l, r, start, stop, reads, writes, skip=False):
        e = 'pe'
        self._deps(e, reads, writes)
        inst = self.nc.tensor.matmul(out_ap, l, r, start=start, stop=stop, skip_group_check=skip)
        self.cnt[e] += 1
        inst.then_inc(self.sems[e], 1)
        self.ninst += 1
        self._record((e, self.cnt[e], e), reads, writes)

    def tr(self, items, reads, writes):
        e = 'pe'
        self._deps(e, reads, writes)
        inst = None
        for out_ap, in_ap, ident_ap in items:
            inst = self.nc.tensor.transpose(out_ap, in_ap, ident_ap)
            self.ninst += 1
        self.cnt[e] += 1
        inst.then_inc(self.sems[e], 1)
        self._record((e, self.cnt[e], e), reads, writes)

    def dma(self, q, fn, reads=(), writes=()):
        k = self.dnext
        self.dnext = (self.dnext + 1) % self.NDMA
        if self.dcnt[k] > 0:
            self._wait(q, (k, self.dcnt[k], 'dma'))
        self._deps(q, reads, writes)
        inst = fn(self.eng[q])
        self.dcnt[k] += 16
        inst.then_inc(self.dsems[k], 16)
        self.ninst += 1
        self._record((k, self.dcnt[k], 'dma'), reads, writes)

    def barrier(self, engines=('pe', 'act', 'dve', 'pool', 'sp')):
        for e in engines:
            for o in ['pe', 'act', 'dve', 'pool']:
                if o != e and self.cnt[o] > 0:
                    self._wait(e, (o, self.cnt[o], o))
            for k in range(self.NDMA):
                if self.dcnt[k] > 0:
                    self._wait(e, (k, self.dcnt[k], 'dma'))


def make_consts():
    c = np.zeros((128, 1280), np.float32)
    i = np.arange(128)
    c[:, 0:128] = np.eye(128)
    tri_u = (i[:, None] <= i[None, :]).astype(np.float32)
    tri_su = (i[:, None] > i[None, :]).astype(np.float32)
    c[:, 128:256] = tri_u
    c[:, 256:384] = 1.0
    c[:, 384:512] = tri_su
    c[:, 512:640] = tri_u * (-1.0 / 16.0)
    c[:, 640:768] = tri_su * (-1.0 / 16.0)
    for h in range(8):
        c[h, 768 + h * 64:768 + (h + 1) * 64] = 1.0
    return c


def build(cfg):
    S, NSEQ, NS, NPG, NPOOL = cfg['S'], cfg['NSEQ'], cfg['NS'], cfg['NPG'], cfg['NPOOL']
    T = 512
    NT = S // T
    NKB = S // 128
    NTILE = NSEQ * NT
    nc = bass.Bass("TRN2", target_bir_lowering=False)

    def din(name, shape, dt=F32):
        return nc.dram_tensor(name, list(shape), dt, kind="ExternalInput").ap()

    def dout(name, shape, dt=F32):
        return nc.dram_tensor(name, list(shape), dt, kind="ExternalOutput").ap()

    def dscr(name, shape, dt=F32):
        return nc.dram_tensor(name, list(shape), dt, kind="Internal").ap()

    xp = din("xp", [NSEQ, S, 1024]); xsm = din("xsm", [NS, 1024])
    ck = din("ck", [NPOOL * 128, 512]); cv = din("cv", [NPOOL * 128, 512]); cl = din("cl", [NPOOL * 128, 8])
    sg = din("sg", [NS, 4, 64, 128]); pt = din("pt", [1, NS * NPG], I32)
    w_in = din("w_in", [1024, 5144]); gpre = din("gpre", [128, 8]); bfv = din("bfv", [1, 8])
    wau = din("wau", [16, 256]); bal = din("bal", [1, 256]); ggn = din("ggn", [1, 128])
    wfo = din("wfo", [512, 1024]); wgo = din("wgo", [512, 1024]); wo = din("wo", [1024, 1024])
    gpm = din("gpm", [1, 1024]); gpl = din("gpl", [128, 8])
    wup = din("wup", [1024, 4096]); wdn = din("wdn", [4096, 1024]); gpo = din("gpo", [1, 1024])
    cst = din("cst", [128, 1280])
    yp = dout("yp", [NSEQ, S, 1024]); ysm = dout("ysm", [NS, 1024])
    kp = dout("kp", [NSEQ, S, 512]); vp = dout("vp", [NSEQ, S, 512]); lfp = dout("lfp", [NSEQ, S, 8])
    gsp = dout("gsp", [NSEQ, 4, 64, 128])
    ksm = dout("ksm", [NS, 512]); vsm = dout("vsm", [NS, 512]); lfs = dout("lfs", [NS, 8])
    gss = dout("gss", [NS, 4, 64, 128])
    hT_s = dscr("hT_s", [NTILE, 128, 8, 512], BF16)
    oT_s = dscr("oT_s", [NTILE, 128, 8, 512], BF16)
    x1_s = dscr("x1_s", [NTILE, 4, 128, 1024])
    x1s_s = dscr("x1s_s", [NS, 1024])

    w_in_v = w_in.rearrange("(kc p) n -> p kc n", p=128)

    try:
        with contextlib.ExitStack() as es0:
            P = Prog(nc, es0)
            es0.enter_context(nc.Block())
            op, mm, dma = P.op, P.mm, P.dma
            hTs = P.sb(es0, "hTs", [128, 8, NS], BF16)
            oTs = P.sb(es0, "oTs", [128, 8, NS], BF16)
            hT_sb = [Buf("hT_s%d" % i, hT_s[i]) for i in range(NTILE)]
            oT_sb = [Buf("oT_s%d" % i, oT_s[i]) for i in range(NTILE)]
            x1_sb = [Buf("x1_s%d" % i, x1_s[i]) for i in range(NTILE)]
            x1s_b = Buf("x1s_s", x1s_s)

            def load_w(es, name, src_v, kc, ncols, c0=0, chunk=1024):
                wb = P.sb(es, name, [128, kc, ncols], BF16)
                for k0 in range(0, kc, 8):
                    for a in range(0, ncols, chunk):
                        b = min(ncols, a + chunk)
                        dma('pool', lambda e: e.dma_start(out=wb[:, k0:min(kc, k0 + 8), a:b], in_=src_v[:, k0:min(kc, k0 + 8), c0 + a:c0 + b]), [], [wb])
                return wb

            def norm_T(es_unused, xt, npart, g_sb, identb, xn, stat, trb, hT, col0):
                ssq, std, rstd = stat
                op('dve', lambda e: e.memset(ssq[0:npart, 0:1], 0.0), [], [ssq])
                op('act', lambda e: e.activation(out=xn[0:npart, :], in_=xt[0:npart, :], func=AF.Square,
                                                 accum_out=ssq[0:npart, 0:1]), [xt, ssq], [xn, ssq])
                op('act', lambda e: e.activation(out=std[0:npart, 0:1], in_=ssq[0:npart, 0:1], func=AF.Sqrt,
                                                 bias=EPS, scale=1.0 / 1024.0), [ssq], [std])
                op('dve', lambda e: e.reciprocal(out=rstd[0:npart, 0:1], in_=std[0:npart, 0:1]), [std], [rstd])
                op('dve', lambda e: e.tensor_scalar(out=xn[0:npart, :], in0=xt[0:npart, :], scalar1=rstd[0:npart, 0:1],
                                                    scalar2=None, op0=ALU.mult), [xt, rstd], [xn])
                P.tr([(trb[:, kc * 128:kc * 128 + npart], xn[0:npart, kc * 128:(kc + 1) * 128], identb[0:npart, 0:npart])
                      for kc in range(8)], [xn, identb], [trb])
                op('dve', lambda e: e.tensor_tensor(
                    out=hT[:, :, col0:col0 + npart],
                    in0=trb[:, :].rearrange("p (k t) -> p k t", k=8)[:, :, 0:npart],
                    in1=g_sb[:, 0:8].unsqueeze(2).to_broadcast([128, 8, npart]), op=ALU.mult), [trb, g_sb], [hT])

            def chk(stage):
                if cfg.get('stop') == stage:
                    raise _Stop()

            def phase_a1():
              with contextlib.ExitStack() as es:
                cf = P.sb(es, "cf", [128, 1280], F32)
                cb = P.sb(es, "cb", [128, 256], BF16)
                dma('sp', lambda e: e.dma_start(out=cf[:], in_=cst[:, :]), [], [cf])
                dma('pool', lambda e: e.dma_start(out=cb[:], in_=cst[:, 0:256]), [], [cb])
                ID = lambda n: cb[0:n, 0:n]
                MASKB = cb[:, 128:256]
                TRIU, ONES, TRISU, TRIU16, TRISU16 = cf[:, 128:256], cf[:, 256:384], cf[:, 384:512], cf[:, 512:640], cf[:, 640:768]
                BLKM = cf[0:8, 768:1280]
                gpre_sb = P.sb(es, "gpre_sb", [128, 8], F32)
                bf_b = P.sb(es, "bf_b", [128, 8], F32)
                wau_aug = P.sb(es, "wau_aug", [17, 256], F32)
                ggn_b = P.sb(es, "ggn_b", [128, 128], F32)
                dma('sp', lambda e: e.dma_start(out=gpre_sb[:], in_=gpre[:, :]), [], [gpre_sb])
                dma('sp', lambda e: e.dma_start(out=bf_b[:], in_=bfv[0:1, :].partition_broadcast(128)), [], [bf_b])
                dma('sp', lambda e: e.dma_start(out=wau_aug[0:16, :], in_=wau[:, :]), [], [wau_aug])
                dma('sp', lambda e: e.dma_start(out=wau_aug[16:17, :], in_=bal[0:1, :]), [], [wau_aug])
                dma('sp', lambda e: e.dma_start(out=ggn_b[:], in_=ggn[0:1, :].partition_broadcast(128)), [], [ggn_b])
                W1 = load_w(es, "W1", w_in_v, 8, 3096, 0, 516)
                CQ, CK, CV, CF, CQG, CKG, CVG, COG, CLR = 0, 512, 1024, 1536, 1544, 1800, 2056, 2568, 3080

                G = [P.ps(es, "G%d" % i, [128, 512], F32) for i in range(2)]
                STB = [P.ps(es, "STB%d" % i, [128, 512], F32) for i in range(2)]
                OACC = [P.ps(es, "OACC%d" % i, [128, 4, 128], F32) for i in range(2)]
                SACC = P.ps(es, "SACC", [128, 512], F32)
                TRB = [P.ps(es, "TRB%d" % i, [128, 1024], BF16) for i in range(1)]
                gi = [0]; ti = [0]; sti = [0]

                fox_mode = [False]

                def gp():
                    ring = G if fox_mode[0] else G + STB
                    gi[0] = (gi[0] + 1) % len(ring)
                    return ring[gi[0]]

                def tp():
                    ti[0] = (ti[0] + 1) % len(TRB)
                    return TRB[ti[0]]

                class Ring:
                    def __init__(self, name, n, shape, dt):
                        self.b = [P.sb(es, "%s%d" % (name, i), shape, dt) for i in range(n)]
                        self.i = 0

                    def next(self):
                        self.i = (self.i + 1) % len(self.b)
                        return self.b[self.i]

                xt_r = Ring("xt", 2, [128, 1024], F32)
                junk = P.sb(es, "junk", [128, 128], BF16)
                xn_r = Ring("xn", 2, [128, 1024], BF16)
                ssq = P.sb(es, "ssq", [128, 8], F32); std = P.sb(es, "std", [128, 8], F32); rstd = P.sb(es, "rstd", [128, 8], F32)
                stat = (ssq, std, rstd)
                hT = P.sb(es, "hT", [128, 8, T], BF16)
                QT = P.sb(es, "QT", [128, 8, T], BF16)
                KT = P.sb(es, "KT", [128, 4, S], BF16)
                Vaug = P.sb(es, "Vaug", [128, NKB, 8, 65], BF16)
                Cb = P.sb(es, "Cb", [128, NKB + 1, 8], F32)
                Ffull = P.sb(es, "Ffull", [128, NKB, 8], F32)
                biasT = P.sb(es, "biasT", [128, 4, NKB, 8], F32)
                kv_r = Ring("kvo", 2, [128, 512], F32)
                kb16_r = Ring("kb16", 2, [128, 512], BF16)
                lf_e = P.sb(es, "lf_e", [128, 8], F32)
                lf_r = Ring("lf", 2, [128, 8], F32)
                PT_r = Ring("PT", 8, [128, 128], BF16)
                rden_r = Ring("rden", 2, [128, 4], F32)
                ofx = P.sb(es, "ofx", [128, 4, 512], BF16)
                oT = P.sb(es, "oT", [128, 8, T], BF16)
                qgT = P.sb(es, "qgT", [128, 2, T], F32); kgT = P.sb(es, "kgT", [128, 2, T], F32)
                lrT = P.sb(es, "lrT", [17, T], F32)
                la_r = Ring("la", 2, [128, 256], F32)
                le_t = P.sb(es, "le_t", [128, 256], F32)
                kg_r = Ring("kg", 2, [128, 256], F32)
                vg_r = Ring("vg", 2, [128, 512], BF16)
                sog_r = Ring("sog", 2, [128, 512], BF16)
                EbT_r = Ring("EbT", 2, [128, 2, 128], F32); EnbT_r = Ring("EnbT", 2, [128, 2, 128], F32)
                qtl_r = Ring("qtl", 2, [128, 2, 128], F32); ktl_r = Ring("ktl", 2, [128, 2, 128], F32)
                ebrev = P.sb(es, "ebrev", [128, 256], F32)
                khat_r = Ring("khat", 2, [128, 256], BF16)
                ATs_r = Ring("ATs", 2, [128, 4, 128], BF16)
                Sst = P.sb(es, "Sst", [128, 2, 128], F32)
                t1 = P.sb(es, "t1", [128, 4, 128], F32)
                ogl_r = Ring("ogl", 2, [128, 512], BF16)

                op('dve', lambda e: e.memset(Vaug[:, :, :, 64:65], 1.0), [], [Vaug])
                op('dve', lambda e: e.memset(QT[:, :, :], 0.0), [], [QT])
                op('dve', lambda e: e.memset(lrT[:, :], 1.0), [], [lrT])

                chk('a1_setup')
                idxA = P.sb(es, "idxA", [128, NS * NPG], I32)
                idxB = P.sb(es, "idxB", [128, NS * NPG], F32)
                rid = P.sb(es, "rid", [128, 1], I32); ridf = P.sb(es, "ridf", [128, 1], F32)
                dma('sp', lambda e: e.dma_start(out=idxA[:], in_=pt[0:1, :].partition_broadcast(128)), [], [idxA])
                op('pool', lambda e: e.iota(rid[:], pattern=[[0, 1]], base=0, channel_multiplier=1), [], [rid])
                op('dve', lambda e: e.tensor_copy(out=ridf[:], in_=rid[:]), [rid], [ridf])
                op('dve', lambda e: e.tensor_copy(out=idxB[:], in_=idxA[:]), [idxA], [idxB])
                op('dve', lambda e: e.tensor_scalar(out=idxB[:], in0=idxB[:], scalar1=128.0, scalar2=ridf[:, 0:1],
                                                    op0=ALU.mult, op1=ALU.add), [idxB, ridf], [idxB])
                op('dve', lambda e: e.tensor_copy(out=idxA[:], in_=idxB[:]), [idxB], [idxA])

                row_r = Ring("row", 4, [1, 512], F32)
                qrow1 = P.sb(es, "qrow1", [1, 512], F32)
                krow1 = P.sb(es, "krow1", [1, 512], F32)
                vrowb1 = P.sb(es, "vrowb1", [1, 512], BF16)
                lfrow = [P.sb(es, "lfrow%d" % b, [1, 8], F32) for b in range(NS)]
                lrTs = P.sb(es, "lrTs", [17, NS], F32)
                qgTs = P.sb(es, "qgTs", [128, 2, NS], F32)
                laTs = P.sb(es, "laTs", [128, 2], F32); elaTs = P.sb(es, "elaTs", [128, 2], F32)
                Ssm = P.sb(es, "Ssm", [128, 2, 128], F32)
                one11 = cf[0:1, 256:257]

                chk('a1_idx')
                xs_t = xt_r.next()
                dma('sp', lambda e: e.dma_start(out=xs_t[0:NS, :], in_=xsm[:, :]), [], [xs_t])
                norm_T(es, xs_t, NS, gpre_sb, cb, xn_r.next(), stat, tp(), hTs, 0)
                chk('s1')
                op('dve', lambda e: e.memset(lrTs[:, :], 1.0), [], [lrTs])
                g = gp()
                mm([(g[0:16, 0:NS], [(W1[:, kc, CLR:CLR + 16], hTs[:, kc, 0:NS]) for kc in range(8)])], [W1, hTs], [g])
                op('dve', lambda e: e.tensor_copy(out=lrTs[0:16, :], in_=g[0:16, 0:NS]), [g], [lrTs])
                g = gp()
                mm([(g[:, hp * NS:(hp + 1) * NS], [(W1[:, kc, CQG + hp * 128:CQG + (hp + 1) * 128], hTs[:, kc, 0:NS]) for kc in range(8)])
                    for hp in range(2)], [W1, hTs], [g])
                op('act', lambda e: e.activation(out=qgTs[:, :, :], in_=g[:, 0:2 * NS].rearrange("p (h b) -> p h b", h=2),
                                                 func=AF.Copy, scale=0.125), [g], [qgTs])

                chk('s2')
                def rowproj(b, c0, n):
                    g = gp()
                    mm([(g[0:1, 0:n], [(hTs[:, kc, b:b + 1], W1[:, kc, c0:c0 + n]) for kc in range(8)])], [W1, hTs], [g])
                    return g

                def softplus_neg(dst, src_ap, npart, n, tmp, bias_ap=None, src_bufs=()):
                    if bias_ap is not None:
                        op('dve', lambda e: e.tensor_tensor(out=tmp[0:npart, 0:n], in0=src_ap, in1=bias_ap, op=ALU.add),
                           list(src_bufs), [tmp])
                        op('act', lambda e: e.activation(out=tmp[0:npart, 0:n], in_=tmp[0:npart, 0:n], func=AF.Exp, scale=-1.0),
                           [tmp], [tmp])
                    else:
                        op('act', lambda e: e.activation(out=tmp[0:npart, 0:n], in_=src_ap, func=AF.Exp, scale=-1.0),
                           list(src_bufs), [tmp])
                    op('act', lambda e: e.activation(out=dst[0:npart, 0:n], in_=tmp[0:npart, 0:n], func=AF.Ln, bias=1.0, scale=1.0),
                       [tmp], [dst])

                for b in range(NS):
                    g = rowproj(b, CK, 512)
                    kr = row_r.next()
                    op('act', lambda e: e.copy(out=kr[:, :], in_=g[0:1, :]), [g], [kr])
                    dma('sp', lambda e: e.dma_start(out=ksm[b:b + 1, :], in_=kr[:, :]), [kr], [])
                    g = rowproj(b, CV, 512)
                    vr = row_r.next()
                    op('act', lambda e: e.copy(out=vr[:, :], in_=g[0:1, :]), [g], [vr])
                    dma('sp', lambda e: e.dma_start(out=vsm[b:b + 1, :], in_=vr[:, :]), [vr], [])
                    chk('s3')
                    g = rowproj(b, CF, 8)
                    tmpr = row_r.next()
                    softplus_neg(lfrow[b], g[0:1, 0:8], 1, 8, tmpr, bias_ap=bf_b[0:1, :], src_bufs=[g, bf_b])
                    op('dve', lambda e: e.tensor_scalar(out=lfrow[b][:, :], in0=lfrow[b][:, :], scalar1=-1.0, scalar2=None,
                                                        op0=ALU.mult), [lfrow[b]], [lfrow[b]])
                    dma('sp', lambda e: e.dma_start(out=lfs[b:b + 1, :], in_=lfrow[b][:, :]), [lfrow[b]], [])
                    chk('s4')
                    g = rowproj(b, CKG, 256)
                    kgr = row_r.next()
                    op('act', lambda e: e.copy(out=kgr[:, 0:256], in_=g[0:1, 0:256]), [g], [kgr])
                    g = rowproj(b, CVG, 512)
                    vgr = row_r.next()
                    op('act', lambda e: e.copy(out=vgr[:, :], in_=g[0:1, :]), [g], [vgr])
                    g = rowproj(b, COG, 512)
                    sogr = row_r.next()
                    op('act', lambda e: e.activation(out=sogr[:, :], in_=g[0:1, :], func=AF.Silu), [g], [sogr])
                    chk('s5')
                    g = gp()
                    mm([(g[:, hp:hp + 1], [(wau_aug[0:17, hp * 128:(hp + 1) * 128], lrTs[0:17, b:b + 1])]) for hp in range(2)],
                       [wau_aug, lrTs], [g])
                    softplus_neg(laTs, g[:, 0:2], 128, 2, le_t, src_bufs=[g])
                    op('act', lambda e: e.activation(out=elaTs[:, :], in_=laTs[:, :], func=AF.Exp, scale=-1.0 / 16.0), [laTs], [elaTs])
                    chk('s6')
                    dma('sp', lambda e: e.dma_start(out=Ssm[:, :, :], in_=sg[b].rearrange("(hp dh) d e -> (dh d) hp e", dh=2)), [], [Ssm])
                    g = gp()
                    mm([(g[:, hp * 256:(hp + 1) * 256], [(kgr[0:1, hp * 128:(hp + 1) * 128], vgr[0:1, hp * 256:(hp + 1) * 256])])
                        for hp in range(2)], [kgr, vgr], [g])
                    for hp in range(2):
                        for dh in range(2):
                            pd = slice(dh * 64, dh * 64 + 64)
                            op('dve', lambda e: e.scalar_tensor_tensor(
                                out=Ssm[pd, hp, :], in0=Ssm[pd, hp, :], scalar=elaTs[pd, hp:hp + 1],
                                in1=g[pd, hp * 256 + dh * 128:hp * 256 + dh * 128 + 128], op0=ALU.mult, op1=ALU.add),
                               [Ssm, elaTs, g], [Ssm])
                    chk('s7')
                    dma('sp', lambda e: e.dma_start(out=gss[b].rearrange("(hp dh) d e -> (dh d) hp e", dh=2), in_=Ssm[:, :, :]), [Ssm], [])
                    chk('s7b')
                    gab = [gp(), gp()]
                    mm([(gab[hd % 2][0:1, (hd // 2) * 128:(hd // 2 + 1) * 128],
                         [(qgTs[(hd % 2) * 64:(hd % 2) * 64 + 64, hd // 2, b:b + 1], Ssm[(hd % 2) * 64:(hd % 2) * 64 + 64, hd // 2, :])])
                        for hd in (0, 2, 1, 3)], [qgTs, Ssm], gab)
                    chk('s8')
                    orow = row_r.next()
                    op('dve', lambda e: e.memset(ssq[0:1, 0:4], 0.0), [], [ssq])
                    for hd in range(4):
                        gsrc = gab[hd % 2]
                        op('act', lambda e: e.activation(out=junk[0:1, 0:128], in_=gsrc[0:1, (hd // 2) * 128:(hd // 2 + 1) * 128], func=AF.Square,
                                                         accum_out=ssq[0:1, hd:hd + 1]), [gsrc, ssq], [junk, ssq])
                    op('act', lambda e: e.activation(out=std[0:1, 0:4], in_=ssq[0:1, 0:4], func=AF.Sqrt, bias=EPS, scale=1.0 / 128.0), [ssq], [std])
                    op('dve', lambda e: e.reciprocal(out=rstd[0:1, 0:4], in_=std[0:1, 0:4]), [std], [rstd])
                    for hd in range(4):
                        gsrc = gab[hd % 2]
                        op('dve', lambda e: e.tensor_scalar(out=orow[0:1, hd * 128:(hd + 1) * 128], in0=gsrc[0:1, (hd // 2) * 128:(hd // 2 + 1) * 128],
                                                            scalar1=rstd[0:1, hd:hd + 1], scalar2=None, op0=ALU.mult), [gsrc, rstd], [orow])
                    op('dve', lambda e: e.tensor_tensor(out=orow[:, :].rearrange("p (h e) -> p h e", h=4),
                                                        in0=orow[:, :].rearrange("p (h e) -> p h e", h=4),
                                                        in1=ggn_b[0:1, :].unsqueeze(1).to_broadcast([1, 4, 128]), op=ALU.mult), [orow, ggn_b], [orow])
                    op('dve', lambda e: e.tensor_tensor(out=orow[:, :], in0=orow[:, :], in1=sogr[:, :], op=ALU.mult), [orow, sogr], [orow])
                    chk('s9')
                    g = gp()
                    mm([(g[:, kc:kc + 1], [(orow[0:1, kc * 128:(kc + 1) * 128], one11)]) for kc in range(4)], [orow, cf], [g])
                    op('dve', lambda e: e.tensor_copy(out=oTs[:, 4:8, b], in_=g[:, 0:4]), [g], [oTs])

                chk('a1_sproj')
                Kpg_r = Ring("Kpg", 3, [128, 512], BF16); Vpg_r = Ring("Vpg", 5, [128, 512], BF16)
                lpg_r = Ring("lpg", 4, [128, 8], F32)
                prod_r = Ring("prod", 2, [128, 512], F32)
                qbc = P.sb(es, "qbc", [128, 512], BF16)
                sc_r = Ring("sc", 2, [128, 8], F32); lg_r = Ring("lg", 3, [128, 8], F32)
                pb_r = Ring("pb", 3, [128, 8], BF16)
                carry = P.sb(es, "carry", [128, 8], F32); pacc = P.sb(es, "pacc", [128, 8], F32)
                den8 = P.sb(es, "den8", [8, 1], F32)

                def sample_decode():
                    for b in range(NS):
                        g = rowproj(b, CQ, 512)
                        op('act', lambda e: e.activation(out=qrow1[:, :], in_=g[0:1, :], func=AF.Copy, scale=0.125), [g], [qrow1])
                        g = rowproj(b, CK, 512)
                        op('act', lambda e: e.copy(out=krow1[:, :], in_=g[0:1, :]), [g], [krow1])
                        g = rowproj(b, CV, 512)
                        op('dve', lambda e: e.tensor_copy(out=vrowb1[:, :], in_=g[0:1, :]), [g], [vrowb1])
                        g = gp()
                        mm([(g[:, :], [(cf[0:1, 256:384], qrow1[0:1, :])])], [cf, qrow1], [g])
                        op('dve', lambda e: e.tensor_copy(out=qbc[:, :], in_=g[:, :]), [g], [qbc])
                        g = gp()
                        mm([(g[:, 0:8], [(cf[0:1, 256:384], lfrow[b][0:1, :])])], [cf, lfrow[b]], [g])
                        op('dve', lambda e: e.tensor_copy(out=carry[:, :], in_=g[:, 0:8]), [g], [carry])
                        op('dve', lambda e: e.memset(pacc[:, :], 0.0), [], [pacc])
                        pr = prod_r.next()
                        op('dve', lambda e: e.tensor_tensor(out=pr[0:1, :], in0=krow1[0:1, :], in1=qrow1[0:1, :], op=ALU.mult),
                           [krow1, qrow1], [pr])
                        sc = sc_r.next()
                        op('dve', lambda e: e.tensor_reduce(out=sc[0:1, :], in_=pr[0:1, :].rearrange("p (h d) -> p h d", h=8),
                                                            axis=AX.X, op=ALU.add), [pr], [sc])
                        pb = pb_r.next()
                        op('act', lambda e: e.activation(out=pb[0:1, :], in_=sc[0:1, :], func=AF.Exp), [sc], [pb])
                        op('dve', lambda e: e.tensor_tensor(out=pacc[0:1, :], in0=pacc[0:1, :], in1=pb[0:1, :], op=ALU.add), [pacc, pb], [pacc])
                        P.mm_raw(SACC[0:8, :], pb[0:1, :], vrowb1[0:1, :], True, False, [pb, vrowb1], [SACC])
                        yield
                        order = list(range(NPG - 1, -1, -1))
                        stp = {}

                        def stA(j):
                            col = b * NPG + j
                            Kp, Vp, lp = Kpg_r.next(), Vpg_r.next(), lpg_r.next()
                            off = bass.IndirectOffsetOnAxis(ap=idxA[:, col:col + 1], axis=0)
                            dma('pool', lambda e: e.indirect_dma_start(out=Kp[:, :], out_offset=None, in_=ck[:, :], in_offset=off), [idxA], [Kp])
                            dma('pool', lambda e: e.indirect_dma_start(out=Vp[:, :], out_offset=None, in_=cv[:, :], in_offset=off), [idxA], [Vp])
                            dma('pool', lambda e: e.indirect_dma_start(out=lp[:, :], out_offset=None, in_=cl[:, :], in_offset=off), [idxA], [lp])
                            stp[j] = [Kp, Vp, lp, None]

                        def stB1(j):
                            Kp, Vp, lp, _ = stp[j]
                            g = gp()
                            mm([(g[:, 0:8], [(TRISU, lp[:, :])]), (g[:, 8:16], [(ONES, lp[:, :])])], [cf, lp], [g])
                            pr = prod_r.next()
                            op('dve', lambda e: e.tensor_tensor(out=pr[:, :], in0=Kp[:, :], in1=qbc[:, :], op=ALU.mult), [Kp, qbc], [pr])
                            sc = sc_r.next()
                            op('dve', lambda e: e.tensor_reduce(out=sc[:, :], in_=pr[:, :].rearrange("p (h d) -> p h d", h=8),
                                                                axis=AX.X, op=ALU.add), [pr], [sc])
                            lg = lg_r.next()
                            op('dve', lambda e: e.tensor_tensor(out=lg[:, :], in0=g[:, 0:8], in1=carry[:, :], op=ALU.add), [g, carry], [lg])
                            op('dve', lambda e: e.tensor_tensor(out=lg[:, :], in0=lg[:, :], in1=sc[:, :], op=ALU.add), [lg, sc], [lg])
                            op('dve', lambda e: e.tensor_tensor(out=carry[:, :], in0=g[:, 8:16], in1=carry[:, :], op=ALU.add), [g, carry], [carry])
                            stp[j][3] = lg

                        def stB2(j):
                            lg = stp[j][3]
                            pb = pb_r.next()
                            op('act', lambda e: e.activation(out=pb[:, :], in_=lg[:, :], func=AF.Exp), [lg], [pb])
                            op('dve', lambda e: e.tensor_tensor(out=pacc[:, :], in0=pacc[:, :], in1=pb[:, :], op=ALU.add), [pacc, pb], [pacc])
                            stp[j][3] = pb

                        def stC(j, last):
                            Kp, Vp, lp, pb = stp.pop(j)
                            P.mm_raw(SACC[0:8, :], pb[:, :], Vp[:, :], False, last, [pb, Vp], [SACC])

                        stA(order[0])
                        if NPG > 1:
                            stA(order[1])
                        for i in range(NPG + 2):
                            if i + 2 < NPG:
                                stA(order[i + 2])
                            if i < NPG:
                                stB1(order[i])
                            if 0 <= i - 1 < NPG:
                                stB2(order[i - 1])
                            if 0 <= i - 2 < NPG:
                                stC(order[i - 2], i - 2 == NPG - 1)
                            yield
                        g = gp()
                        mm([(g[0:8, 0:1], [(pacc[:, :], cf[:, 256:257])])], [pacc, cf], [g])
                        op('dve', lambda e: e.reciprocal(out=den8[:, :], in_=g[0:8, 0:1]), [g], [den8])
                        on8 = prod_r.next()
                        op('dve', lambda e: e.scalar_tensor_tensor(out=on8[0:8, :], in0=SACC[0:8, :], scalar=den8[:, 0:1], in1=BLKM,
                                                                   op0=ALU.mult, op1=ALU.mult), [SACC, den8, cf], [on8])
                        g = gp()
                        mm([(g[:, kc:kc + 1], [(on8[0:8, kc * 128:(kc + 1) * 128], cf[0:8, 256:257])]) for kc in range(4)], [on8, cf], [g])
                        op('dve', lambda e: e.tensor_copy(out=oTs[:, 0:4, b], in_=g[:, 0:4]), [g], [oTs])
                        yield

                sgen = sample_decode()
                sdone = [False]

                def pump(n=1):
                    for _ in range(n):
                        if sdone[0]:
                            return
                        try:
                            next(sgen)
                        except StopIteration:
                            sdone[0] = True

                total_pumps = NS * (NPG + 4)
                n_slots = NSEQ * sum(8 * (4 * it + 4) for it in range(NT))
                ppump = max(1, -(-total_pumps // max(1, n_slots)))

                if cfg.get('stop') == 'a1_sdec':
                    while not sdone[0]:
                        pump(1)
                    raise _Stop()
                for q in range(NSEQ):
                    op('dve', lambda e: e.memset(Cb[:, 0, :], 0.0), [], [Cb])
                    op('dve', lambda e: e.memset(Sst[:, :, :], 0.0), [], [Sst])
                    for it in range(NT):
                        tile_id = q * NT + it
                        t0 = it * T
                        for sub in range(4):
                            r0 = t0 + sub * 128
                            kb = it * 4 + sub
                            xt = xt_r.next()
                            dma('sp', lambda e: e.dma_start(out=xt[:, :], in_=xp[q, r0:r0 + 128, :]), [], [xt])
                            norm_T(es, xt, 128, gpre_sb, cb, xn_r.next(), stat, tp(), hT, sub * 128)
                            chk('p1')
                            tok = slice(sub * 128, sub * 128 + 128)

                            def tproj(c0, n):
                                g = gp()
                                mm([(g[:, 0:n], [(hT[:, kc, tok], W1[:, kc, c0:c0 + n]) for kc in range(8)])], [hT, W1], [g])
                                return g
                            g = tproj(CK, 512)
                            chk('p1a')
                            ko = kv_r.next()
                            op('act', lambda e: e.copy(out=ko[:, :], in_=g[:, :]), [g], [ko])
                            chk('p1b')
                            k16 = kb16_r.next()
                            op('dve', lambda e: e.tensor_copy(out=k16[:, :], in_=ko[:, :]), [ko], [k16])
                            chk('p1c')
                            dma('sp', lambda e: e.dma_start(out=kp[q, r0:r0 + 128, :], in_=ko[:, :]), [ko], [])
                            chk('p2')
                            tb = tp()
                            P.tr([(tb[:, hp * 128:(hp + 1) * 128], k16[:, hp * 128:(hp + 1) * 128], ID(128)) for hp in range(4)], [k16, cb], [tb])
                            op('act', lambda e: e.copy(out=KT[:, :, kb * 128:(kb + 1) * 128],
                                                       in_=tb[:, 0:512].rearrange("p (h t) -> p h t", h=4)), [tb], [KT])
                            chk('p3')
                            g = tproj(CV, 512)
                            vo = kv_r.next()
                            op('act', lambda e: e.copy(out=vo[:, :], in_=g[:, :]), [g], [vo])
                            op('dve', lambda e: e.tensor_copy(out=Vaug[:, kb, :, 0:64], in_=vo[:, :].rearrange("p (h d) -> p h d", h=8)), [vo], [Vaug])
                            dma('sp', lambda e: e.dma_start(out=vp[q, r0:r0 + 128, :], in_=vo[:, :]), [vo], [])
                            chk('p4')
                            g = tproj(CF, 8)
                            lf = lf_r.next()
                            softplus_neg(lf, g[:, 0:8], 128, 8, lf_e, bias_ap=bf_b[:, :], src_bufs=[g, bf_b])
                            op('dve', lambda e: e.tensor_scalar(out=lf[:, :], in0=lf[:, :], scalar1=-1.0, scalar2=None, op0=ALU.mult), [lf], [lf])
                            dma('sp', lambda e: e.dma_start(out=lfp[q, r0:r0 + 128, :], in_=lf[:, :]), [lf], [])
                            chk('p5')
                            g = gp()
                            mm([(g[:, 0:8], [(TRIU, lf[:, :])]), (g[:, 8:16], [(ONES, lf[:, :])])], [cf, lf], [g])
                            op('dve', lambda e: e.tensor_tensor(out=Ffull[:, kb, :], in0=g[:, 0:8], in1=Cb[:, kb, :], op=ALU.add), [g, Cb], [Ffull])
                            op('dve', lambda e: e.tensor_tensor(out=Cb[:, kb + 1, :], in0=g[:, 8:16], in1=Cb[:, kb, :], op=ALU.add), [g, Cb], [Cb])
                        chk('a1_t1')
                        for hp in range(4):
                            g = gp()
                            mm([(g[:, :], [(W1[:, kc, CQ + hp * 128:CQ + (hp + 1) * 128], hT[:, kc, :]) for kc in range(8)])], [W1, hT], [g])
                            op('act', lambda e: e.activation(out=QT[0:64, 2 * hp, :], in_=g[0:64, :], func=AF.Copy, scale=0.125), [g], [QT])
                            op('act', lambda e: e.activation(out=QT[64:128, 2 * hp + 1, :], in_=g[64:128, :], func=AF.Copy, scale=0.125), [g], [QT])
                        for jl in range(4):
                            jg = it * 4 + jl
                            op('dve', lambda e: e.tensor_tensor(out=biasT[:, jl, 0:jg + 1, :],
                                                                in0=Cb[:, jg:jg + 1, :].to_broadcast([128, jg + 1, 8]),
                                                                in1=Ffull[:, 0:jg + 1, :], op=ALU.subtract), [Cb, Ffull], [biasT])
                        nkb = it * 4 + 4
                        slots = [(h, kb) for h in range(8) for kb in range(nkb)]

                        def issue_st(h, kb):
                            hp, dh = h // 2, h % 2
                            pd = slice(dh * 64, dh * 64 + 64)
                            c0 = max(0, kb - it * 4) * 128
                            sti[0] += 1
                            g = STB[sti[0] % 2]
                            mm([(g[:, c0:512], [(KT[:, hp, kb * 128:(kb + 1) * 128], QT[:, h, c0:512])])], [KT, QT], [g])
                            return g
                        ahead = cfg.get('st_ahead', True)
                        fox_mode[0] = True
                        g_next = issue_st(*slots[0]) if ahead else None
                        for si, (h, kb) in enumerate(slots):
                            if ahead:
                                g = g_next
                                g_next = issue_st(*slots[si + 1]) if si + 1 < len(slots) else None
                            else:
                                g = issue_st(h, kb)
                            oacc = OACC[h % 2]
                            jl0 = max(0, kb - it * 4)
                            if kb == 0 and cfg.get('zero_acc'):
                                op('dve', lambda e: e.memset(oacc[:, :, :], 0.0), [], [oacc])
                            for jl in range(jl0, 4):
                                jg = it * 4 + jl
                                pt_ = PT_r.next()
                                op('act', lambda e: e.activation(out=pt_[:, :], in_=g[:, jl * 128:(jl + 1) * 128], func=AF.Exp,
                                                                 bias=biasT[:, jl, kb, h:h + 1], scale=1.0), [g, biasT], [pt_])
                                if kb == jg:
                                    op(cfg.get('mask_eng', 'dve'), lambda e: e.tensor_tensor(out=pt_[:, :], in0=pt_[:, :], in1=MASKB, op=ALU.mult), [pt_, cb], [pt_])
                                P.mm_raw(oacc[:, jl, 0:65], pt_[:, :], Vaug[:, kb, h, :], (kb == 0 and jl == 0) and not cfg.get('zero_acc'), (kb == jg), [pt_, Vaug], [oacc], skip=True)
                            pump(ppump)
                            if kb == nkb - 1:
                                rd = rden_r.next()
                                op('dve', lambda e: e.reciprocal(out=rd[:, :], in_=oacc[:, :, 64]), [oacc], [rd])
                                op('dve', lambda e: e.tensor_tensor(out=ofx[:, :, h * 64:(h + 1) * 64], in0=oacc[:, :, 0:64],
                                                                    in1=rd[:, 0:4].unsqueeze(2).to_broadcast([128, 4, 64]), op=ALU.mult), [oacc, rd], [ofx])
                        fox_mode[0] = False
                        chk('a1_t4')
                        for half in range(2):
                            tb = tp()
                            P.tr([(tb[:, (i * 4 + jl) * 128:(i * 4 + jl + 1) * 128], ofx[:, jl, (half * 2 + i) * 128:(half * 2 + i + 1) * 128], ID(128))
                                  for i in range(2) for jl in range(4)], [ofx, cb], [tb])
                            op('act', lambda e: e.copy(out=oT[:, half * 2:half * 2 + 2, :],
                                                       in_=tb[:, :].rearrange("p (c t) -> p c t", c=2)), [tb], [oT])
                        chk('a1_t4b')
                        for hp in range(2):
                            g = gp()
                            mm([(g[:, :], [(W1[:, kc, CQG + hp * 128:CQG + (hp + 1) * 128], hT[:, kc, :]) for kc in range(8)])], [W1, hT], [g])
                            op('act', lambda e: e.activation(out=qgT[:, hp, :], in_=g[:, :], func=AF.Copy, scale=0.125), [g], [qgT])
                            g = gp()
                            mm([(g[:, :], [(W1[:, kc, CKG + hp * 128:CKG + (hp + 1) * 128], hT[:, kc, :]) for kc in range(8)])], [W1, hT], [g])
                            op('act', lambda e: e.copy(out=kgT[:, hp, :], in_=g[:, :]), [g], [kgT])
                        g = gp()
                        mm([(g[0:16, :], [(W1[:, kc, CLR:CLR + 16], hT[:, kc, :]) for kc in range(8)])], [W1, hT], [g])
                        op('dve', lambda e: e.tensor_copy(out=lrT[0:16, :], in_=g[0:16, :]), [g], [lrT])
                        for sub in range(4):
                            tok = slice(sub * 128, sub * 128 + 128)
                            g = gp()
                            mm([(g[:, 0:256], [(hT[:, kc, tok], W1[:, kc, CKG:CKG + 256]) for kc in range(8)])], [hT, W1], [g])
                            kg = kg_r.next()
                            op('act', lambda e: e.copy(out=kg[:, :], in_=g[:, 0:256]), [g], [kg])
                            g = gp()
                            mm([(g[:, :], [(hT[:, kc, tok], W1[:, kc, CVG:CVG + 512]) for kc in range(8)])], [hT, W1], [g])
                            vg = vg_r.next()
                            op('dve', lambda e: e.tensor_copy(out=vg[:, :], in_=g[:, :]), [g], [vg])
                            g = gp()
                            mm([(g[:, :], [(hT[:, kc, tok], W1[:, kc, COG:COG + 512]) for kc in range(8)])], [hT, W1], [g])
                            sog = sog_r.next()
                            op('act', lambda e: e.activation(out=sog[:, :], in_=g[:, :], func=AF.Silu), [g], [sog])
                            g = gp()
                            mm([(g[:, 0:256], [(lrT[0:17, tok], wau_aug[0:17, :])])], [lrT, wau_aug], [g])
                            la = la_r.next()
                            softplus_neg(la, g[:, 0:256], 128, 256, le_t, src_bufs=[g])
                            g = gp()
                            mm([(g[:, hp * 128:(hp + 1) * 128], [(la[:, hp * 128:(hp + 1) * 128], TRIU16)]) for hp in range(2)]
                               + [(g[:, 256:512], [(TRISU16, la[:, :])])], [la, cf], [g])
                            EbT, EnbT = EbT_r.next(), EnbT_r.next()
                            gv = g[:, 0:256].rearrange("p (h t) -> p h t", h=2)
                            op('act', lambda e: e.activation(out=EbT[:, :, :], in_=gv, func=AF.Exp), [g], [EbT])
                            op('act', lambda e: e.activation(out=EnbT[:, :, :], in_=gv, func=AF.Exp, scale=-1.0), [g], [EnbT])
                            op('act', lambda e: e.activation(out=ebrev[:, :], in_=g[:, 256:512], func=AF.Exp), [g], [ebrev])
                            qtl, ktl, khat = qtl_r.next(), ktl_r.next(), khat_r.next()
                            op('dve', lambda e: e.tensor_tensor(out=qtl[:, :, :], in0=qgT[:, :, tok], in1=EbT[:, :, :], op=ALU.mult), [qgT, EbT], [qtl])
                            op('pool', lambda e: e.tensor_tensor(out=ktl[:, :, :], in0=kgT[:, :, tok], in1=EnbT[:, :, :], op=ALU.mult), [kgT, EnbT], [ktl])
                            op('pool', lambda e: e.tensor_tensor(out=khat[:, :], in0=kg[:, :], in1=ebrev[:, :], op=ALU.mult), [kg, ebrev], [khat])
                            gat = [gp(), gp()]
                            mm([(gat[hd % 2][:, (hd // 2) * 128:(hd // 2 + 1) * 128],
                                 [(ktl[(hd % 2) * 64:(hd % 2) * 64 + 64, hd // 2, :], qtl[(hd % 2) * 64:(hd % 2) * 64 + 64, hd // 2, :])])
                                for hd in (0, 2, 1, 3)], [ktl, qtl], gat)
                            ATs = ATs_r.next()
                            for dh in range(2):
                                op('dve', lambda e: e.tensor_tensor(out=ATs[:, dh * 2:dh * 2 + 2, :],
                                                                    in0=gat[dh][:, 0:256].rearrange("p (h t) -> p h t", h=2),
                                                                    in1=cf[:, 128:256].unsqueeze(1).to_broadcast([128, 2, 128]), op=ALU.mult),
                                   [gat[dh], cf], [ATs])
                            gov = [gp(), gp()]
                            mm([(gov[hd % 2][:, (hd // 2) * 128:(hd // 2 + 1) * 128],
                                 [(qtl[(hd % 2) * 64:(hd % 2) * 64 + 64, hd // 2, :], Sst[(hd % 2) * 64:(hd % 2) * 64 + 64, hd // 2, :]),
                                  (ATs[:, (hd % 2) * 2 + hd // 2, :], vg[:, hd * 128:(hd + 1) * 128])]) for hd in (0, 2, 1, 3)],
                               [qtl, Sst, ATs, vg], gov)
                            gs = gp()
                            mm([(gs[:, hp * 256:(hp + 1) * 256], [(khat[:, hp * 128:(hp + 1) * 128], vg[:, hp * 256:(hp + 1) * 256])])
                                for hp in range(2)], [khat, vg], [gs])
                            for hp in range(2):
                                for dh in range(2):
                                    pd = slice(dh * 64, dh * 64 + 64)
                                    op('dve', lambda e: e.scalar_tensor_tensor(
                                        out=Sst[pd, hp, :], in0=Sst[pd, hp, :], scalar=EbT[pd, hp, 127:128],
                                        in1=gs[pd, hp * 256 + dh * 128:hp * 256 + dh * 128 + 128], op0=ALU.mult, op1=ALU.add),
                                       [Sst, EbT, gs], [Sst])
                            op('dve', lambda e: e.memset(ssq[:, 0:4], 0.0), [], [ssq])
                            for hd in range(4):
                                gsrc = gov[hd % 2]
                                op('act', lambda e: e.activation(out=junk[:, 0:128], in_=gsrc[:, (hd // 2) * 128:(hd // 2 + 1) * 128], func=AF.Square,
                                                                 accum_out=ssq[:, hd:hd + 1]), [gsrc, ssq], [junk, ssq])
                            op('act', lambda e: e.activation(out=std[:, 0:4], in_=ssq[:, 0:4], func=AF.Sqrt, bias=EPS, scale=1.0 / 128.0), [ssq], [std])
                            op('dve', lambda e: e.reciprocal(out=rstd[:, 0:4], in_=std[:, 0:4]), [std], [rstd])
                            for dh in range(2):
                                op('dve', lambda e: e.tensor_tensor(
                                    out=t1[:, :, :].rearrange("p (hp dh) e -> p hp dh e", dh=2)[:, :, dh, :],
                                    in0=gov[dh][:, 0:256].rearrange("p (h e) -> p h e", h=2),
                                    in1=rstd[:, 0:4].rearrange("p (hp dh) -> p hp dh", dh=2)[:, :, dh].unsqueeze(2).to_broadcast([128, 2, 128]),
                                    op=ALU.mult), [gov[dh], rstd], [t1])
                            op('pool', lambda e: e.tensor_tensor(out=t1[:, :, :], in0=t1[:, :, :],
                                                                 in1=ggn_b[:, :].unsqueeze(1).to_broadcast([128, 4, 128]), op=ALU.mult), [t1, ggn_b], [t1])
                            ogl = ogl_r.next()
                            op('pool', lambda e: e.tensor_tensor(out=ogl[:, :], in0=t1[:, :, :].rearrange("p h e -> p (h e)"), in1=sog[:, :], op=ALU.mult),
                               [t1, sog], [ogl])
                            tb = tp()
                            P.tr([(tb[:, kc * 128:(kc + 1) * 128], ogl[:, kc * 128:(kc + 1) * 128], ID(128)) for kc in range(4)], [ogl, cb], [tb])
                            op('act', lambda e: e.copy(out=oT[:, 4:8, tok], in_=tb[:, 0:512].rearrange("p (c t) -> p c t", c=4)), [tb], [oT])
                        chk('a1_t5')
                        dma('sp', lambda e: e.dma_start(out=hT_s[tile_id], in_=hT[:, :, :]), [hT], [hT_sb[tile_id]])
                        dma('sp', lambda e: e.dma_start(out=oT_s[tile_id], in_=oT[:, :, :]), [oT], [oT_sb[tile_id]])
                    dma('sp', lambda e: e.dma_start(out=gsp[q].rearrange("(hp dh) d e -> (dh d) hp e", dh=2), in_=Sst[:, :, :]), [Sst], [])
                while not sdone[0]:
                    pump(1)
                P.barrier()

            def phase_a2():
              with contextlib.ExitStack() as es:
                Wg = load_w(es, "Wg", w_in_v, 8, 2048, 3096, 512)
                Wfo = load_w(es, "Wfo", wfo.rearrange("(kc p) n -> p kc n", p=128), 4, 1024, 0, 1024)
                Wgo = load_w(es, "Wgo", wgo.rearrange("(kc p) n -> p kc n", p=128), 4, 1024, 0, 1024)
                Wo = load_w(es, "Wo", wo.rearrange("(kc p) n -> p kc n", p=128), 8, 1024, 0, 512)
                gpm_b = P.sb(es, "gpm_b", [128, 1024], F32)
                dma('sp', lambda e: e.dma_start(out=gpm_b[:], in_=gpm[0:1, :].partition_broadcast(128)), [], [gpm_b])
                G = [P.ps(es, "H%d" % i, [128, 512], F32) for i in range(8)]
                gi = [0]

                def gp2():
                    gi[0] = (gi[0] + 1) % len(G)
                    return G[gi[0]]
                hT2 = [P.sb(es, "hT2_%d" % i, [128, 8, T], BF16) for i in range(2)]
                oT2 = [P.sb(es, "oT2_%d" % i, [128, 8, T], BF16) for i in range(2)]
                s1 = [P.sb(es, "s1_%d" % i, [128, T], F32) for i in range(2)]
                s2 = [P.sb(es, "s2_%d" % i, [128, T], F32) for i in range(2)]
                uT = P.sb(es, "uT", [128, 8, T], BF16)
                xr = [P.sb(es, "xr%d" % i, [128, 1024], F32) for i in range(2)]
                zn = [P.sb(es, "zn%d" % i, [128, 1024], F32) for i in range(2)]
                junk2 = P.sb(es, "junk2", [128, 512], BF16)
                ssq = P.sb(es, "ssq2", [128, 2], F32); std = P.sb(es, "std2", [128, 2], F32); rstd = P.sb(es, "rstd2", [128, 2], F32)
                cnt = [0]
                tiles = [(i, T, 128) for i in range(NTILE)] + [(-1, NS, NS)]
                for (tile_id, Tn, npart) in tiles:
                    nsub = max(1, Tn // 128)
                    if tile_id >= 0:
                        hb, ob = hT2[tile_id % 2], oT2[tile_id % 2]
                        dma('sp', lambda e: e.dma_start(out=hb[:, :, :], in_=hT_s[tile_id]), [hT_sb[tile_id]], [hb])
                        dma('sp', lambda e: e.dma_start(out=ob[:, :, :], in_=oT_s[tile_id]), [oT_sb[tile_id]], [ob])
                    else:
                        hb, ob = hTs, oTs
                    for c in range(8):
                        cs = slice(c * 128, (c + 1) * 128)
                        ga, gb, g1, g2 = gp2(), gp2(), gp2(), gp2()
                        mm([(g1[:, 0:Tn], [(Wg[:, kc, c * 128:(c + 1) * 128], hb[:, kc, 0:Tn]) for kc in range(8)])], [Wg, hb], [g1])
                        mm([(g2[:, 0:Tn], [(Wg[:, kc, 1024 + c * 128:1024 + (c + 1) * 128], hb[:, kc, 0:Tn]) for kc in range(8)])], [Wg, hb], [g2])
                        mm([(ga[:, 0:Tn], [(Wfo[:, kc, cs], ob[:, kc, 0:Tn]) for kc in range(4)])], [Wfo, ob], [ga])
                        mm([(gb[:, 0:Tn], [(Wgo[:, kc, cs], ob[:, 4 + kc, 0:Tn]) for kc in range(4)])], [Wgo, ob], [gb])
                        a1, a2 = s1[c % 2], s2[c % 2]
                        op('act', lambda e: e.activation(out=a1[:, 0:Tn], in_=g1[:, 0:Tn], func=AF.Sigmoid), [g1], [a1])
                        op('act', lambda e: e.activation(out=a2[:, 0:Tn], in_=g2[:, 0:Tn], func=AF.Sigmoid), [g2], [a2])
                        op('dve', lambda e: e.tensor_tensor(out=a1[:, 0:Tn], in0=a1[:, 0:Tn], in1=ga[:, 0:Tn], op=ALU.mult), [a1, ga], [a1])
                        op('dve', lambda e: e.tensor_tensor(out=a2[:, 0:Tn], in0=a2[:, 0:Tn], in1=gb[:, 0:Tn], op=ALU.mult), [a2, gb], [a2])
                        op('pool', lambda e: e.tensor_tensor(out=uT[:, c, 0:Tn], in0=a1[:, 0:Tn], in1=a2[:, 0:Tn], op=ALU.add), [a1, a2], [uT])
                    for sub in range(nsub):
                        tok = slice(sub * npart, (sub + 1) * npart)
                        xb, zb = xr[cnt[0] % 2], zn[cnt[0] % 2]
                        cnt[0] += 1
                        if tile_id >= 0:
                            q, it = tile_id // NT, tile_id % NT
                            r0 = it * T + sub * 128
                            dma('sp', lambda e: e.dma_start(out=xb[:, :], in_=xp[q, r0:r0 + 128, :]), [], [xb])
                        else:
                            dma('sp', lambda e: e.dma_start(out=xb[0:NS, :], in_=xsm[:, :]), [], [xb])
                        gz = [gp2(), gp2()]
                        for half in range(2):
                            mm([(gz[half][0:npart, :], [(uT[:, kc, tok], Wo[:, kc, half * 512:(half + 1) * 512]) for kc in range(8)])], [uT, Wo], [gz[half]])
                        op('dve', lambda e: e.memset(ssq[0:npart, 0:2], 0.0), [], [ssq])
                        for half in range(2):
                            op('act', lambda e: e.activation(out=junk2[0:npart, :], in_=gz[half][0:npart, :], func=AF.Square,
                                                             accum_out=ssq[0:npart, half:half + 1]), [gz[half], ssq], [junk2, ssq])
                        op('dve', lambda e: e.tensor_tensor(out=ssq[0:npart, 0:1], in0=ssq[0:npart, 0:1], in1=ssq[0:npart, 1:2], op=ALU.add), [ssq], [ssq])
                        op('act', lambda e: e.activation(out=std[0:npart, 0:1], in_=ssq[0:npart, 0:1], func=AF.Sqrt, bias=EPS, scale=1.0 / 1024.0), [ssq], [std])
                        op('dve', lambda e: e.reciprocal(out=rstd[0:npart, 0:1], in_=std[0:npart, 0:1]), [std], [rstd])
                        for half in range(2):
                            hs = slice(half * 512, (half + 1) * 512)
                            op('dve', lambda e: e.scalar_tensor_tensor(out=zb[0:npart, hs], in0=gz[half][0:npart, :], scalar=rstd[0:npart, 0:1],
                                                                       in1=gpm_b[0:npart, hs], op0=ALU.mult, op1=ALU.mult), [gz[half], rstd, gpm_b], [zb])
                        op('pool', lambda e: e.tensor_tensor(out=zb[0:npart, :], in0=zb[0:npart, :], in1=xb[0:npart, :], op=ALU.add), [zb, xb], [zb])
                        if tile_id >= 0:
                            dma('sp', lambda e: e.dma_start(out=x1_s[tile_id, sub], in_=zb[:, :]), [zb], [x1_sb[tile_id]])
                        else:
                            dma('sp', lambda e: e.dma_start(out=x1s_s[:, :], in_=zb[0:NS, :]), [zb], [x1s_b])
                P.barrier()

            def phase_b():
              with contextlib.ExitStack() as es:
                Wup = load_w(es, "Wup", wup.rearrange("(kc p) n -> p kc n", p=128), 8, 4096, 0, 512)
                Wdn = load_w(es, "Wdn", wdn.rearrange("(kc p) n -> p kc n", p=128), 32, 1024, 0, 512)
                identb = P.sb(es, "identb", [128, 128], BF16)
                dma('pool', lambda e: e.dma_start(out=identb[:], in_=cst[:, 0:128]), [], [identb])
                gpl_sb = P.sb(es, "gpl_sb", [128, 8], F32)
                dma('sp', lambda e: e.dma_start(out=gpl_sb[:], in_=gpl[:, :]), [], [gpl_sb])
                gpo_b = P.sb(es, "gpo_b", [128, 1024], F32)
                dma('sp', lambda e: e.dma_start(out=gpo_b[:], in_=gpo[0:1, :].partition_broadcast(128)), [], [gpo_b])
                G = [P.ps(es, "M%d" % i, [128, 512], F32) for i in range(6)]
                TRB = [P.ps(es, "TRC%d" % i, [128, 1024], BF16) for i in range(2)]
                gi = [0]; ti = [0]

                def gp3():
                    gi[0] = (gi[0] + 1) % len(G)
                    return G[gi[0]]

                def tp3():
                    ti[0] = (ti[0] + 1) % len(TRB)
                    return TRB[ti[0]]
                x1a = [P.sb(es, "x1a%d" % i, [128, 1024], F32) for i in range(2)]
                x1b = [P.sb(es, "x1b%d" % i, [128, 1024], F32) for i in range(2)]
                xn2 = [P.sb(es, "xn2_%d" % i, [128, 1024], BF16) for i in range(2)]
                junk3 = P.sb(es, "junk3", [128, 512], BF16)
                ssq = P.sb(es, "ssq3", [128, 2], F32); std = P.sb(es, "std3", [128, 2], F32); rstd = P.sb(es, "rstd3", [128, 2], F32)
                stat = (ssq, std, rstd)
                h2T = P.sb(es, "h2T", [128, 8, T], BF16)
                aT = P.sb(es, "aT", [128, 32, T], BF16)
                rt = [P.sb(es, "rt%d" % i, [128, T], F32) for i in range(2)]
                zn = [P.sb(es, "zn3_%d" % i, [128, 1024], F32) for i in range(2)]
                cnt = [0]
                tiles = [(i, T, 128) for i in range(NTILE)] + [(-1, NS, NS)]
                for (tile_id, Tn, npart) in tiles:
                    nsub = max(1, Tn // 128)
                    for sub in range(nsub):
                        xa = x1a[cnt[0] % 2]
                        if tile_id >= 0:
                            dma('sp', lambda e: e.dma_start(out=xa[:, :], in_=x1_s[tile_id, sub]), [x1_sb[tile_id]], [xa])
                        else:
                            dma('sp', lambda e: e.dma_start(out=xa[0:NS, :], in_=x1s_s[:, :]), [x1s_b], [xa])
                        norm_T(es, xa, npart, gpl_sb, identb, xn2[cnt[0] % 2], stat, tp3(), h2T, sub * npart)
                        cnt[0] += 1
                    for c in range(32):
                        g = gp3()
                        mm([(g[:, 0:Tn], [(Wup[:, kc, c * 128:(c + 1) * 128], h2T[:, kc, 0:Tn]) for kc in range(8)])], [Wup, h2T], [g])
                        r = rt[c % 2]
                        op('act', lambda e: e.activation(out=r[:, 0:Tn], in_=g[:, 0:Tn], func=AF.Relu), [g], [r])
                        op('dve' if c % 2 == 0 else 'pool', lambda e: e.tensor_tensor(out=aT[:, c, 0:Tn], in0=r[:, 0:Tn], in1=r[:, 0:Tn], op=ALU.mult), [r], [aT])
                    for sub in range(nsub):
                        tok = slice(sub * npart, (sub + 1) * npart)
                        xb, zb = x1b[cnt[0] % 2], zn[cnt[0] % 2]
                        cnt[0] += 1
                        if tile_id >= 0:
                            dma('sp', lambda e: e.dma_start(out=xb[:, :], in_=x1_s[tile_id, sub]), [x1_sb[tile_id]], [xb])
                        else:
                            dma('sp', lambda e: e.dma_start(out=xb[0:NS, :], in_=x1s_s[:, :]), [x1s_b], [xb])
                        gz = [gp3(), gp3()]
                        for half in range(2):
                            mm([(gz[half][0:npart, :], [(aT[:, c, tok], Wdn[:, c, half * 512:(half + 1) * 512]) for c in range(32)])], [aT, Wdn], [gz[half]])
                        op('dve', lambda e: e.memset(ssq[0:npart, 0:2], 0.0), [], [ssq])
                        for half in range(2):
                            op('act', lambda e: e.activation(out=junk3[0:npart, 0:512], in_=gz[half][0:npart, :], func=AF.Square,
                                                             accum_out=ssq[0:npart, half:half + 1]), [gz[half], ssq], [junk3, ssq])
                        op('dve', lambda e: e.tensor_tensor(out=ssq[0:npart, 0:1], in0=ssq[0:npart, 0:1], in1=ssq[0:npart, 1:2], op=ALU.add), [ssq], [ssq])
                        op('act', lambda e: e.activation(out=std[0:npart, 0:1], in_=ssq[0:npart, 0:1], func=AF.Sqrt, bias=EPS, scale=1.0 / 1024.0), [ssq], [std])
                        op('dve', lambda e: e.reciprocal(out=rstd[0:npart, 0:1], in_=std[0:npart, 0:1]), [std], [rstd])
                        for half in range(2):
                            hs = slice(half * 512, (half + 1) * 512)
                            op('dve', lambda e: e.scalar_tensor_tensor(out=zb[0:npart, hs], in0=gz[half][0:npart, :], scalar=rstd[0:npart, 0:1],
                                                                       in1=gpo_b[0:npart, hs], op0=ALU.mult, op1=ALU.mult), [gz[half], rstd, gpo_b], [zb])
                        op('pool', lambda e: e.tensor_tensor(out=zb[0:npart, :], in0=zb[0:npart, :], in1=xb[0:npart, :], op=ALU.add), [zb, xb], [zb])
                        if tile_id >= 0:
                            q, it = tile_id // NT, tile_id % NT
                            r0 = it * T + sub * 128
                            dma('sp', lambda e: e.dma_start(out=yp[q, r0:r0 + 128, :], in_=zb[:, :]), [zb], [])
                        else:
                            dma('sp', lambda e: e.dma_start(out=ysm[:, :], in_=zb[0:NS, :]), [zb], [])
            try:
                phase_a1()
                chk('A1')
                phase_a2()
                chk('A2')
                phase_b()
            except _Stop:
                pass
            P.barrier(engines=('sp',))
    except AssertionError:
        if not cfg.get('stop'):
            raise
    return nc, P


FULL_CFG = dict(S=2048, NSEQ=2, NS=4, NPG=128, NPOOL=5120)
_CACHE = {}


def make_in_maps(inp, cfg, n_cores):
    NSEQ, NS = cfg['NSEQ'], cfg['NS']
    f = lambda a: np.ascontiguousarray(a, dtype=np.float32)
    ck = f(inp['cache_k'][0]).reshape(-1, 512)
    cv = f(inp['cache_v'][0]).reshape(-1, 512)
    cl = f(inp['cache_logf'][0]).reshape(-1, 8)
    shared = dict(
        ck=ck, cv=cv, cl=cl,
        w_in=f(inp['w_in'][0]), gpre=f(inp['g_pre_mix'][0].reshape(8, 128).T), bfv=f(inp['b_f'][0].reshape(1, 8)),
        wau=f(inp['w_alpha_up'][0]), bal=f(inp['b_alpha'][0].reshape(1, 256)), ggn=f(inp['g_gla_norm'][0].reshape(1, 128)),
        wfo=f(inp['w_fox_out'][0]), wgo=f(inp['w_gla_out'][0]), wo=f(inp['w_o'][0]),
        gpm=f(inp['g_post_mix'][0].reshape(1, 1024)), gpl=f(inp['g_pre_mlp'][0].reshape(8, 128).T),
        wup=f(inp['w_up'][0]), wdn=f(inp['w_down'][0]), gpo=f(inp['g_post_mlp'][0].reshape(1, 1024)),
        cst=make_consts(),
    )
    maps = []
    for c in range(n_cores):
        m = dict(shared)
        m['xp'] = f(inp['x_prompt'][c * NSEQ:(c + 1) * NSEQ])
        m['xsm'] = f(inp['x_sample'][c * NS:(c + 1) * NS, 0, :])
        m['sg'] = f(inp['state_gla'][0, c * NS:(c + 1) * NS])
        m['pt'] = np.ascontiguousarray(inp['page_table'][c * NS:(c + 1) * NS].reshape(1, -1), dtype=np.int32)
        maps.append(m)
    return maps


def assemble(results, cfg, n_cores):
    S, NSEQ, NS = cfg['S'], cfg['NSEQ'], cfg['NS']
    cat = lambda k: np.concatenate([np.asarray(r[k]) for r in results], axis=0)
    B = NSEQ * n_cores
    DB = NS * n_cores
    return (
        cat('yp').reshape(B, S, 1024).astype(np.float32),
        cat('ysm').reshape(DB, 1, 1024).astype(np.float32),
        cat('kp').reshape(1, B, S, 8, 64).astype(np.float32),
        cat('vp').reshape(1, B, S, 8, 64).astype(np.float32),
        cat('lfp').reshape(1, B, S, 8).astype(np.float32),
        cat('gsp').reshape(1, B, 4, 64, 128).astype(np.float32),
        cat('ksm').reshape(1, DB, 1, 8, 64).astype(np.float32),
        cat('vsm').reshape(1, DB, 1, 8, 64).astype(np.float32),
        cat('lfs').reshape(1, DB, 1, 8).astype(np.float32),
        cat('gss').reshape(1, DB, 4, 64, 128).astype(np.float32),
    )


def run(inp, cfg, n_cores):
    key = tuple(sorted(cfg.items()))
    if key not in _CACHE:
        _CACHE[key] = build(cfg)[0]
    nc = _CACHE[key]
    maps = make_in_maps(inp, cfg, n_cores)
    res = run_bass_kernel_spmd(nc, maps, core_ids=list(range(n_cores)))
    return assemble(res.results, cfg, n_cores)


def kernel(**inputs):
    inp = {k: np.asarray(v) for k, v in inputs.items()}
    return run(inp, FULL_CFG, N_CORES)
```

```python
import contextlib
import numpy as np
import concourse.bass as bass
import concourse.mybir as mybir
from concourse.bass_utils import run_bass_kernel_spmd

F32 = mybir.dt.float32
BF16 = mybir.dt.bfloat16
I32 = mybir.dt.int32
AF = mybir.ActivationFunctionType
ALU = mybir.AluOpType
AX = mybir.AxisListType
EPS = 1e-6
N_CORES = 8


class _Stop(Exception):
    pass


class Buf:
    def __init__(self, name, t):
        self.name = name
        self.t = t
        self.w = None
        self.r = []
        self.psum = False
        self.wl = []

    def __getitem__(self, idx):
        return self.t[idx]


class Prog:
    NDMA = 40

    def __init__(self, nc, es):
        self.nc = nc
        self.eng = {'pe': nc.tensor, 'act': nc.scalar, 'dve': nc.vector, 'pool': nc.gpsimd, 'sp': nc.sync}
        self.sems = {}
        self.cnt = {}
        for e in ['pe', 'act', 'dve', 'pool']:
            self.sems[e] = es.enter_context(nc.semaphore('s_' + e))
            self.cnt[e] = 0
        self.dsems = [es.enter_context(nc.semaphore('s_dma%d' % i)) for i in range(self.NDMA)]
        self.dcnt = [0] * self.NDMA
        self.dnext = 0
        self.waited = {}
        self.ninst = 0

    def sb(self, es, name, shape, dt):
        return Buf(name, es.enter_context(self.nc.sbuf_tensor(name, list(shape), dt)))

    def ps(self, es, name, shape, dt=F32):
        b = Buf(name, es.enter_context(self.nc.psum_tensor(name, list(shape), dt)))
        b.psum = True
        return b

    def _sem(self, key):
        return self.sems[key] if isinstance(key, str) else self.dsems[key]

    def _wait(self, e, ev):
        key, val, _ = ev
        k = (e, key)
        if self.waited.get(k, 0) >= val:
            return
        self.waited[k] = val
        self.eng[e].wait_ge(self._sem(key), val)
        self.ninst += 1

    def _deps(self, e, reads, writes):
        best = {}

        def add(ev):
            if ev[0] not in best or best[ev[0]][1] < ev[1]:
                best[ev[0]] = ev
        for b in reads:
            if b.w is not None:
                add(b.w)
            for ev in b.wl:
                add(ev)
            if b.psum:
                for r in b.r:
                    if r[2] != e:
                        add(r)
        for b in writes:
            if b.w is not None and b.w[2] != e:
                add(b.w)
            for r in b.r:
                if r[2] != e:
                    add(r)
        for ev in best.values():
            self._wait(e, ev)

    def _record(self, ev, reads, writes):
        for b in reads:
            b.r = [r for r in b.r if r[0] != ev[0]] + [ev]
        for b in writes:
            b.w = ev
            b.r = []

    def op(self, e, fn, reads=(), writes=()):
        self._deps(e, reads, writes)
        inst = fn(self.eng[e])
        self.cnt[e] += 1
        inst.then_inc(self.sems[e], 1)
        self.ninst += 1
        self._record((e, self.cnt[e], e), reads, writes)

    def mm(self, groups, reads, writes):
        e = 'pe'
        self._deps(e, reads, writes)
        inst = None
        for out_ap, pairs in groups:
            n = len(pairs)
            for i, (l, r) in enumerate(pairs):
                inst = self.nc.tensor.matmul(out_ap, l, r, start=(i == 0), stop=(i == n - 1))
                self.ninst += 1
        self.cnt[e] += 1
        inst.then_inc(self.sems[e], 1)
        self._record((e, self.cnt[e], e), reads, writes)

    def mm_raw(self, out_ap, l, r, start, stop, reads, writes, skip=False):
        e = 'pe'
        self._deps(e, reads, writes)
        inst = self.nc.tensor.matmul(out_ap, l, r, start=start, stop=stop, skip_group_check=skip)
        self.cnt[e] += 1
        inst.then_inc(self.sems[e], 1)
        self.ninst += 1
        self._record((e, self.cnt[e], e), reads, writes)

    def tr(self, items, reads, writes):
        e = 'pe'
        self._deps(e, reads, writes)
        inst = None
        for out_ap, in_ap, ident_ap in items:
            inst = self.nc.tensor.transpose(out_ap, in_ap, ident_ap)
            self.ninst += 1
        self.cnt[e] += 1
        inst.then_inc(self.sems[e], 1)
        self._record((e, self.cnt[e], e), reads, writes)

    def dma(self, q, fn, reads=(), writes=()):
        k = self.dnext
        self.dnext = (self.dnext + 1) % self.NDMA
        if self.dcnt[k] > 0:
            self._wait(q, (k, self.dcnt[k], 'dma'))
        self._deps(q, reads, writes)
        inst = fn(self.eng[q])
        self.dcnt[k] += 16
        inst.then_inc(self.dsems[k], 16)
        self.ninst += 1
        self._record((k, self.dcnt[k], 'dma'), reads, writes)

    def barrier(self, engines=('pe', 'act', 'dve', 'pool', 'sp')):
        for e in engines:
            for o in ['pe', 'act', 'dve', 'pool']:
                if o != e and self.cnt[o] > 0:
                    self._wait(e, (o, self.cnt[o], o))
            for k in range(self.NDMA):
                if self.dcnt[k] > 0:
                    self._wait(e, (k, self.dcnt[k], 'dma'))


def make_consts():
    c = np.zeros((128, 1280), np.float32)
    i = np.arange(128)
    c[:, 0:128] = np.eye(128)
    tri_u = (i[:, None] <= i[None, :]).astype(np.float32)
    tri_su = (i[:, None] > i[None, :]).astype(np.float32)
    c[:, 128:256] = tri_u
    c[:, 256:384] = 1.0
    c[:, 384:512] = tri_su
    c[:, 512:640] = tri_u * (-1.0 / 16.0)
    c[:, 640:768] = tri_su * (-1.0 / 16.0)
    for h in range(8):
        c[h, 768 + h * 64:768 + (h + 1) * 64] = 1.0
    return c


def build(cfg):
    S, NSEQ, NS, NPG, NPOOL = cfg['S'], cfg['NSEQ'], cfg['NS'], cfg['NPG'], cfg['NPOOL']
    T = 512
    NT = S // T
    NKB = S // 128
    NTILE = NSEQ * NT
    nc = bass.Bass("TRN2", target_bir_lowering=False)

    def din(name, shape, dt=F32):
        return nc.dram_tensor(name, list(shape), dt, kind="ExternalInput").ap()

    def dout(name, shape, dt=F32):
        return nc.dram_tensor(name, list(shape), dt, kind="ExternalOutput").ap()

    def dscr(name, shape, dt=F32):
        return nc.dram_tensor(name, list(shape), dt, kind="Internal").ap()

    xp = din("xp", [NSEQ, S, 1024]); xsm = din("xsm", [NS, 1024])
    ck = din("ck", [NPOOL * 128, 512]); cv = din("cv", [NPOOL * 128, 512]); cl = din("cl", [NPOOL * 128, 8])
    sg = din("sg", [NS, 4, 64, 128]); pt = din("pt", [1, NS * NPG], I32)
    w_in = din("w_in", [1024, 5144]); gpre = din("gpre", [128, 8]); bfv = din("bfv", [1, 8])
    wau = din("wau", [16, 256]); bal = din("bal", [1, 256]); ggn = din("ggn", [1, 128])
    wfo = din("wfo", [512, 1024]); wgo = din("wgo", [512, 1024]); wo = din("wo", [1024, 1024])
    gpm = din("gpm", [1, 1024]); gpl = din("gpl", [128, 8])
    wup = din("wup", [1024, 4096]); wdn = din("wdn", [4096, 1024]); gpo = din("gpo", [1, 1024])
    cst = din("cst", [128, 1280])
    yp = dout("yp", [NSEQ, S, 1024]); ysm = dout("ysm", [NS, 1024])
    kp = dout("kp", [NSEQ, S, 512]); vp = dout("vp", [NSEQ, S, 512]); lfp = dout("lfp", [NSEQ, S, 8])
    gsp = dout("gsp", [NSEQ, 4, 64, 128])
    ksm = dout("ksm", [NS, 512]); vsm = dout("vsm", [NS, 512]); lfs = dout("lfs", [NS, 8])
    gss = dout("gss", [NS, 4, 64, 128])
    hT_s = dscr("hT_s", [NTILE, 128, 8, 512], BF16)
    oT_s = dscr("oT_s", [NTILE, 128, 8, 512], BF16)
    x1_s = dscr("x1_s", [NTILE, 4, 128, 1024])
    x1s_s = dscr("x1s_s", [NS, 1024])

    w_in_v = w_in.rearrange("(kc p) n -> p kc n", p=128)

    try:
        with contextlib.ExitStack() as es0:
            P = Prog(nc, es0)
            es0.enter_context(nc.Block())
            op, mm, dma = P.op, P.mm, P.dma
            hTs = P.sb(es0, "hTs", [128, 8, NS], BF16)
            oTs = P.sb(es0, "oTs", [128, 8, NS], BF16)
            hT_sb = [Buf("hT_s%d" % i, hT_s[i]) for i in range(NTILE)]
            oT_sb = [Buf("oT_s%d" % i, oT_s[i]) for i in range(NTILE)]
            x1_sb = [Buf("x1_s%d" % i, x1_s[i]) for i in range(NTILE)]
            x1s_b = Buf("x1s_s", x1s_s)

            def load_w(es, name, src_v, kc, ncols, c0=0, chunk=1024):
                wb = P.sb(es, name, [128, kc, ncols], BF16)
                for k0 in range(0, kc, 8):
                    for a in range(0, ncols, chunk):
                        b = min(ncols, a + chunk)
                        part = Buf(name + "_part", None)
                        dma('pool', lambda e: e.dma_start(out=wb[:, k0:min(kc, k0 + 8), a:b], in_=src_v[:, k0:min(kc, k0 + 8), c0 + a:c0 + b]), [], [part])
                        wb.wl.append(part.w)
                return wb

            def norm_T(es_unused, xt, npart, g_sb, identb, xn, stat, trb, hT, col0):
                ssq, std, rstd = stat
                op('dve', lambda e: e.memset(ssq[0:npart, 0:1], 0.0), [], [ssq])
                op('act', lambda e: e.activation(out=xn[0:npart, :], in_=xt[0:npart, :], func=AF.Square,
                                                 accum_out=ssq[0:npart, 0:1]), [xt, ssq], [xn, ssq])
                op('act', lambda e: e.activation(out=std[0:npart, 0:1], in_=ssq[0:npart, 0:1], func=AF.Sqrt,
                                                 bias=EPS, scale=1.0 / 1024.0), [ssq], [std])
                op('dve', lambda e: e.reciprocal(out=rstd[0:npart, 0:1], in_=std[0:npart, 0:1]), [std], [rstd])
                op('dve', lambda e: e.tensor_scalar(out=xn[0:npart, :], in0=xt[0:npart, :], scalar1=rstd[0:npart, 0:1],
                                                    scalar2=None, op0=ALU.mult), [xt, rstd], [xn])
                P.tr([(trb[:, kc * 128:kc * 128 + npart], xn[0:npart, kc * 128:(kc + 1) * 128], identb[0:npart, 0:npart])
                      for kc in range(8)], [xn, identb], [trb])
                op('dve', lambda e: e.tensor_tensor(
                    out=hT[:, :, col0:col0 + npart],
                    in0=trb[:, :].rearrange("p (k t) -> p k t", k=8)[:, :, 0:npart],
                    in1=g_sb[:, 0:8].unsqueeze(2).to_broadcast([128, 8, npart]), op=ALU.mult), [trb, g_sb], [hT])

            def chk(stage):
                if cfg.get('stop') == stage:
                    raise _Stop()

            def phase_a1():
              with contextlib.ExitStack() as es:
                cf = P.sb(es, "cf", [128, 1280], F32)
                cb = P.sb(es, "cb", [128, 256], BF16)
                dma('sp', lambda e: e.dma_start(out=cf[:], in_=cst[:, :]), [], [cf])
                dma('pool', lambda e: e.dma_start(out=cb[:], in_=cst[:, 0:256]), [], [cb])
                ID = lambda n: cb[0:n, 0:n]
                MASKB = cb[:, 128:256]
                TRIU, ONES, TRISU, TRIU16, TRISU16 = cf[:, 128:256], cf[:, 256:384], cf[:, 384:512], cf[:, 512:640], cf[:, 640:768]
                BLKM = cf[0:8, 768:1280]
                gpre_sb = P.sb(es, "gpre_sb", [128, 8], F32)
                bf_b = P.sb(es, "bf_b", [128, 8], F32)
                wau_aug = P.sb(es, "wau_aug", [17, 256], F32)
                ggn_b = P.sb(es, "ggn_b", [128, 128], F32)
                dma('sp', lambda e: e.dma_start(out=gpre_sb[:], in_=gpre[:, :]), [], [gpre_sb])
                dma('sp', lambda e: e.dma_start(out=bf_b[:], in_=bfv[0:1, :].partition_broadcast(128)), [], [bf_b])
                dma('sp', lambda e: e.dma_start(out=wau_aug[0:16, :], in_=wau[:, :]), [], [wau_aug])
                dma('sp', lambda e: e.dma_start(out=wau_aug[16:17, :], in_=bal[0:1, :]), [], [wau_aug])
                dma('sp', lambda e: e.dma_start(out=ggn_b[:], in_=ggn[0:1, :].partition_broadcast(128)), [], [ggn_b])
                W1 = load_w(es, "W1", w_in_v, 8, 3096, 0, 516)
                CQ, CK, CV, CF, CQG, CKG, CVG, COG, CLR = 0, 512, 1024, 1536, 1544, 1800, 2056, 2568, 3080

                G = [P.ps(es, "G%d" % i, [128, 512], F32) for i in range(2)]
                STB = [P.ps(es, "STB%d" % i, [128, 512], F32) for i in range(2)]
                OACC = [P.ps(es, "OACC%d" % i, [128, 4, 128], F32) for i in range(2)]
                SACC = P.ps(es, "SACC", [128, 512], F32)
                TRB = [P.ps(es, "TRB%d" % i, [128, 1024], BF16) for i in range(1)]
                gi = [0]; ti = [0]; sti = [0]

                fox_mode = [False]

                def gp():
                    ring = G if fox_mode[0] else G + STB
                    gi[0] = (gi[0] + 1) % len(ring)
                    return ring[gi[0]]

                def tp():
                    ti[0] = (ti[0] + 1) % len(TRB)
                    return TRB[ti[0]]

                class Ring:
                    def __init__(self, name, n, shape, dt):
                        self.b = [P.sb(es, "%s%d" % (name, i), shape, dt) for i in range(n)]
                        self.i = 0

                    def next(self):
                        self.i = (self.i + 1) % len(self.b)
                        return self.b[self.i]

                xt_r = Ring("xt", 2, [128, 1024], F32)
                junk = P.sb(es, "junk", [128, 128], BF16)
                xn_r = Ring("xn", 2, [128, 1024], BF16)
                ssq = P.sb(es, "ssq", [128, 8], F32); std = P.sb(es, "std", [128, 8], F32); rstd = P.sb(es, "rstd", [128, 8], F32)
                stat = (ssq, std, rstd)
                hT = P.sb(es, "hT", [128, 8, T], BF16)
                QT = P.sb(es, "QT", [128, 8, T], BF16)
                KT = P.sb(es, "KT", [128, 4, S], BF16)
                Vaug = P.sb(es, "Vaug", [128, NKB, 8, 65], BF16)
                Cb = P.sb(es, "Cb", [128, NKB + 1, 8], F32)
                Ffull = P.sb(es, "Ffull", [128, NKB, 8], F32)
                biasT = P.sb(es, "biasT", [128, 4, NKB, 8], F32)
                kv_r = Ring("kvo", 2, [128, 512], F32)
                kb16_r = Ring("kb16", 2, [128, 512], BF16)
                lf_e = P.sb(es, "lf_e", [128, 8], F32)
                lf_r = Ring("lf", 2, [128, 8], F32)
                PT_r = Ring("PT", 8, [128, 128], BF16)
                rden_r = Ring("rden", 2, [128, 4], F32)
                ofx = P.sb(es, "ofx", [128, 4, 512], BF16)
                oT = P.sb(es, "oT", [128, 8, T], BF16)
                qgT = P.sb(es, "qgT", [128, 2, T], F32); kgT = P.sb(es, "kgT", [128, 2, T], F32)
                lrT = P.sb(es, "lrT", [17, T], F32)
                la_r = Ring("la", 2, [128, 256], F32)
                le_t = P.sb(es, "le_t", [128, 256], F32)
                kg_r = Ring("kg", 2, [128, 256], F32)
                vg_r = Ring("vg", 2, [128, 512], BF16)
                sog_r = Ring("sog", 2, [128, 512], BF16)
                EbT_r = Ring("EbT", 2, [128, 2, 128], F32); EnbT_r = Ring("EnbT", 2, [128, 2, 128], F32)
                qtl_r = Ring("qtl", 2, [128, 2, 128], F32); ktl_r = Ring("ktl", 2, [128, 2, 128], F32)
                ebrev = P.sb(es, "ebrev", [128, 256], F32)
                khat_r = Ring("khat", 2, [128, 256], BF16)
                ATs_r = Ring("ATs", 2, [128, 4, 128], BF16)
                Sst = P.sb(es, "Sst", [128, 2, 128], F32)
                t1 = P.sb(es, "t1", [128, 4, 128], F32)
                ogl_r = Ring("ogl", 2, [128, 512], BF16)

                op('dve', lambda e: e.memset(Vaug[:, :, :, 64:65], 1.0), [], [Vaug])
                op('dve', lambda e: e.memset(QT[:, :, :], 0.0), [], [QT])
                op('dve', lambda e: e.memset(lrT[:, :], 1.0), [], [lrT])

                chk('a1_setup')
                idxA = P.sb(es, "idxA", [128, NS * NPG], I32)
                idxB = P.sb(es, "idxB", [128, NS * NPG], F32)
                rid = P.sb(es, "rid", [128, 1], I32); ridf = P.sb(es, "ridf", [128, 1], F32)
                dma('sp', lambda e: e.dma_start(out=idxA[:], in_=pt[0:1, :].partition_broadcast(128)), [], [idxA])
                op('pool', lambda e: e.iota(rid[:], pattern=[[0, 1]], base=0, channel_multiplier=1), [], [rid])
                op('dve', lambda e: e.tensor_copy(out=ridf[:], in_=rid[:]), [rid], [ridf])
                op('dve', lambda e: e.tensor_copy(out=idxB[:], in_=idxA[:]), [idxA], [idxB])
                op('dve', lambda e: e.tensor_scalar(out=idxB[:], in0=idxB[:], scalar1=128.0, scalar2=ridf[:, 0:1],
                                                    op0=ALU.mult, op1=ALU.add), [idxB, ridf], [idxB])
                op('dve', lambda e: e.tensor_copy(out=idxA[:], in_=idxB[:]), [idxB], [idxA])

                row_r = Ring("row", 4, [1, 512], F32)
                qrow1 = P.sb(es, "qrow1", [1, 512], F32)
                krow1 = P.sb(es, "krow1", [1, 512], F32)
                vrowb1 = P.sb(es, "vrowb1", [1, 512], BF16)
                lfrow = [P.sb(es, "lfrow%d" % b, [1, 8], F32) for b in range(NS)]
                lrTs = P.sb(es, "lrTs", [17, NS], F32)
                qgTs = P.sb(es, "qgTs", [128, 2, NS], F32)
                laTs = P.sb(es, "laTs", [128, 2], F32); elaTs = P.sb(es, "elaTs", [128, 2], F32)
                Ssm = P.sb(es, "Ssm", [128, 2, 128], F32)
                one11 = cf[0:1, 256:257]

                chk('a1_idx')
                xs_t = xt_r.next()
                dma('sp', lambda e: e.dma_start(out=xs_t[0:NS, :], in_=xsm[:, :]), [], [xs_t])
                norm_T(es, xs_t, NS, gpre_sb, cb, xn_r.next(), stat, tp(), hTs, 0)
                chk('s1')
                op('dve', lambda e: e.memset(lrTs[:, :], 1.0), [], [lrTs])
                g = gp()
                mm([(g[0:16, 0:NS], [(W1[:, kc, CLR:CLR + 16], hTs[:, kc, 0:NS]) for kc in range(8)])], [W1, hTs], [g])
                op('dve', lambda e: e.tensor_copy(out=lrTs[0:16, :], in_=g[0:16, 0:NS]), [g], [lrTs])
                g = gp()
                mm([(g[:, hp * NS:(hp + 1) * NS], [(W1[:, kc, CQG + hp * 128:CQG + (hp + 1) * 128], hTs[:, kc, 0:NS]) for kc in range(8)])
                    for hp in range(2)], [W1, hTs], [g])
                op('act', lambda e: e.activation(out=qgTs[:, :, :], in_=g[:, 0:2 * NS].rearrange("p (h b) -> p h b", h=2),
                                                 func=AF.Copy, scale=0.125), [g], [qgTs])

                chk('s2')
                def rowproj(b, c0, n):
                    g = gp()
                    mm([(g[0:1, 0:n], [(hTs[:, kc, b:b + 1], W1[:, kc, c0:c0 + n]) for kc in range(8)])], [W1, hTs], [g])
                    return g

                def softplus_neg(dst, src_ap, npart, n, tmp, bias_ap=None, src_bufs=()):
                    if bias_ap is not None:
                        op('dve', lambda e: e.tensor_tensor(out=tmp[0:npart, 0:n], in0=src_ap, in1=bias_ap, op=ALU.add),
                           list(src_bufs), [tmp])
                        op('act', lambda e: e.activation(out=tmp[0:npart, 0:n], in_=tmp[0:npart, 0:n], func=AF.Exp, scale=-1.0),
                           [tmp], [tmp])
                    else:
                        op('act', lambda e: e.activation(out=tmp[0:npart, 0:n], in_=src_ap, func=AF.Exp, scale=-1.0),
                           list(src_bufs), [tmp])
                    op('act', lambda e: e.activation(out=dst[0:npart, 0:n], in_=tmp[0:npart, 0:n], func=AF.Ln, bias=1.0, scale=1.0),
                       [tmp], [dst])

                for b in range(NS):
                    g = rowproj(b, CK, 512)
                    kr = row_r.next()
                    op('act', lambda e: e.copy(out=kr[:, :], in_=g[0:1, :]), [g], [kr])
                    dma('sp', lambda e: e.dma_start(out=ksm[b:b + 1, :], in_=kr[:, :]), [kr], [])
                    g = rowproj(b, CV, 512)
                    vr = row_r.next()
                    op('act', lambda e: e.copy(out=vr[:, :], in_=g[0:1, :]), [g], [vr])
                    dma('sp', lambda e: e.dma_start(out=vsm[b:b + 1, :], in_=vr[:, :]), [vr], [])
                    chk('s3')
                    g = rowproj(b, CF, 8)
                    tmpr = row_r.next()
                    softplus_neg(lfrow[b], g[0:1, 0:8], 1, 8, tmpr, bias_ap=bf_b[0:1, :], src_bufs=[g, bf_b])
                    op('dve', lambda e: e.tensor_scalar(out=lfrow[b][:, :], in0=lfrow[b][:, :], scalar1=-1.0, scalar2=None,
                                                        op0=ALU.mult), [lfrow[b]], [lfrow[b]])
                    dma('sp', lambda e: e.dma_start(out=lfs[b:b + 1, :], in_=lfrow[b][:, :]), [lfrow[b]], [])
                    chk('s4')
                    g = rowproj(b, CKG, 256)
                    kgr = row_r.next()
                    op('act', lambda e: e.copy(out=kgr[:, 0:256], in_=g[0:1, 0:256]), [g], [kgr])
                    g = rowproj(b, CVG, 512)
                    vgr = row_r.next()
                    op('act', lambda e: e.copy(out=vgr[:, :], in_=g[0:1, :]), [g], [vgr])
                    g = rowproj(b, COG, 512)
                    sogr = row_r.next()
                    op('act', lambda e: e.activation(out=sogr[:, :], in_=g[0:1, :], func=AF.Silu), [g], [sogr])
                    chk('s5')
                    g = gp()
                    mm([(g[:, hp:hp + 1], [(wau_aug[0:17, hp * 128:(hp + 1) * 128], lrTs[0:17, b:b + 1])]) for hp in range(2)],
                       [wau_aug, lrTs], [g])
                    softplus_neg(laTs, g[:, 0:2], 128, 2, le_t, src_bufs=[g])
                    op('act', lambda e: e.activation(out=elaTs[:, :], in_=laTs[:, :], func=AF.Exp, scale=-1.0 / 16.0), [laTs], [elaTs])
                    chk('s6')
                    dma('sp', lambda e: e.dma_start(out=Ssm[:, :, :], in_=sg[b].rearrange("(hp dh) d e -> (dh d) hp e", dh=2)), [], [Ssm])
                    g = gp()
                    mm([(g[:, hp * 256:(hp + 1) * 256], [(kgr[0:1, hp * 128:(hp + 1) * 128], vgr[0:1, hp * 256:(hp + 1) * 256])])
                        for hp in range(2)], [kgr, vgr], [g])
                    for hp in range(2):
                        for dh in range(2):
                            pd = slice(dh * 64, dh * 64 + 64)
                            op('dve', lambda e: e.scalar_tensor_tensor(
                                out=Ssm[pd, hp, :], in0=Ssm[pd, hp, :], scalar=elaTs[pd, hp:hp + 1],
                                in1=g[pd, hp * 256 + dh * 128:hp * 256 + dh * 128 + 128], op0=ALU.mult, op1=ALU.add),
                               [Ssm, elaTs, g], [Ssm])
                    chk('s7')
                    dma('sp', lambda e: e.dma_start(out=gss[b].rearrange("(hp dh) d e -> (dh d) hp e", dh=2), in_=Ssm[:, :, :]), [Ssm], [])
                    chk('s7b')
                    gab = [gp(), gp()]
                    mm([(gab[hd % 2][0:1, (hd // 2) * 128:(hd // 2 + 1) * 128],
                         [(qgTs[(hd % 2) * 64:(hd % 2) * 64 + 64, hd // 2, b:b + 1], Ssm[(hd % 2) * 64:(hd % 2) * 64 + 64, hd // 2, :])])
                        for hd in (0, 2, 1, 3)], [qgTs, Ssm], gab)
                    chk('s8')
                    orow = row_r.next()
                    op('dve', lambda e: e.memset(ssq[0:1, 0:4], 0.0), [], [ssq])
                    for hd in range(4):
                        gsrc = gab[hd % 2]
                        op('act', lambda e: e.activation(out=junk[0:1, 0:128], in_=gsrc[0:1, (hd // 2) * 128:(hd // 2 + 1) * 128], func=AF.Square,
                                                         accum_out=ssq[0:1, hd:hd + 1]), [gsrc, ssq], [junk, ssq])
                    op('act', lambda e: e.activation(out=std[0:1, 0:4], in_=ssq[0:1, 0:4], func=AF.Sqrt, bias=EPS, scale=1.0 / 128.0), [ssq], [std])
                    op('dve', lambda e: e.reciprocal(out=rstd[0:1, 0:4], in_=std[0:1, 0:4]), [std], [rstd])
                    for hd in range(4):
                        gsrc = gab[hd % 2]
                        op('dve', lambda e: e.tensor_scalar(out=orow[0:1, hd * 128:(hd + 1) * 128], in0=gsrc[0:1, (hd // 2) * 128:(hd // 2 + 1) * 128],
                                                            scalar1=rstd[0:1, hd:hd + 1], scalar2=None, op0=ALU.mult), [gsrc, rstd], [orow])
                    op('dve', lambda e: e.tensor_tensor(out=orow[:, :].rearrange("p (h e) -> p h e", h=4),
                                                        in0=orow[:, :].rearrange("p (h e) -> p h e", h=4),
                                                        in1=ggn_b[0:1, :].unsqueeze(1).to_broadcast([1, 4, 128]), op=ALU.mult), [orow, ggn_b], [orow])
                    op('dve', lambda e: e.tensor_tensor(out=orow[:, :], in0=orow[:, :], in1=sogr[:, :], op=ALU.mult), [orow, sogr], [orow])
                    chk('s9')
                    g = gp()
                    mm([(g[:, kc:kc + 1], [(orow[0:1, kc * 128:(kc + 1) * 128], one11)]) for kc in range(4)], [orow, cf], [g])
                    op('dve', lambda e: e.tensor_copy(out=oTs[:, 4:8, b], in_=g[:, 0:4]), [g], [oTs])

                chk('a1_sproj')
                Kpg_r = Ring("Kpg", 3, [128, 512], BF16); Vpg_r = Ring("Vpg", 5, [128, 512], BF16)
                lpg_r = Ring("lpg", 4, [128, 8], F32)
                prod_r = Ring("prod", 2, [128, 512], F32)
                qbc = P.sb(es, "qbc", [128, 512], BF16)
                sc_r = Ring("sc", 2, [128, 8], F32); lg_r = Ring("lg", 3, [128, 8], F32)
                pb_r = Ring("pb", 3, [128, 8], BF16)
                carry = P.sb(es, "carry", [128, 8], F32); pacc = P.sb(es, "pacc", [128, 8], F32)
                den8 = P.sb(es, "den8", [8, 1], F32)

                def sample_decode():
                    for b in range(NS):
                        g = rowproj(b, CQ, 512)
                        op('act', lambda e: e.activation(out=qrow1[:, :], in_=g[0:1, :], func=AF.Copy, scale=0.125), [g], [qrow1])
                        g = rowproj(b, CK, 512)
                        op('act', lambda e: e.copy(out=krow1[:, :], in_=g[0:1, :]), [g], [krow1])
                        g = rowproj(b, CV, 512)
                        op('dve', lambda e: e.tensor_copy(out=vrowb1[:, :], in_=g[0:1, :]), [g], [vrowb1])
                        g = gp()
                        mm([(g[:, :], [(cf[0:1, 256:384], qrow1[0:1, :])])], [cf, qrow1], [g])
                        op('dve', lambda e: e.tensor_copy(out=qbc[:, :], in_=g[:, :]), [g], [qbc])
                        g = gp()
                        mm([(g[:, 0:8], [(cf[0:1, 256:384], lfrow[b][0:1, :])])], [cf, lfrow[b]], [g])
                        op('dve', lambda e: e.tensor_copy(out=carry[:, :], in_=g[:, 0:8]), [g], [carry])
                        op('dve', lambda e: e.memset(pacc[:, :], 0.0), [], [pacc])
                        pr = prod_r.next()
                        op('dve', lambda e: e.tensor_tensor(out=pr[0:1, :], in0=krow1[0:1, :], in1=qrow1[0:1, :], op=ALU.mult),
                           [krow1, qrow1], [pr])
                        sc = sc_r.next()
                        op('dve', lambda e: e.tensor_reduce(out=sc[0:1, :], in_=pr[0:1, :].rearrange("p (h d) -> p h d", h=8),
                                                            axis=AX.X, op=ALU.add), [pr], [sc])
                        pb = pb_r.next()
                        op('act', lambda e: e.activation(out=pb[0:1, :], in_=sc[0:1, :], func=AF.Exp), [sc], [pb])
                        op('dve', lambda e: e.tensor_tensor(out=pacc[0:1, :], in0=pacc[0:1, :], in1=pb[0:1, :], op=ALU.add), [pacc, pb], [pacc])
                        P.mm_raw(SACC[0:8, :], pb[0:1, :], vrowb1[0:1, :], True, False, [pb, vrowb1], [SACC])
                        yield
                        order = list(range(NPG - 1, -1, -1))
                        stp = {}

                        def stA(j):
                            col = b * NPG + j
                            Kp, Vp, lp = Kpg_r.next(), Vpg_r.next(), lpg_r.next()
                            off = bass.IndirectOffsetOnAxis(ap=idxA[:, col:col + 1], axis=0)
                            dma('pool', lambda e: e.indirect_dma_start(out=Kp[:, :], out_offset=None, in_=ck[:, :], in_offset=off), [idxA], [Kp])
                            dma('pool', lambda e: e.indirect_dma_start(out=Vp[:, :], out_offset=None, in_=cv[:, :], in_offset=off), [idxA], [Vp])
                            dma('pool', lambda e: e.indirect_dma_start(out=lp[:, :], out_offset=None, in_=cl[:, :], in_offset=off), [idxA], [lp])
                            stp[j] = [Kp, Vp, lp, None]

                        def stB1(j):
                            Kp, Vp, lp, _ = stp[j]
                            g = gp()
                            mm([(g[:, 0:8], [(TRISU, lp[:, :])]), (g[:, 8:16], [(ONES, lp[:, :])])], [cf, lp], [g])
                            pr = prod_r.next()
                            op('dve', lambda e: e.tensor_tensor(out=pr[:, :], in0=Kp[:, :], in1=qbc[:, :], op=ALU.mult), [Kp, qbc], [pr])
                            sc = sc_r.next()
                            op('dve', lambda e: e.tensor_reduce(out=sc[:, :], in_=pr[:, :].rearrange("p (h d) -> p h d", h=8),
                                                                axis=AX.X, op=ALU.add), [pr], [sc])
                            lg = lg_r.next()
                            op('dve', lambda e: e.tensor_tensor(out=lg[:, :], in0=g[:, 0:8], in1=carry[:, :], op=ALU.add), [g, carry], [lg])
                            op('dve', lambda e: e.tensor_tensor(out=lg[:, :], in0=lg[:, :], in1=sc[:, :], op=ALU.add), [lg, sc], [lg])
                            op('dve', lambda e: e.tensor_tensor(out=carry[:, :], in0=g[:, 8:16], in1=carry[:, :], op=ALU.add), [g, carry], [carry])
                            stp[j][3] = lg

                        def stB2(j):
                            lg = stp[j][3]
                            pb = pb_r.next()
                            op('act', lambda e: e.activation(out=pb[:, :], in_=lg[:, :], func=AF.Exp), [lg], [pb])
                            op('dve', lambda e: e.tensor_tensor(out=pacc[:, :], in0=pacc[:, :], in1=pb[:, :], op=ALU.add), [pacc, pb], [pacc])
                            stp[j][3] = pb

                        def stC(j, last):
                            Kp, Vp, lp, pb = stp.pop(j)
                            P.mm_raw(SACC[0:8, :], pb[:, :], Vp[:, :], False, last, [pb, Vp], [SACC])

                        stA(order[0])
                        if NPG > 1:
                            stA(order[1])
                        for i in range(NPG + 2):
                            if i + 2 < NPG:
                                stA(order[i + 2])
                            if i < NPG:
                                stB1(order[i])
                            if 0 <= i - 1 < NPG:
                                stB2(order[i - 1])
                            if 0 <= i - 2 < NPG:
                                stC(order[i - 2], i - 2 == NPG - 1)
                            yield
                        g = gp()
                        mm([(g[0:8, 0:1], [(pacc[:, :], cf[:, 256:257])])], [pacc, cf], [g])
                        op('dve', lambda e: e.reciprocal(out=den8[:, :], in_=g[0:8, 0:1]), [g], [den8])
                        on8 = prod_r.next()
                        op('dve', lambda e: e.scalar_tensor_tensor(out=on8[0:8, :], in0=SACC[0:8, :], scalar=den8[:, 0:1], in1=BLKM,
                                                                   op0=ALU.mult, op1=ALU.mult), [SACC, den8, cf], [on8])
                        g = gp()
                        mm([(g[:, kc:kc + 1], [(on8[0:8, kc * 128:(kc + 1) * 128], cf[0:8, 256:257])]) for kc in range(4)], [on8, cf], [g])
                        op('dve', lambda e: e.tensor_copy(out=oTs[:, 0:4, b], in_=g[:, 0:4]), [g], [oTs])
                        yield

                sgen = sample_decode()
                sdone = [False]

                def pump(n=1):
                    for _ in range(n):
                        if sdone[0]:
                            return
                        try:
                            next(sgen)
                        except StopIteration:
                            sdone[0] = True

                total_pumps = NS * (NPG + 4)
                n_slots = NSEQ * sum(8 * (4 * it + 4) for it in range(NT))
                ppump = max(1, -(-total_pumps // max(1, n_slots)))

                if cfg.get('stop') == 'a1_sdec':
                    while not sdone[0]:
                        pump(1)
                    raise _Stop()
                for q in range(NSEQ):
                    op('dve', lambda e: e.memset(Cb[:, 0, :], 0.0), [], [Cb])
                    op('dve', lambda e: e.memset(Sst[:, :, :], 0.0), [], [Sst])
                    for it in range(NT):
                        tile_id = q * NT + it
                        t0 = it * T
                        for sub in range(4):
                            r0 = t0 + sub * 128
                            kb = it * 4 + sub
                            xt = xt_r.next()
                            dma('sp', lambda e: e.dma_start(out=xt[:, :], in_=xp[q, r0:r0 + 128, :]), [], [xt])
                            norm_T(es, xt, 128, gpre_sb, cb, xn_r.next(), stat, tp(), hT, sub * 128)
                            chk('p1')
                            tok = slice(sub * 128, sub * 128 + 128)

                            def tproj(c0, n):
                                g = gp()
                                mm([(g[:, 0:n], [(hT[:, kc, tok], W1[:, kc, c0:c0 + n]) for kc in range(8)])], [hT, W1], [g])
                                return g
                            g = tproj(CK, 512)
                            chk('p1a')
                            ko = kv_r.next()
                            op('act', lambda e: e.copy(out=ko[:, :], in_=g[:, :]), [g], [ko])
                            chk('p1b')
                            k16 = kb16_r.next()
                            op('dve', lambda e: e.tensor_copy(out=k16[:, :], in_=ko[:, :]), [ko], [k16])
                            chk('p1c')
                            dma('sp', lambda e: e.dma_start(out=kp[q, r0:r0 + 128, :], in_=ko[:, :]), [ko], [])
                            chk('p2')
                            tb = tp()
                            P.tr([(tb[:, hp * 128:(hp + 1) * 128], k16[:, hp * 128:(hp + 1) * 128], ID(128)) for hp in range(4)], [k16, cb], [tb])
                            op('act', lambda e: e.copy(out=KT[:, :, kb * 128:(kb + 1) * 128],
                                                       in_=tb[:, 0:512].rearrange("p (h t) -> p h t", h=4)), [tb], [KT])
                            chk('p3')
                            g = tproj(CV, 512)
                            vo = kv_r.next()
                            op('act', lambda e: e.copy(out=vo[:, :], in_=g[:, :]), [g], [vo])
                            op('dve', lambda e: e.tensor_copy(out=Vaug[:, kb, :, 0:64], in_=vo[:, :].rearrange("p (h d) -> p h d", h=8)), [vo], [Vaug])
                            dma('sp', lambda e: e.dma_start(out=vp[q, r0:r0 + 128, :], in_=vo[:, :]), [vo], [])
                            chk('p4')
                            g = tproj(CF, 8)
                            lf = lf_r.next()
                            softplus_neg(lf, g[:, 0:8], 128, 8, lf_e, bias_ap=bf_b[:, :], src_bufs=[g, bf_b])
                            op('dve', lambda e: e.tensor_scalar(out=lf[:, :], in0=lf[:, :], scalar1=-1.0, scalar2=None, op0=ALU.mult), [lf], [lf])
                            dma('sp', lambda e: e.dma_start(out=lfp[q, r0:r0 + 128, :], in_=lf[:, :]), [lf], [])
                            chk('p5')
                            g = gp()
                            mm([(g[:, 0:8], [(TRIU, lf[:, :])]), (g[:, 8:16], [(ONES, lf[:, :])])], [cf, lf], [g])
                            op('dve', lambda e: e.tensor_tensor(out=Ffull[:, kb, :], in0=g[:, 0:8], in1=Cb[:, kb, :], op=ALU.add), [g, Cb], [Ffull])
                            op('dve', lambda e: e.tensor_tensor(out=Cb[:, kb + 1, :], in0=g[:, 8:16], in1=Cb[:, kb, :], op=ALU.add), [g, Cb], [Cb])
                        chk('a1_t1')
                        for hp in range(4):
                            g = gp()
                            mm([(g[:, :], [(W1[:, kc, CQ + hp * 128:CQ + (hp + 1) * 128], hT[:, kc, :]) for kc in range(8)])], [W1, hT], [g])
                            op('act', lambda e: e.activation(out=QT[0:64, 2 * hp, :], in_=g[0:64, :], func=AF.Copy, scale=0.125), [g], [QT])
                            op('act', lambda e: e.activation(out=QT[64:128, 2 * hp + 1, :], in_=g[64:128, :], func=AF.Copy, scale=0.125), [g], [QT])
                        for jl in range(4):
                            jg = it * 4 + jl
                            op('dve', lambda e: e.tensor_tensor(out=biasT[:, jl, 0:jg + 1, :],
                                                                in0=Cb[:, jg:jg + 1, :].to_broadcast([128, jg + 1, 8]),
                                                                in1=Ffull[:, 0:jg + 1, :], op=ALU.subtract), [Cb, Ffull], [biasT])
                        nkb = it * 4 + 4
                        slots = [(h, kb) for h in range(8) for kb in range(nkb)]

                        def issue_st(h, kb):
                            hp, dh = h // 2, h % 2
                            pd = slice(dh * 64, dh * 64 + 64)
                            c0 = max(0, kb - it * 4) * 128
                            sti[0] += 1
                            g = STB[sti[0] % 2]
                            mm([(g[:, c0:512], [(KT[:, hp, kb * 128:(kb + 1) * 128], QT[:, h, c0:512])])], [KT, QT], [g])
                            return g
                        ahead = cfg.get('st_ahead', True)
                        fox_mode[0] = True
                        g_next = issue_st(*slots[0]) if ahead else None
                        for si, (h, kb) in enumerate(slots):
                            if ahead:
                                g = g_next
                                g_next = issue_st(*slots[si + 1]) if si + 1 < len(slots) else None
                            else:
                                g = issue_st(h, kb)
                            oacc = OACC[h % 2]
                            jl0 = max(0, kb - it * 4)
                            if kb == 0 and cfg.get('zero_acc'):
                                op('dve', lambda e: e.memset(oacc[:, :, :], 0.0), [], [oacc])
                            for jl in range(jl0, 4):
                                jg = it * 4 + jl
                                pt_ = PT_r.next()
                                op('act', lambda e: e.activation(out=pt_[:, :], in_=g[:, jl * 128:(jl + 1) * 128], func=AF.Exp,
                                                                 bias=biasT[:, jl, kb, h:h + 1], scale=1.0), [g, biasT], [pt_])
                                if kb == jg:
                                    op(cfg.get('mask_eng', 'dve'), lambda e: e.tensor_tensor(out=pt_[:, :], in0=pt_[:, :], in1=MASKB, op=ALU.mult), [pt_, cb], [pt_])
                                P.mm_raw(oacc[:, jl, 0:65], pt_[:, :], Vaug[:, kb, h, :], (kb == 0 and jl == 0) and not cfg.get('zero_acc'), (kb == jg), [pt_, Vaug], [oacc], skip=True)
                            pump(ppump)
                            if kb == nkb - 1:
                                rd = rden_r.next()
                                op('dve', lambda e: e.reciprocal(out=rd[:, :], in_=oacc[:, :, 64]), [oacc], [rd])
                                op('dve', lambda e: e.tensor_tensor(out=ofx[:, :, h * 64:(h + 1) * 64], in0=oacc[:, :, 0:64],
                                                                    in1=rd[:, 0:4].unsqueeze(2).to_broadcast([128, 4, 64]), op=ALU.mult), [oacc, rd], [ofx])
                        fox_mode[0] = False
                        chk('a1_t4')
                        for half in range(2):
                            tb = tp()
                            P.tr([(tb[:, (i * 4 + jl) * 128:(i * 4 + jl + 1) * 128], ofx[:, jl, (half * 2 + i) * 128:(half * 2 + i + 1) * 128], ID(128))
                                  for i in range(2) for jl in range(4)], [ofx, cb], [tb])
                            op('act', lambda e: e.copy(out=oT[:, half * 2:half * 2 + 2, :],
                                                       in_=tb[:, :].rearrange("p (c t) -> p c t", c=2)), [tb], [oT])
                        chk('a1_t4b')
                        for hp in range(2):
                            g = gp()
                            mm([(g[:, :], [(W1[:, kc, CQG + hp * 128:CQG + (hp + 1) * 128], hT[:, kc, :]) for kc in range(8)])], [W1, hT], [g])
                            op('act', lambda e: e.activation(out=qgT[:, hp, :], in_=g[:, :], func=AF.Copy, scale=0.125), [g], [qgT])
                            g = gp()
                            mm([(g[:, :], [(W1[:, kc, CKG + hp * 128:CKG + (hp + 1) * 128], hT[:, kc, :]) for kc in range(8)])], [W1, hT], [g])
                            op('act', lambda e: e.copy(out=kgT[:, hp, :], in_=g[:, :]), [g], [kgT])
                        g = gp()
                        mm([(g[0:16, :], [(W1[:, kc, CLR:CLR + 16], hT[:, kc, :]) for kc in range(8)])], [W1, hT], [g])
                        op('dve', lambda e: e.tensor_copy(out=lrT[0:16, :], in_=g[0:16, :]), [g], [lrT])
                        for sub in range(4):
                            tok = slice(sub * 128, sub * 128 + 128)
                            g = gp()
                            mm([(g[:, 0:256], [(hT[:, kc, tok], W1[:, kc, CKG:CKG + 256]) for kc in range(8)])], [hT, W1], [g])
                            kg = kg_r.next()
                            op('act', lambda e: e.copy(out=kg[:, :], in_=g[:, 0:256]), [g], [kg])
                            g = gp()
                            mm([(g[:, :], [(hT[:, kc, tok], W1[:, kc, CVG:CVG + 512]) for kc in range(8)])], [hT, W1], [g])
                            vg = vg_r.next()
                            op('dve', lambda e: e.tensor_copy(out=vg[:, :], in_=g[:, :]), [g], [vg])
                            g = gp()
                            mm([(g[:, :], [(hT[:, kc, tok], W1[:, kc, COG:COG + 512]) for kc in range(8)])], [hT, W1], [g])
                            sog = sog_r.next()
                            op('act', lambda e: e.activation(out=sog[:, :], in_=g[:, :], func=AF.Silu), [g], [sog])
                            g = gp()
                            mm([(g[:, 0:256], [(lrT[0:17, tok], wau_aug[0:17, :])])], [lrT, wau_aug], [g])
                            la = la_r.next()
                            softplus_neg(la, g[:, 0:256], 128, 256, le_t, src_bufs=[g])
                            g = gp()
                            mm([(g[:, hp * 128:(hp + 1) * 128], [(la[:, hp * 128:(hp + 1) * 128], TRIU16)]) for hp in range(2)]
                               + [(g[:, 256:512], [(TRISU16, la[:, :])])], [la, cf], [g])
                            EbT, EnbT = EbT_r.next(), EnbT_r.next()
                            gv = g[:, 0:256].rearrange("p (h t) -> p h t", h=2)
                            op('act', lambda e: e.activation(out=EbT[:, :, :], in_=gv, func=AF.Exp), [g], [EbT])
                            op('act', lambda e: e.activation(out=EnbT[:, :, :], in_=gv, func=AF.Exp, scale=-1.0), [g], [EnbT])
                            op('act', lambda e: e.activation(out=ebrev[:, :], in_=g[:, 256:512], func=AF.Exp), [g], [ebrev])
                            qtl, ktl, khat = qtl_r.next(), ktl_r.next(), khat_r.next()
                            op('dve', lambda e: e.tensor_tensor(out=qtl[:, :, :], in0=qgT[:, :, tok], in1=EbT[:, :, :], op=ALU.mult), [qgT, EbT], [qtl])
                            op('pool', lambda e: e.tensor_tensor(out=ktl[:, :, :], in0=kgT[:, :, tok], in1=EnbT[:, :, :], op=ALU.mult), [kgT, EnbT], [ktl])
                            op('pool', lambda e: e.tensor_tensor(out=khat[:, :], in0=kg[:, :], in1=ebrev[:, :], op=ALU.mult), [kg, ebrev], [khat])
                            gat = [gp(), gp()]
                            mm([(gat[hd % 2][:, (hd // 2) * 128:(hd // 2 + 1) * 128],
                                 [(ktl[(hd % 2) * 64:(hd % 2) * 64 + 64, hd // 2, :], qtl[(hd % 2) * 64:(hd % 2) * 64 + 64, hd // 2, :])])
                                for hd in (0, 2, 1, 3)], [ktl, qtl], gat)
                            ATs = ATs_r.next()
                            for dh in range(2):
                                op('dve', lambda e: e.tensor_tensor(out=ATs[:, dh * 2:dh * 2 + 2, :],
                                                                    in0=gat[dh][:, 0:256].rearrange("p (h t) -> p h t", h=2),
                                                                    in1=cf[:, 128:256].unsqueeze(1).to_broadcast([128, 2, 128]), op=ALU.mult),
                                   [gat[dh], cf], [ATs])
                            gov = [gp(), gp()]
                            mm([(gov[hd % 2][:, (hd // 2) * 128:(hd // 2 + 1) * 128],
                                 [(qtl[(hd % 2) * 64:(hd % 2) * 64 + 64, hd // 2, :], Sst[(hd % 2) * 64:(hd % 2) * 64 + 64, hd // 2, :]),
                                  (ATs[:, (hd % 2) * 2 + hd // 2, :], vg[:, hd * 128:(hd + 1) * 128])]) for hd in (0, 2, 1, 3)],
                               [qtl, Sst, ATs, vg], gov)
                            gs = gp()
                            mm([(gs[:, hp * 256:(hp + 1) * 256], [(khat[:, hp * 128:(hp + 1) * 128], vg[:, hp * 256:(hp + 1) * 256])])
                                for hp in range(2)], [khat, vg], [gs])
                            for hp in range(2):
                                for dh in range(2):
                                    pd = slice(dh * 64, dh * 64 + 64)
                                    op('dve', lambda e: e.scalar_tensor_tensor(
                                        out=Sst[pd, hp, :], in0=Sst[pd, hp, :], scalar=EbT[pd, hp, 127:128],
                                        in1=gs[pd, hp * 256 + dh * 128:hp * 256 + dh * 128 + 128], op0=ALU.mult, op1=ALU.add),
                                       [Sst, EbT, gs], [Sst])
                            op('dve', lambda e: e.memset(ssq[:, 0:4], 0.0), [], [ssq])
                            for hd in range(4):
                                gsrc = gov[hd % 2]
                                op('act', lambda e: e.activation(out=junk[:, 0:128], in_=gsrc[:, (hd // 2) * 128:(hd // 2 + 1) * 128], func=AF.Square,
                                                                 accum_out=ssq[:, hd:hd + 1]), [gsrc, ssq], [junk, ssq])
                            op('act', lambda e: e.activation(out=std[:, 0:4], in_=ssq[:, 0:4], func=AF.Sqrt, bias=EPS, scale=1.0 / 128.0), [ssq], [std])
                            op('dve', lambda e: e.reciprocal(out=rstd[:, 0:4], in_=std[:, 0:4]), [std], [rstd])
                            for dh in range(2):
                                op('dve', lambda e: e.tensor_tensor(
                                    out=t1[:, :, :].rearrange("p (hp dh) e -> p hp dh e", dh=2)[:, :, dh, :],
                                    in0=gov[dh][:, 0:256].rearrange("p (h e) -> p h e", h=2),
                                    in1=rstd[:, 0:4].rearrange("p (hp dh) -> p hp dh", dh=2)[:, :, dh].unsqueeze(2).to_broadcast([128, 2, 128]),
                                    op=ALU.mult), [gov[dh], rstd], [t1])
                            op('pool', lambda e: e.tensor_tensor(out=t1[:, :, :], in0=t1[:, :, :],
                                                                 in1=ggn_b[:, :].unsqueeze(1).to_broadcast([128, 4, 128]), op=ALU.mult), [t1, ggn_b], [t1])
                            ogl = ogl_r.next()
                            op('pool', lambda e: e.tensor_tensor(out=ogl[:, :], in0=t1[:, :, :].rearrange("p h e -> p (h e)"), in1=sog[:, :], op=ALU.mult),
                               [t1, sog], [ogl])
                            tb = tp()
                            P.tr([(tb[:, kc * 128:(kc + 1) * 128], ogl[:, kc * 128:(kc + 1) * 128], ID(128)) for kc in range(4)], [ogl, cb], [tb])
                            op('act', lambda e: e.copy(out=oT[:, 4:8, tok], in_=tb[:, 0:512].rearrange("p (c t) -> p c t", c=4)), [tb], [oT])
                        chk('a1_t5')
                        dma('sp', lambda e: e.dma_start(out=hT_s[tile_id], in_=hT[:, :, :]), [hT], [hT_sb[tile_id]])
                        dma('sp', lambda e: e.dma_start(out=oT_s[tile_id], in_=oT[:, :, :]), [oT], [oT_sb[tile_id]])
                    dma('sp', lambda e: e.dma_start(out=gsp[q].rearrange("(hp dh) d e -> (dh d) hp e", dh=2), in_=Sst[:, :, :]), [Sst], [])
                while not sdone[0]:
                    pump(1)
                P.barrier()

            def phase_a2():
              with contextlib.ExitStack() as es:
                Wg = load_w(es, "Wg", w_in_v, 8, 2048, 3096, 512)
                Wfo = load_w(es, "Wfo", wfo.rearrange("(kc p) n -> p kc n", p=128), 4, 1024, 0, 1024)
                Wgo = load_w(es, "Wgo", wgo.rearrange("(kc p) n -> p kc n", p=128), 4, 1024, 0, 1024)
                Wo = load_w(es, "Wo", wo.rearrange("(kc p) n -> p kc n", p=128), 8, 1024, 0, 512)
                gpm_b = P.sb(es, "gpm_b", [128, 1024], F32)
                dma('sp', lambda e: e.dma_start(out=gpm_b[:], in_=gpm[0:1, :].partition_broadcast(128)), [], [gpm_b])
                G = [P.ps(es, "H%d" % i, [128, 512], F32) for i in range(8)]
                gi = [0]

                def gp2():
                    gi[0] = (gi[0] + 1) % len(G)
                    return G[gi[0]]
                hT2 = [P.sb(es, "hT2_%d" % i, [128, 8, T], BF16) for i in range(2)]
                oT2 = [P.sb(es, "oT2_%d" % i, [128, 8, T], BF16) for i in range(2)]
                s1 = [P.sb(es, "s1_%d" % i, [128, T], F32) for i in range(2)]
                s2 = [P.sb(es, "s2_%d" % i, [128, T], F32) for i in range(2)]
                uT = P.sb(es, "uT", [128, 8, T], BF16)
                xr = [P.sb(es, "xr%d" % i, [128, 1024], F32) for i in range(2)]
                zn = [P.sb(es, "zn%d" % i, [128, 1024], F32) for i in range(2)]
                junk2 = P.sb(es, "junk2", [128, 512], BF16)
                ssq = P.sb(es, "ssq2", [128, 2], F32); std = P.sb(es, "std2", [128, 2], F32); rstd = P.sb(es, "rstd2", [128, 2], F32)
                cnt = [0]
                tiles = [(i, T, 128) for i in range(NTILE)] + [(-1, NS, NS)]
                for (tile_id, Tn, npart) in tiles:
                    nsub = max(1, Tn // 128)
                    if tile_id >= 0:
                        hb, ob = hT2[tile_id % 2], oT2[tile_id % 2]
                        dma('sp', lambda e: e.dma_start(out=hb[:, :, :], in_=hT_s[tile_id]), [hT_sb[tile_id]], [hb])
                        dma('sp', lambda e: e.dma_start(out=ob[:, :, :], in_=oT_s[tile_id]), [oT_sb[tile_id]], [ob])
                    else:
                        hb, ob = hTs, oTs
                    for c in range(8):
                        cs = slice(c * 128, (c + 1) * 128)
                        ga, gb, g1, g2 = gp2(), gp2(), gp2(), gp2()
                        mm([(g1[:, 0:Tn], [(Wg[:, kc, c * 128:(c + 1) * 128], hb[:, kc, 0:Tn]) for kc in range(8)])], [Wg, hb], [g1])
                        mm([(g2[:, 0:Tn], [(Wg[:, kc, 1024 + c * 128:1024 + (c + 1) * 128], hb[:, kc, 0:Tn]) for kc in range(8)])], [Wg, hb], [g2])
                        mm([(ga[:, 0:Tn], [(Wfo[:, kc, cs], ob[:, kc, 0:Tn]) for kc in range(4)])], [Wfo, ob], [ga])
                        mm([(gb[:, 0:Tn], [(Wgo[:, kc, cs], ob[:, 4 + kc, 0:Tn]) for kc in range(4)])], [Wgo, ob], [gb])
                        a1, a2 = s1[c % 2], s2[c % 2]
                        op('act', lambda e: e.activation(out=a1[:, 0:Tn], in_=g1[:, 0:Tn], func=AF.Sigmoid), [g1], [a1])
                        op('act', lambda e: e.activation(out=a2[:, 0:Tn], in_=g2[:, 0:Tn], func=AF.Sigmoid), [g2], [a2])
                        op('dve', lambda e: e.tensor_tensor(out=a1[:, 0:Tn], in0=a1[:, 0:Tn], in1=ga[:, 0:Tn], op=ALU.mult), [a1, ga], [a1])
                        op('dve', lambda e: e.tensor_tensor(out=a2[:, 0:Tn], in0=a2[:, 0:Tn], in1=gb[:, 0:Tn], op=ALU.mult), [a2, gb], [a2])
                        op('pool', lambda e: e.tensor_tensor(out=uT[:, c, 0:Tn], in0=a1[:, 0:Tn], in1=a2[:, 0:Tn], op=ALU.add), [a1, a2], [uT])
                    for sub in range(nsub):
                        tok = slice(sub * npart, (sub + 1) * npart)
                        xb, zb = xr[cnt[0] % 2], zn[cnt[0] % 2]
                        cnt[0] += 1
                        if tile_id >= 0:
                            q, it = tile_id // NT, tile_id % NT
                            r0 = it * T + sub * 128
                            dma('sp', lambda e: e.dma_start(out=xb[:, :], in_=xp[q, r0:r0 + 128, :]), [], [xb])
                        else:
                            dma('sp', lambda e: e.dma_start(out=xb[0:NS, :], in_=xsm[:, :]), [], [xb])
                        gz = [gp2(), gp2()]
                        for half in range(2):
                            mm([(gz[half][0:npart, :], [(uT[:, kc, tok], Wo[:, kc, half * 512:(half + 1) * 512]) for kc in range(8)])], [uT, Wo], [gz[half]])
                        op('dve', lambda e: e.memset(ssq[0:npart, 0:2], 0.0), [], [ssq])
                        for half in range(2):
                            op('act', lambda e: e.activation(out=junk2[0:npart, :], in_=gz[half][0:npart, :], func=AF.Square,
                                                             accum_out=ssq[0:npart, half:half + 1]), [gz[half], ssq], [junk2, ssq])
                        op('dve', lambda e: e.tensor_tensor(out=ssq[0:npart, 0:1], in0=ssq[0:npart, 0:1], in1=ssq[0:npart, 1:2], op=ALU.add), [ssq], [ssq])
                        op('act', lambda e: e.activation(out=std[0:npart, 0:1], in_=ssq[0:npart, 0:1], func=AF.Sqrt, bias=EPS, scale=1.0 / 1024.0), [ssq], [std])
                        op('dve', lambda e: e.reciprocal(out=rstd[0:npart, 0:1], in_=std[0:npart, 0:1]), [std], [rstd])
                        for half in range(2):
                            hs = slice(half * 512, (half + 1) * 512)
                            op('dve', lambda e: e.scalar_tensor_tensor(out=zb[0:npart, hs], in0=gz[half][0:npart, :], scalar=rstd[0:npart, 0:1],
                                                                       in1=gpm_b[0:npart, hs], op0=ALU.mult, op1=ALU.mult), [gz[half], rstd, gpm_b], [zb])
                        op('pool', lambda e: e.tensor_tensor(out=zb[0:npart, :], in0=zb[0:npart, :], in1=xb[0:npart, :], op=ALU.add), [zb, xb], [zb])
                        if tile_id >= 0:
                            dma('sp', lambda e: e.dma_start(out=x1_s[tile_id, sub], in_=zb[:, :]), [zb], [x1_sb[tile_id]])
                        else:
                            dma('sp', lambda e: e.dma_start(out=x1s_s[:, :], in_=zb[0:NS, :]), [zb], [x1s_b])
                P.barrier()

            def phase_b():
              with contextlib.ExitStack() as es:
                Wup = load_w(es, "Wup", wup.rearrange("(kc p) n -> p kc n", p=128), 8, 4096, 0, 512)
                Wdn = load_w(es, "Wdn", wdn.rearrange("(kc p) n -> p kc n", p=128), 32, 1024, 0, 512)
                identb = P.sb(es, "identb", [128, 128], BF16)
                dma('pool', lambda e: e.dma_start(out=identb[:], in_=cst[:, 0:128]), [], [identb])
                gpl_sb = P.sb(es, "gpl_sb", [128, 8], F32)
                dma('sp', lambda e: e.dma_start(out=gpl_sb[:], in_=gpl[:, :]), [], [gpl_sb])
                gpo_b = P.sb(es, "gpo_b", [128, 1024], F32)
                dma('sp', lambda e: e.dma_start(out=gpo_b[:], in_=gpo[0:1, :].partition_broadcast(128)), [], [gpo_b])
                G = [P.ps(es, "M%d" % i, [128, 512], F32) for i in range(6)]
                TRB = [P.ps(es, "TRC%d" % i, [128, 1024], BF16) for i in range(2)]
                gi = [0]; ti = [0]

                def gp3():
                    gi[0] = (gi[0] + 1) % len(G)
                    return G[gi[0]]

                def tp3():
                    ti[0] = (ti[0] + 1) % len(TRB)
                    return TRB[ti[0]]
                x1a = [P.sb(es, "x1a%d" % i, [128, 1024], F32) for i in range(2)]
                x1b = [P.sb(es, "x1b%d" % i, [128, 1024], F32) for i in range(2)]
                xn2 = [P.sb(es, "xn2_%d" % i, [128, 1024], BF16) for i in range(2)]
                junk3 = P.sb(es, "junk3", [128, 512], BF16)
                ssq = P.sb(es, "ssq3", [128, 2], F32); std = P.sb(es, "std3", [128, 2], F32); rstd = P.sb(es, "rstd3", [128, 2], F32)
                stat = (ssq, std, rstd)
                h2T = P.sb(es, "h2T", [128, 8, T], BF16)
                aT = P.sb(es, "aT", [128, 32, T], BF16)
                rt = [P.sb(es, "rt%d" % i, [128, T], F32) for i in range(2)]
                zn = [P.sb(es, "zn3_%d" % i, [128, 1024], F32) for i in range(2)]
                cnt = [0]
                tiles = [(i, T, 128) for i in range(NTILE)] + [(-1, NS, NS)]
                for (tile_id, Tn, npart) in tiles:
                    nsub = max(1, Tn // 128)
                    for sub in range(nsub):
                        xa = x1a[cnt[0] % 2]
                        if tile_id >= 0:
                            dma('sp', lambda e: e.dma_start(out=xa[:, :], in_=x1_s[tile_id, sub]), [x1_sb[tile_id]], [xa])
                        else:
                            dma('sp', lambda e: e.dma_start(out=xa[0:NS, :], in_=x1s_s[:, :]), [x1s_b], [xa])
                        norm_T(es, xa, npart, gpl_sb, identb, xn2[cnt[0] % 2], stat, tp3(), h2T, sub * npart)
                        cnt[0] += 1
                    for c in range(32):
                        g = gp3()
                        mm([(g[:, 0:Tn], [(Wup[:, kc, c * 128:(c + 1) * 128], h2T[:, kc, 0:Tn]) for kc in range(8)])], [Wup, h2T], [g])
                        r = rt[c % 2]
                        op('act', lambda e: e.activation(out=r[:, 0:Tn], in_=g[:, 0:Tn], func=AF.Relu), [g], [r])
                        op('dve' if c % 2 == 0 else 'pool', lambda e: e.tensor_tensor(out=aT[:, c, 0:Tn], in0=r[:, 0:Tn], in1=r[:, 0:Tn], op=ALU.mult), [r], [aT])
                    for sub in range(nsub):
                        tok = slice(sub * npart, (sub + 1) * npart)
                        xb, zb = x1b[cnt[0] % 2], zn[cnt[0] % 2]
                        cnt[0] += 1
                        if tile_id >= 0:
                            dma('sp', lambda e: e.dma_start(out=xb[:, :], in_=x1_s[tile_id, sub]), [x1_sb[tile_id]], [xb])
                        else:
                            dma('sp', lambda e: e.dma_start(out=xb[0:NS, :], in_=x1s_s[:, :]), [x1s_b], [xb])
                        gz = [gp3(), gp3()]
                        for half in range(2):
                            mm([(gz[half][0:npart, :], [(aT[:, c, tok], Wdn[:, c, half * 512:(half + 1) * 512]) for c in range(32)])], [aT, Wdn], [gz[half]])
                        op('dve', lambda e: e.memset(ssq[0:npart, 0:2], 0.0), [], [ssq])
                        for half in range(2):
                            op('act', lambda e: e.activation(out=junk3[0:npart, 0:512], in_=gz[half][0:npart, :], func=AF.Square,
                                                             accum_out=ssq[0:npart, half:half + 1]), [gz[half], ssq], [junk3, ssq])
                        op('dve', lambda e: e.tensor_tensor(out=ssq[0:npart, 0:1], in0=ssq[0:npart, 0:1], in1=ssq[0:npart, 1:2], op=ALU.add), [ssq], [ssq])
                        op('act', lambda e: e.activation(out=std[0:npart, 0:1], in_=ssq[0:npart, 0:1], func=AF.Sqrt, bias=EPS, scale=1.0 / 1024.0), [ssq], [std])
                        op('dve', lambda e: e.reciprocal(out=rstd[0:npart, 0:1], in_=std[0:npart, 0:1]), [std], [rstd])
                        for half in range(2):
                            hs = slice(half * 512, (half + 1) * 512)
                            op('dve', lambda e: e.scalar_tensor_tensor(out=zb[0:npart, hs], in0=gz[half][0:npart, :], scalar=rstd[0:npart, 0:1],
                                                                       in1=gpo_b[0:npart, hs], op0=ALU.mult, op1=ALU.mult), [gz[half], rstd, gpo_b], [zb])
                        op('pool', lambda e: e.tensor_tensor(out=zb[0:npart, :], in0=zb[0:npart, :], in1=xb[0:npart, :], op=ALU.add), [zb, xb], [zb])
                        if tile_id >= 0:
                            q, it = tile_id // NT, tile_id % NT
                            r0 = it * T + sub * 128
                            dma('sp', lambda e: e.dma_start(out=yp[q, r0:r0 + 128, :], in_=zb[:, :]), [zb], [])
                        else:
                            dma('sp', lambda e: e.dma_start(out=ysm[:, :], in_=zb[0:NS, :]), [zb], [])
            try:
                phase_a1()
                chk('A1')
                phase_a2()
                chk('A2')
                phase_b()
            except _Stop:
                pass
            P.barrier(engines=('sp',))
    except AssertionError:
        if not cfg.get('stop'):
            raise
    return nc, P


FULL_CFG = dict(S=2048, NSEQ=2, NS=4, NPG=128, NPOOL=5120)
_CACHE = {}


def make_in_maps(inp, cfg, n_cores):
    NSEQ, NS = cfg['NSEQ'], cfg['NS']
    f = lambda a: np.ascontiguousarray(a, dtype=np.float32)
    ck = f(inp['cache_k'][0]).reshape(-1, 512)
    cv = f(inp['cache_v'][0]).reshape(-1, 512)
    cl = f(inp['cache_logf'][0]).reshape(-1, 8)
    shared = dict(
        ck=ck, cv=cv, cl=cl,
        w_in=f(inp['w_in'][0]), gpre=f(inp['g_pre_mix'][0].reshape(8, 128).T), bfv=f(inp['b_f'][0].reshape(1, 8)),
        wau=f(inp['w_alpha_up'][0]), bal=f(inp['b_alpha'][0].reshape(1, 256)), ggn=f(inp['g_gla_norm'][0].reshape(1, 128)),
        wfo=f(inp['w_fox_out'][0]), wgo=f(inp['w_gla_out'][0]), wo=f(inp['w_o'][0]),
        gpm=f(inp['g_post_mix'][0].reshape(1, 1024)), gpl=f(inp['g_pre_mlp'][0].reshape(8, 128).T),
        wup=f(inp['w_up'][0]), wdn=f(inp['w_down'][0]), gpo=f(inp['g_post_mlp'][0].reshape(1, 1024)),
        cst=make_consts(),
    )
    maps = []
    for c in range(n_cores):
        m = dict(shared)
        m['xp'] = f(inp['x_prompt'][c * NSEQ:(c + 1) * NSEQ])
        m['xsm'] = f(inp['x_sample'][c * NS:(c + 1) * NS, 0, :])
        m['sg'] = f(inp['state_gla'][0, c * NS:(c + 1) * NS])
        m['pt'] = np.ascontiguousarray(inp['page_table'][c * NS:(c + 1) * NS].reshape(1, -1), dtype=np.int32)
        maps.append(m)
    return maps


def assemble(results, cfg, n_cores):
    S, NSEQ, NS = cfg['S'], cfg['NSEQ'], cfg['NS']
    cat = lambda k: np.concatenate([np.asarray(r[k]) for r in results], axis=0)
    B = NSEQ * n_cores
    DB = NS * n_cores
    return (
        cat('yp').reshape(B, S, 1024).astype(np.float32),
        cat('ysm').reshape(DB, 1, 1024).astype(np.float32),
        cat('kp').reshape(1, B, S, 8, 64).astype(np.float32),
        cat('vp').reshape(1, B, S, 8, 64).astype(np.float32),
        cat('lfp').reshape(1, B, S, 8).astype(np.float32),
        cat('gsp').reshape(1, B, 4, 64, 128).astype(np.float32),
        cat('ksm').reshape(1, DB, 1, 8, 64).astype(np.float32),
        cat('vsm').reshape(1, DB, 1, 8, 64).astype(np.float32),
        cat('lfs').reshape(1, DB, 1, 8).astype(np.float32),
        cat('gss').reshape(1, DB, 4, 64, 128).astype(np.float32),
    )


def run(inp, cfg, n_cores):
    key = tuple(sorted(cfg.items()))
    if key not in _CACHE:
        _CACHE[key] = build(cfg)[0]
    nc = _CACHE[key]
    maps = make_in_maps(inp, cfg, n_cores)
    res = run_bass_kernel_spmd(nc, maps, core_ids=list(range(n_cores)))
    return assemble(res.results, cfg, n_cores)


def kernel(**inputs):
    inp = {k: np.asarray(v) for k, v in inputs.items()}
    return run(inp, FULL_CFG, N_CORES)
```
